# Optimizing a Trainium2 kernel written in Bass

```python
import math
import jax, jax.numpy as jnp
from jax import lax
import numpy as np

D_MODEL = 1024
BATCH = 32
SEQ = 256
DEPTH = 1
DEC_BATCH = 8
DEC_SEQ = 2048
PAST_LEN = 256

GRID_W = 64
N_HEADS = 4
HEAD_DIM = 256
D_MLSTM = N_HEADS * HEAD_DIM
D_HYENA = D_MODEL
HYENA_ORDER = 2
SHORT_CONV = 3
FILT_BANDS = 16
FILT_EMB = 1 + 2 * FILT_BANDS
FILT_WIDTH = 64
HYENA_MIN_DECAY = math.log(1e-2) / 1.5
HYENA_MAX_DECAY = math.log(1e-2) / 0.3
D_FF = 4 * D_MODEL
CHUNK = 128
RMS_EPS = 1e-6
SPLIT_SIZES = (D_MLSTM, D_MLSTM, D_MLSTM, D_MLSTM, 4 * N_HEADS, 3 * D_HYENA, D_MODEL, D_MODEL)
IN_COLS = 4 * D_MLSTM + 4 * N_HEADS + 3 * D_HYENA + 2 * D_MODEL

kernel_name = "hybrid_mlstm_hyena_diffusion_step"


def rmsnorm(x, w):
    xf = x.astype(jnp.float32)
    y = xf * lax.rsqrt(jnp.mean(xf * xf, axis=-1, keepdims=True) + RMS_EPS)
    return (y * w.astype(jnp.float32)).astype(x.dtype)


def grid_pos_embed(n_tokens):
    rows = n_tokens // GRID_W
    quarter = D_MODEL // 4
    omega = 1.0 / (10000.0 ** (jnp.arange(quarter, dtype=jnp.float32) / quarter))

    def axis_embed(pos):
        a = pos[:, None] * omega[None, :]
        return jnp.concatenate([jnp.sin(a), jnp.cos(a)], axis=-1)

    er = axis_embed(jnp.arange(rows, dtype=jnp.float32))
    ec = axis_embed(jnp.arange(GRID_W, dtype=jnp.float32))
    half = D_MODEL // 2
    pos = jnp.concatenate([jnp.broadcast_to(er[:, None, :], (rows, GRID_W, half)),
                           jnp.broadcast_to(ec[None, :, :], (rows, GRID_W, half))], axis=-1)
    return pos.reshape(rows * GRID_W, D_MODEL)


def mlstm_scan(q, k, v, li, lf, C0, n0, m0):
    B, S, H, DK = q.shape
    nc = S // CHUNK

    def to_chunks(a):
        a = a.reshape((B, nc, CHUNK, H) + a.shape[3:])
        return jnp.moveaxis(a, (1, 3), (0, 2))

    tril = jnp.tril(jnp.ones((CHUNK, CHUNK), dtype=bool))

    def step(carry, inp):
        C, n, m = carry
        qc, kc, vc, lic, lfc = inp
        b = jnp.cumsum(lfc, axis=-1)
        d = jnp.where(tril, b[..., :, None] - b[..., None, :] + lic[..., None, :], -jnp.inf)
        m_inter = b + m[..., None]
        m_comb = jnp.maximum(m_inter, jnp.max(d, axis=-1))
        s = jnp.einsum('bhjd,bhsd->bhjs', qc, kc) * jnp.exp(d - m_comb[..., None])
        w_inter = jnp.exp(m_inter - m_comb)
        num = w_inter[..., None] * jnp.einsum('bhjk,bhkv->bhjv', qc, C) + jnp.einsum('bhjs,bhsv->bhjv', s, vc)
        den = w_inter * jnp.einsum('bhjk,bhk->bhj', qc, n) + jnp.sum(s, axis=-1)
        h = num / jnp.maximum(jnp.abs(den), jnp.exp(-m_comb))[..., None]
        b_last = b[..., -1]
        a = b_last[..., None] - b + lic
        m_new = jnp.maximum(b_last + m, jnp.max(a, axis=-1))
        w_state = jnp.exp(a - m_new[..., None])
        decay = jnp.exp(b_last + m - m_new)
        C_new = decay[..., None, None] * C + jnp.einsum('bhsk,bhsv->bhkv', kc * w_state[..., None], vc)
        n_new = decay[..., None] * n + jnp.einsum('bhs,bhsk->bhk', w_state, kc)
        return (C_new, n_new, m_new), h

    xs = (to_chunks(q), to_chunks(k), to_chunks(v), to_chunks(li), to_chunks(lf))
    (C, n, m), h = lax.scan(step, (C0, n0, m0), xs)
    h = jnp.moveaxis(h, (0, 2), (1, 3)).reshape(B, S, H, v.shape[-1])
    return h, C, n, m


def mlstm_bidir(q, k, v, i_f, f_f, i_b, f_b, C0, n0, m0):
    hf, Cf, nf, mf = mlstm_scan(q, k, v, i_f, jax.nn.log_sigmoid(f_f), C0[:, 0], n0[:, 0], m0[:, 0])
    fl = lambda a: jnp.flip(a, axis=1)
    hb, Cb, nb, mb = mlstm_scan(fl(q), fl(k), fl(v), fl(i_b), fl(jax.nn.log_sigmoid(f_b)),
                                C0[:, 1], n0[:, 1], m0[:, 1])
    return (hf + fl(hb), jnp.stack([Cf, Cb], axis=1), jnp.stack([nf, nb], axis=1),
            jnp.stack([mf, mb], axis=1))


def hyena_filters(L, w1, b1, fr1, w2, b2, fr2, w3):
    f32 = jnp.float32
    t_idx = jnp.arange(L, dtype=f32)
    t = jnp.linspace(0.0, 1.0, L, dtype=f32)
    bands = jnp.arange(1, FILT_BANDS + 1, dtype=f32)
    ang = (2.0 * math.pi / L) * t_idx[:, None] * bands[None, :]
    z = jnp.concatenate([t[:, None], jnp.cos(ang), jnp.sin(ang)], axis=-1)
    h = jnp.sin(fr1.astype(f32) * (z @ w1.astype(f32) + b1.astype(f32)))
    h = jnp.sin(fr2.astype(f32) * (h @ w2.astype(f32) + b2.astype(f32)))
    h = (h @ w3.astype(f32)).reshape(L, HYENA_ORDER, D_HYENA)
    centre = L // 2
    dist = jnp.abs(t_idx - centre) / centre
    deltas = jnp.abs(jnp.linspace(HYENA_MIN_DECAY, HYENA_MAX_DECAY, D_HYENA, dtype=f32))
    h = h * jnp.exp(-dist[:, None, None] * deltas[None, None, :])
    return h / (jnp.sum(jnp.abs(h), axis=0, keepdims=True) + 1e-6)


def long_conv(z, h):
    L = z.shape[1]
    n = 2 * L
    y = jnp.fft.irfft(jnp.fft.rfft(z, n=n, axis=1) * jnp.fft.rfft(h, n=n, axis=0)[None], n=n, axis=1)
    return y[:, L // 2: L // 2 + L]


def short_conv(u, w, b):
    L = u.shape[1]
    pad = SHORT_CONV // 2
    up = jnp.pad(u, ((0, 0), (pad, SHORT_CONV - 1 - pad), (0, 0)))
    out = b
    for j in range(SHORT_CONV):
        out = out + up[:, j:j + L] * w[j]
    return out


def hyena(u, conv_w, conv_b, w1, b1, fr1, w2, b2, fr2, w3, skip):
    L = u.shape[1]
    uc = short_conv(u.astype(jnp.float32), conv_w.astype(jnp.float32), conv_b.astype(jnp.float32))
    x1, x2, z = jnp.split(uc, 3, axis=-1)
    h = hyena_filters(L, w1, b1, fr1, w2, b2, fr2, w3)
    skip = skip.astype(jnp.float32)
    for o, gate in enumerate((x1, x2)):
        z = gate * (long_conv(z, h[:, o]) + skip[o] * z)
    return z


def layer(x, cond, C0, n0, m0, w_ada, b_ada, norm1_w, w_in, b_in, hy_conv_w, hy_conv_b,
          filt_w1, filt_b1, filt_freq1, filt_w2, filt_b2, filt_freq2, filt_w3, hy_skip,
          mlstm_norm_w, w_br_m, w_br_h, w_out, norm2_w, w_mlp1, b_mlp1, w_mlp2, b_mlp2):
    B, S, _ = x.shape
    f32 = jnp.float32
    mod = jax.nn.silu(cond) @ w_ada + b_ada
    sh1, sc1, g1, sh2, sc2, g2 = jnp.split(mod[:, None, :], 6, axis=-1)
    hn = rmsnorm(x, norm1_w) * (1 + sc1) + sh1
    proj = hn @ w_in + b_in
    idx = [int(i) for i in np.cumsum(SPLIT_SIZES)[:-1]]
    q, k, v, o, gates, u_h, gm, gh = jnp.split(proj, idx, axis=-1)
    q = q.astype(f32).reshape(B, S, N_HEADS, HEAD_DIM) * (HEAD_DIM ** -0.5)
    k = k.astype(f32).reshape(B, S, N_HEADS, HEAD_DIM)
    v = v.astype(f32).reshape(B, S, N_HEADS, HEAD_DIM)
    gates = gates.astype(f32).reshape(B, S, 4, N_HEADS)
    h, C, n, m = mlstm_bidir(q, k, v, gates[:, :, 0], gates[:, :, 1], gates[:, :, 2], gates[:, :, 3],
                             C0.astype(f32), n0.astype(f32), m0.astype(f32))
    h = h * lax.rsqrt(jnp.mean(h * h, axis=-1, keepdims=True) + RMS_EPS)
    h = h.reshape(B, S, D_MLSTM) * mlstm_norm_w.astype(f32) * jax.nn.sigmoid(o.astype(f32))
    h = h.astype(x.dtype)
    yh = hyena(u_h, hy_conv_w, hy_conv_b, filt_w1, filt_b1, filt_freq1, filt_w2, filt_b2,
               filt_freq2, filt_w3, hy_skip).astype(x.dtype)
    merged = jax.nn.sigmoid(gm) * (h @ w_br_m) + jax.nn.sigmoid(gh) * (yh @ w_br_h)
    x = x + g1 * (merged @ w_out)
    hn2 = rmsnorm(x, norm2_w) * (1 + sc2) + sh2
    ff = jnp.square(jax.nn.relu(hn2 @ w_mlp1 + b_mlp1)) @ w_mlp2 + b_mlp2
    x = x + g2 * ff
    return x, C, n, m


def setup_inputs(seed: int = 0) -> dict:
    key = jax.random.key(seed)
    ks = jax.random.split(key, 32)
    nrm = lambda k, shape, s=1.0: s * jax.random.normal(k, shape, dtype=jnp.float32)
    gate_off = 4 * D_MLSTM
    forget_bias = jnp.linspace(3.0, 6.0, N_HEADS, dtype=jnp.float32)
    b_in = nrm(ks[10], (DEPTH, IN_COLS), 0.01)
    b_in = b_in.at[:, gate_off + N_HEADS: gate_off + 2 * N_HEADS].add(forget_bias)
    b_in = b_in.at[:, gate_off + 3 * N_HEADS: gate_off + 4 * N_HEADS].add(forget_bias)
    return {
        "x_prompt": nrm(ks[0], (BATCH, SEQ, D_MODEL)),
        "x_sample": nrm(ks[1], (DEC_BATCH, DEC_SEQ, D_MODEL)),
        "state_mlstm_C": nrm(ks[2], (DEC_BATCH, DEPTH, 2, N_HEADS, HEAD_DIM, HEAD_DIM), 0.3),
        "state_mlstm_n": nrm(ks[3], (DEC_BATCH, DEPTH, 2, N_HEADS, HEAD_DIM), 0.3),
        "state_mlstm_m": nrm(ks[4], (DEC_BATCH, DEPTH, 2, N_HEADS), 0.5),
        "c": nrm(ks[5], (DEC_BATCH, D_MODEL)),
        "c_ctx": nrm(ks[6], (D_MODEL,)),
        "w_ada": nrm(ks[7], (DEPTH, D_MODEL, 6 * D_MODEL), 0.5 * D_MODEL ** -0.5),
        "b_ada": nrm(ks[8], (DEPTH, 6 * D_MODEL), 0.01),
        "norm1_w": 1.0 + nrm(ks[9], (DEPTH, D_MODEL), 0.05),
        "w_in": nrm(ks[11], (DEPTH, D_MODEL, IN_COLS), D_MODEL ** -0.5),
        "b_in": b_in,
        "hy_conv_w": nrm(ks[12], (DEPTH, SHORT_CONV, 3 * D_HYENA), SHORT_CONV ** -0.5),
        "hy_conv_b": nrm(ks[13], (DEPTH, 3 * D_HYENA), 0.01),
        "filt_w1": nrm(ks[14], (DEPTH, FILT_EMB, FILT_WIDTH), FILT_EMB ** -0.5),
        "filt_b1": nrm(ks[15], (DEPTH, FILT_WIDTH), 0.01),
        "filt_freq1": 1.0 + nrm(ks[16], (DEPTH, FILT_WIDTH), 0.1),
        "filt_w2": nrm(ks[17], (DEPTH, FILT_WIDTH, FILT_WIDTH), FILT_WIDTH ** -0.5),
        "filt_b2": nrm(ks[18], (DEPTH, FILT_WIDTH), 0.01),
        "filt_freq2": 1.0 + nrm(ks[19], (DEPTH, FILT_WIDTH), 0.1),
        "filt_w3": nrm(ks[20], (DEPTH, FILT_WIDTH, HYENA_ORDER * D_HYENA), FILT_WIDTH ** -0.5),
        "hy_skip": nrm(ks[21], (DEPTH, HYENA_ORDER, D_HYENA)),
        "mlstm_norm_w": 1.0 + nrm(ks[22], (DEPTH, D_MLSTM), 0.05),
        "w_br_m": nrm(ks[23], (DEPTH, D_MLSTM, D_MODEL), D_MLSTM ** -0.5),
        "w_br_h": nrm(ks[24], (DEPTH, D_HYENA, D_MODEL), D_HYENA ** -0.5),
        "w_out": nrm(ks[25], (DEPTH, D_MODEL, D_MODEL), D_MODEL ** -0.5),
        "norm2_w": 1.0 + nrm(ks[26], (DEPTH, D_MODEL), 0.05),
        "w_mlp1": nrm(ks[27], (DEPTH, D_MODEL, D_FF), D_MODEL ** -0.5),
        "b_mlp1": nrm(ks[28], (DEPTH, D_FF), 0.01),
        "w_mlp2": nrm(ks[29], (DEPTH, D_FF, D_MODEL), D_FF ** -0.5),
        "b_mlp2": nrm(ks[30], (DEPTH, D_MODEL), 0.01),
        "final_norm_w": 1.0 + nrm(ks[31], (D_MODEL,), 0.05),
    }


def reference(x_prompt, x_sample, state_mlstm_C, state_mlstm_n, state_mlstm_m, c, c_ctx,
              w_ada, b_ada, norm1_w, w_in, b_in, hy_conv_w, hy_conv_b, filt_w1, filt_b1,
              filt_freq1, filt_w2, filt_b2, filt_freq2, filt_w3, hy_skip, mlstm_norm_w,
              w_br_m, w_br_h, w_out, norm2_w, w_mlp1, b_mlp1, w_mlp2, b_mlp2, final_norm_w):
    layer_params = (w_ada, b_ada, norm1_w, w_in, b_in, hy_conv_w, hy_conv_b, filt_w1, filt_b1,
                    filt_freq1, filt_w2, filt_b2, filt_freq2, filt_w3, hy_skip, mlstm_norm_w,
                    w_br_m, w_br_h, w_out, norm2_w, w_mlp1, b_mlp1, w_mlp2, b_mlp2)
    f32 = jnp.float32
    bp = x_prompt.shape[0]
    zC = jnp.zeros((bp, 2, N_HEADS, HEAD_DIM, HEAD_DIM), f32)
    zn = jnp.zeros((bp, 2, N_HEADS, HEAD_DIM), f32)
    zm = jnp.zeros((bp, 2, N_HEADS), f32)
    xp = x_prompt
    Cs, ns, ms = [], [], []
    for l in range(DEPTH):
        lw = [p[l] for p in layer_params]
        xp, Cl, nl, ml = layer(xp, c_ctx[None, :], zC, zn, zm, *lw)
        Cs.append(Cl)
        ns.append(nl)
        ms.append(ml)
    y_prompt = rmsnorm(xp, final_norm_w)
    new_state_C = jnp.stack(Cs, axis=1)
    new_state_n = jnp.stack(ns, axis=1)
    new_state_m = jnp.stack(ms, axis=1)
    xs = x_sample + grid_pos_embed(x_sample.shape[1]).astype(x_sample.dtype)[None]
    for l in range(DEPTH):
        lw = [p[l] for p in layer_params]
        xs, _, _, _ = layer(xs, c, state_mlstm_C[:, l], state_mlstm_n[:, l], state_mlstm_m[:, l], *lw)
    y_sample = rmsnorm(xs, final_norm_w)
    return (y_prompt, y_sample, new_state_C, new_state_n, new_state_m)
```

```python
import math
from contextlib import ExitStack
import numpy as np
import ml_dtypes
import concourse.bass as bass
import concourse.mybir as mybir
from concourse.bass_utils import run_bass_kernel_spmd

F32 = mybir.dt.float32
BF16 = mybir.dt.bfloat16
AF = mybir.ActivationFunctionType
ALU = mybir.AluOpType

D = 1024
NH = 4
HD = 256
TS_ = 2048
TP_ = 1024
TT = TS_ + TP_
LP = 256
NSEQP = 4
FF = 4096
EPS = 1e-6
NCORES = 8
DEBUG = False


class Buf:
    __slots__ = ("name", "w", "rd", "dsem", "excl")

    def __init__(self, name, excl=False):
        self.name = name
        self.excl = excl
        self.w = {}
        self.rd = {}
        self.dsem = None


class KB:
    def __init__(self, nc, es):
        self.nc = nc
        self.es = es
        self.eng = {"pe": nc.tensor, "act": nc.scalar, "dve": nc.vector, "pool": nc.gpsimd, "sp": nc.sync}
        self.sem = {e: es.enter_context(nc.semaphore("sem_" + e)) for e in ("pe", "act", "dve", "pool")}
        self.cnt = {e: 0 for e in self.sem}
        self.known = {e: {} for e in self.eng}
        self.dsems = {}
        self.nsem = 4
        self.nins = 0
        self.nkey = 0
        self.free = {}
        self.dbufs = []

    def buf(self, name):
        return Buf(name)

    def _need(self, e, reads, writes):
        need = {}
        is_dma = e.startswith("dma:")

        def add(ev, key, kind):
            sem, val, eng = ev
            if is_dma and kind == "waw" and eng.startswith("dma:"):
                return
            if eng == e:
                if e == "pe" and kind != "raw":
                    return
                if is_dma:
                    if kind == "waw":
                        return
                elif val > self.cnt.get(e, 0):
                    return
            if key not in need or need[key][1] < val:
                need[key] = (sem, val)

        for b in reads:
            for key, ev in b.w.items():
                add(ev, key, "raw")
            if b.excl:
                for key, ev in b.rd.items():
                    if ev[2] != e:
                        add(ev, key, "rar")
        for b in writes:
            for key, ev in b.w.items():
                add(ev, key, "waw")
            for key, ev in b.rd.items():
                add(ev, key, "war")
        return need

    def _wait(self, e, need):
        eng = self.eng[e]
        kn = self.known[e]
        for key, (sem, val) in need.items():
            if kn.get(key, 0) >= val:
                continue
            eng.wait_ge(sem, val)
            kn[key] = val

    def op(self, e, fn, r=(), w=(), inc=True):
        need = self._need(e, r, w)
        self._wait(e, need)
        ins = fn()
        self.nins += 1
        if inc:
            self.cnt[e] += 1
            ins.then_inc(self.sem[e], 1)
            ev = (self.sem[e], self.cnt[e], e)
        else:
            ev = (self.sem[e], self.cnt[e] + 1, e)
        for b in w:
            b.w = {e: ev}
            b.rd = {}
        for b in r:
            b.rd[e] = ev
        return ins

    def _dsem(self, b, q="sp"):
        q = "sw" if q == "pool" else "hw"
        if b.dsem is None:
            self.nkey += 1
            fl = self.free.setdefault(q, [])
            if fl:
                s, tot = fl.pop()
            else:
                s = self.es.enter_context(self.nc.semaphore("d%d_%s" % (self.nkey, b.name)))
                tot = 0
                self.nsem += 1
            b.dsem = [s, tot, "d_%s_%d" % (b.name, self.nkey), q]
            self.dbufs.append(b)
        return b.dsem

    def mark(self):
        return len(self.dbufs)

    def release(self, mark):
        for b in self.dbufs[mark:]:
            if b.dsem is not None:
                self.free.setdefault(b.dsem[3], []).append((b.dsem[0], b.dsem[1]))
                self._all.pop(b.dsem[2], None)
                b.dsem = None
        del self.dbufs[mark:]

    def dma(self, q, out, in_, r=(), w=(), owner=None, **kw):
        ds = self._dsem(owner, q)
        assert ds[3] == ("sw" if q == "pool" else "hw"), (owner.name, q)
        key = ds[2]
        need = self._need("dma:" + key, r, w)
        self._wait(q, need)
        ds[1] += 16
        kw.setdefault('allow_slow_non_contiguous', True)
        self.eng[q].dma_start(out=out, in_=in_, **kw).then_inc(ds[0], 16)
        self.nins += 1
        ev = (ds[0], ds[1], "dma:" + key)
        for b in w:
            neww = {k: v for k, v in b.w.items() if k.startswith("d_")}
            neww[key] = ev
            b.w = neww
            b.rd = {}
        for b in r:
            b.rd[key] = ev

    def barrier(self):
        evs = {}
        for e in self.sem:
            if self.cnt[e] > 0:
                evs[e] = (self.sem[e], self.cnt[e])
        for key, ds in self.dsems_all().items():
            evs[key] = (ds[0], ds[1])
        for e in self.eng:
            self._wait(e, {k: v for k, v in evs.items() if k != e})
        self.release(0)

    def dsems_all(self):
        return self._all

    _all = None


def _bf(a):
    return np.ascontiguousarray(a.astype(np.float32)).astype(ml_dtypes.bfloat16)


def _dft_mats(L, N):
    k = np.arange(N // 2, dtype=np.float64) + 0.5
    t = np.arange(L, dtype=np.float64)
    th = 2 * np.pi * np.outer(t, k) / N
    FT = np.concatenate([np.cos(th), -np.sin(th)], axis=1)
    th2 = 2 * np.pi * np.outer(k, t + L // 2) / N
    GT = np.concatenate([np.cos(th2), -np.sin(th2)], axis=0) * (2.0 / N)
    return _bf(FT), _bf(GT)


def _pos_table():
    rows = TS_ // 64
    quarter = D // 4
    omega = 1.0 / (10000.0 ** (np.arange(quarter, dtype=np.float32) / quarter))

    def ax(pos):
        a = pos[:, None].astype(np.float32) * omega[None, :]
        return np.concatenate([np.sin(a), np.cos(a)], axis=-1)

    er = ax(np.arange(rows, dtype=np.float32))
    ec = ax(np.arange(64, dtype=np.float32))
    half = D // 2
    pos = np.concatenate([np.broadcast_to(er[:, None, :], (rows, 64, half)),
                          np.broadcast_to(ec[None, :, :], (rows, 64, half))], axis=-1)
    return np.ascontiguousarray(pos.reshape(rows * 64, D).astype(np.float32))


def _filt_feats(L):
    t_idx = np.arange(L, dtype=np.float32)
    t = np.linspace(0.0, 1.0, L, dtype=np.float32)
    bands = np.arange(1, 17, dtype=np.float32)
    ang = (np.float32(2.0 * math.pi / L) * t_idx[:, None] * bands[None, :]).astype(np.float32)
    z = np.concatenate([t[:, None], np.cos(ang), np.sin(ang)], axis=-1).astype(np.float32)
    centre = L // 2
    dist = (np.abs(t_idx - centre) / centre).astype(np.float32)
    return np.ascontiguousarray(z.T), dist


BQ, BK, BU, BGM, BGH, BM1, N1W, N2W, BADA, CW, CB, SKIP = 0, 8, 16, 40, 48, 56, 88, 96, 104, 152, 224, 248
NBF = 264
CQ, CK, CV, CO, CG, CU, CGM, CGH = 0, 1024, 2048, 3072, 4096, 4112, 7184, 8208
NCH = TT // 128
SEGS = [(0, TS_, 1, TS_, 3072), (TS_, TP_, NSEQP, LP, 512)]


class Prog:
    def __init__(self, stop_after=None):
        self.stop_after = stop_after
        self.nc = nc = bass.Bass("TRN2", target_bir_lowering=False)
        self.es = ExitStack()
        self.kb = KB(nc, self.es)
        self.kb._all = {}
        self.din = {}
        self.dout = {}
        self._uid = 0

    def inp(self, name, shape, dt=F32):
        t = self.nc.dram_tensor(name, list(shape), dt, kind="ExternalInput")
        self.din[name] = t
        return t

    def outp(self, name, shape, dt=F32):
        t = self.nc.dram_tensor(name, list(shape), dt, kind="ExternalOutput")
        self.dout[name] = t
        return t

    def scratch(self, name, shape, dt):
        kind = "ExternalOutput" if DEBUG else "Internal"
        t = self.nc.dram_tensor(name, list(shape), dt, kind=kind)
        if DEBUG:
            self.dout[name] = t
        return t

    def sb(self, es, name, shape, dt):
        self._uid += 1
        t = es.enter_context(self.nc.sbuf_tensor(f"{name}_{self._uid}", list(shape), dt))
        return t, Buf(name)

    def dma(self, q, out, in_, r=(), w=(), owner=None, **kw):
        kb = self.kb
        ds_before = owner.dsem
        kb.dma(q, out, in_, r=r, w=w, owner=owner, **kw)
        kb._all[owner.dsem[2]] = owner.dsem

    def load(self, tile_ap, buf, src_ap, srcbuf=None, q="sp", **kw):
        self.dma(q, tile_ap, src_ap, r=([srcbuf] if srcbuf else []), w=[buf], owner=buf, **kw)

    def store(self, dst_ap, dstbuf, tile_ap, buf, q="sp", **kw):
        self.dma(q, dst_ap, tile_ap, r=[buf], w=([dstbuf] if dstbuf else []), owner=buf, **kw)

    def bank(self, pin=False):
        pool = self.cur_pool
        key = tuple(pool)
        k = self._bkp.get(key, 0)
        n = len(pool)
        for _ in range(n):
            i = pool[k % n]
            if i not in self._pinned:
                break
            k += 1
        self._bkp[key] = (k + 1) % n
        if pin:
            self._pinned.add(i)
        return self.banks[i], self.bankb[i]

    def interleave(self, items):
        active = list(items)
        while active:
            for it in list(active):
                gen, pool, stride = it
                self.cur_pool = pool
                for _ in range(stride):
                    try:
                        next(gen)
                    except StopIteration:
                        active.remove(it)
                        break
        self.cur_pool = list(range(8))

    def unpin(self, t):
        self._pinned.discard(self.banks.index(t))

    def build(self):
        nc, kb, es = self.nc, self.kb, self.es
        op = kb.op
        x_all = self.inp("x_all", [TT, D]).ap()
        pos = self.inp("pos", [TS_, D]).ap()
        C0 = self.inp("C0", [2, NH, HD, HD]).ap()
        n0 = self.inp("n0", [2, NH, HD]).ap()
        m0 = self.inp("m0", [2, NH]).ap()
        cvT = self.inp("cvT", [128, 8, 2]).ap()
        w_ada = self.inp("w_ada", [D, 6 * D]).ap()
        b_ada_row = self.inp("b_ada_row", [6 * D]).ap()
        w_in = self.inp("w_in", [D, 9232]).ap()
        w_gate = self.inp("w_gate", [D, 72]).ap()
        b_gate = self.inp("b_gate", [36, 2]).ap()
        bfm_d = self.inp("bfm", [128, NBF]).ap()
        brow = self.inp("brow", [3 * D]).ap()
        mnw = self.inp("mnw", [D]).ap()
        b2row = self.inp("b2row", [D]).ap()
        fnw = self.inp("fnw", [D]).ap()
        w_br_m = self.inp("w_br_m", [D, D]).ap()
        w_br_h = self.inp("w_br_h", [D, D]).ap()
        w_out = self.inp("w_out", [D, D]).ap()
        w_mlp1 = self.inp("w_mlp1", [D, FF]).ap()
        w_mlp2 = self.inp("w_mlp2", [FF, D]).ap()
        fw1 = self.inp("fw1", [33, 64]).ap()
        fw2 = self.inp("fw2", [64, 64]).ap()
        fw3 = self.inp("fw3", [64, 2048]).ap()
        fbv = self.inp("fbv", [64, 4]).ap()
        identf_d = self.inp("identf", [128, 128]).ap()
        identb_d = self.inp("identb", [128, 128], BF16).ap()
        masks_d = self.inp("masks", [128, 2, 128], BF16).ap()
        sel_d = self.inp("sel", [2, 2, 128]).ap()
        cmask_d = self.inp("cmask", [TT]).ap()
        deltas_d = self.inp("deltas", [D]).ap()
        zf_d = [self.inp("zfS", [33, TS_]).ap(), self.inp("zfP", [33, LP]).ap()]
        nd_d = [self.inp("ndS", [128, TS_ // 128]).ap(), self.inp("ndP", [128, LP // 128]).ap()]
        FT_d = [self.inp("FTS", [24, 128, 16 * 128], BF16).ap(), self.inp("FTP", [4, 128, 2 * 128], BF16).ap()]
        GT_d = [self.inp("GTS", [8, 128, 24 * 256], BF16).ap(), self.inp("GTP", [1, 128, 4 * 256], BF16).ap()]

        y_all = self.outp("y_all", [TT, D]).ap()
        nC = self.outp("nC", [NSEQP, 2, NH, HD, HD]).ap()
        nn = self.outp("nn", [NSEQP, 2, NH, HD]).ap()
        nm = self.outp("nm", [NSEQP, 2, NH]).ap()

        qT_d = self.scratch("s_qT", [D, TT], BF16).ap(); qT_b = Buf("s_qT")
        kT_d = self.scratch("s_kT", [D, TT], BF16).ap(); kT_b = Buf("s_kT")
        kv_d = self.scratch("s_kvo", [3, TT, D], BF16).ap(); kv_b = Buf("s_kvo")
        uT_d = self.scratch("s_uT", [3 * D, TT], F32).ap(); uT_b = Buf("s_uT")
        sg_d = self.scratch("s_sg", [2 * D, TT], BF16).ap(); sg_b = Buf("s_sg")
        hT_d = self.scratch("s_hT", [D, TT], BF16).ap(); hT_b = Buf("s_hT")
        yh_d = self.scratch("s_yhT", [D, TT], BF16).ap(); yh_b = Buf("s_yhT")
        Hs_d = [self.scratch("s_HsS", [3072, 2048], BF16).ap(), self.scratch("s_HsP", [512, 2048], BF16).ap()]
        Hs_b = [Buf("s_HsS"), Buf("s_HsP")]
        if DEBUG:
            dbg_hn = self.scratch("s_hn", [D, TT], BF16).ap(); dbg_hn_b = Buf("s_hn")
            dbg_row = self.scratch("s_rows", [8, 36, TT], F32).ap(); dbg_row_b = Buf("s_rows")

        self.banks, self.bankb, self._bk = [], [], 0
        self._pinned = set()
        self._bkp = {}
        self.cur_pool = list(range(8))
        for i in range(8):
            t = es.enter_context(nc.psum_tensor(f"bank{i}", [128, 512], F32))
            self.banks.append(t); self.bankb.append(Buf(f"bank{i}", excl=True))
        identf, identf_b = self.sb(es, "identf", [128, 128], F32)
        identb, identb_b = self.sb(es, "identb", [128, 128], BF16)
        masks, masks_b = self.sb(es, "masks", [128, 2, 128], BF16)
        bfm, bfm_b = self.sb(es, "bfm", [128, NBF], F32)
        onesf, onesf_b = self.sb(es, "onesf", [128, 128], F32)
        sel, sel_b = self.sb(es, "sel", [2, 2, 128], F32)
        Gb, Gb_b = self.sb(es, "Gb", [128, 2, 2, D], F32)
        B2G, B2G_b = self.sb(es, "B2G", [128, 2, D], F32)
        FNW, FNW_b = self.sb(es, "FNW", [128, D], F32)
        AB, AB_b = self.sb(es, "AB", [128, 4, 8, 2], F32)
        self.load(identf[:], identf_b, identf_d)
        self.load(identb[:], identb_b, identb_d)
        self.load(masks[:], masks_b, masks_d)
        self.load(bfm[:], bfm_b, bfm_d)
        self.load(sel[:], sel_b, sel_d)
        self.load(FNW[:], FNW_b, fnw.partition_broadcast(128))
        op("dve", lambda: nc.vector.memset(onesf[:], 1.0), w=[onesf_b])

        def wslot_load(tile, buf, src, cols, c0, kch=8):
            v = src.rearrange("(k p) n -> p k n", p=128)[:, :, c0:c0 + cols]
            self.load(tile[:, 0:kch, 0:cols], buf, v, q="pool")

        def phase0_gen(ph):
            cv, cv_b = self.sb(ph, "cv", [128, 8, 2], F32)
            scv, scv_b = self.sb(ph, "scv", [128, 8, 2], BF16)
            modfm, modfm_b = self.sb(ph, "modfm", [128, 48, 2], F32)
            grow, grow_b = self.sb(ph, "grow", [2, 2 * D], F32)
            brow2, brow2_b = self.sb(ph, "brow2", [2, 2 * D], F32)
            b2b, b2b_b = self.sb(ph, "b2b", [128, D], F32)
            wsl = [self.sb(ph, f"wada{i}", [128, 8, 512], BF16) for i in range(2)]
            self.load(cv[:], cv_b, cvT)
            self.load(brow2[:, 0:D], brow2_b, b_ada_row[2 * D:3 * D].partition_broadcast(2))
            self.load(brow2[:, D:2 * D], brow2_b, b_ada_row[5 * D:6 * D].partition_broadcast(2))
            self.load(b2b[:], b2b_b, b2row.partition_broadcast(128))
            op("act", lambda: nc.scalar.activation(out=scv[:], in_=cv[:], func=AF.Silu), r=[cv_b], w=[scv_b])
            pm, pm_b = self.bank(pin=True)
            yield
            for blk in range(12):
                wt, wb = wsl[blk % 2]
                wslot_load(wt, wb, w_ada, 512, blk * 512)
                for oc in range(4):
                    g_ = blk * 4 + oc
                    for k in range(8):
                        op("pe", lambda: nc.tensor.matmul(pm[:, 2 * g_:2 * g_ + 2], lhsT=wt[:, k, oc * 128:(oc + 1) * 128],
                                                          rhs=scv[:, k, :], start=(k == 0), stop=(k == 7)),
                           r=[wb, scv_b], w=[pm_b], inc=(k == 7))
                if blk in (4, 5, 10, 11):
                    gi = 0 if blk < 6 else 1
                    ct = blk % 2
                    pr, pr_b = self.bank()
                    for k in range(8):
                        op("pe", lambda: nc.tensor.matmul(pr[0:2, :], lhsT=scv[:, k, :], rhs=wt[:, k, 0:512],
                                                          start=(k == 0), stop=(k == 7)),
                           r=[wb, scv_b], w=[pr_b], inc=(k == 7))
                    cs_ = slice(gi * D + ct * 512, gi * D + (ct + 1) * 512)
                    op("dve", lambda: nc.vector.tensor_tensor(out=grow[:, cs_], in0=pr[0:2, :], in1=brow2[:, cs_], op=ALU.add),
                       r=[pr_b, brow2_b], w=[grow_b])
                yield
            op("dve", lambda: nc.vector.tensor_tensor(out=modfm[:], in0=pm[:, 0:96].rearrange("p (c r) -> p c r", r=2),
                                                      in1=bfm[:, BADA:BADA + 48].unsqueeze(2).to_broadcast([128, 48, 2]), op=ALU.add),
               r=[pm_b, bfm_b], w=[modfm_b])
            self.unpin(pm)
            for (ai, sc0, sh0, nw0) in ((0, 8, 0, N1W), (2, 32, 24, N2W)):
                op("dve", lambda: nc.vector.scalar_tensor_tensor(out=AB[:, ai], in0=modfm[:, sc0:sc0 + 8, :], scalar=1.0,
                                                                 in1=bfm[:, nw0:nw0 + 8].unsqueeze(2).to_broadcast([128, 8, 2]),
                                                                 op0=ALU.add, op1=ALU.mult),
                   r=[modfm_b, bfm_b], w=[AB_b])
                op("dve", lambda: nc.vector.tensor_copy(out=AB[:, ai + 1], in_=modfm[:, sh0:sh0 + 8, :]), r=[modfm_b], w=[AB_b])
            yield
            for r_ in range(2):
                for gi in range(2):
                    for ct in range(2):
                        pr, pr_b = self.bank()
                        op("pe", lambda: nc.tensor.matmul(pr[:], lhsT=sel[:, r_, :], rhs=grow[:, gi * D + ct * 512: gi * D + (ct + 1) * 512],
                                                          start=True, stop=True), r=[sel_b, grow_b], w=[pr_b])
                        op("act", lambda: nc.scalar.copy(out=Gb[:, r_, gi, ct * 512:(ct + 1) * 512], in_=pr[:]), r=[pr_b], w=[Gb_b])
                        yield
                op("dve", lambda: nc.vector.tensor_tensor(out=B2G[:, r_], in0=Gb[:, r_, 1], in1=b2b[:], op=ALU.mult),
                   r=[Gb_b, b2b_b], w=[B2G_b])
            yield

        self.phaseC(locals(), None, None, "filter")
        phTSC = es.enter_context(ExitStack())
        TSC, TSC_b = self.sb(phTSC, "TSC", [128, NCH, 5, 8], F32)
        if self.stop_after == 0:
            return self.finish()

        def norm_to_fm(ph_tiles, xt, xt_b, dstT, dstT_b, col0, ai, r_):
            (junk, junk_b, ss, ss_b, rs, rs_b, xn, xn_b) = ph_tiles
            for i in range(4):
                op("act", lambda: nc.scalar.activation(out=junk[:], in_=xt[:, i, :], func=AF.Square, accum_out=ss[:, i:i + 1]),
                   r=[xt_b], w=[junk_b, ss_b])
            op("dve", lambda: nc.vector.tensor_scalar(out=rs[:], in0=ss[:], scalar1=1.0 / D, scalar2=EPS, op0=ALU.mult, op1=ALU.add),
               r=[ss_b], w=[rs_b])
            op("act", lambda: nc.scalar.activation(out=rs[:], in_=rs[:], func=AF.Sqrt), r=[rs_b], w=[rs_b])
            op("dve", lambda: nc.vector.reciprocal(out=rs[:], in_=rs[:]), r=[rs_b], w=[rs_b])
            for i in range(4):
                op("act", lambda: nc.scalar.activation(out=xn[:, i, :], in_=xt[:, i, :], func=AF.Copy, scale=rs[:, i:i + 1]),
                   r=[xt_b, rs_b], w=[xn_b])
            for k in range(8):
                pt, pt_b = self.bank()
                ptb = pt.bitcast(BF16)
                for i in range(4):
                    op("pe", lambda: nc.tensor.transpose(ptb[:, i * 128:(i + 1) * 128], xn[:, i, k * 128:(k + 1) * 128], identb[:]),
                       r=[xn_b, identb_b], w=[pt_b], inc=(i == 3))
                op("dve", lambda: nc.vector.tensor_scalar(out=dstT[:, k, col0:col0 + 512], in0=ptb[:, 0:512],
                                                          scalar1=AB[:, ai, k, r_:r_ + 1], scalar2=AB[:, ai + 1, k, r_:r_ + 1],
                                                          op0=ALU.mult, op1=ALU.add),
                   r=[pt_b, AB_b], w=[dstT_b])

        def load_x(xt, xt_b, pt, pt_b, g):
            self.load(xt[:], xt_b, x_all[g * 512:(g + 1) * 512, :].rearrange("(i p) d -> p i d", p=128))
            if g < 4:
                self.load(pt[:], pt_b, pos[g * 512:(g + 1) * 512, :].rearrange("(i p) d -> p i d", p=128))
                op("dve", lambda: nc.vector.tensor_tensor(out=xt[:], in0=xt[:], in1=pt[:], op=ALU.add), r=[xt_b, pt_b], w=[xt_b])

        phLF = es.enter_context(ExitStack())
        LI, LI_b = self.sb(phLF, "LI", [36, TT], F32)
        FR, FR_b = self.sb(phLF, "FR", [36, TT], F32)
        with ExitStack() as ph:
            hnT, hnT_b = self.sb(ph, "hnT", [128, 8, TT], BF16)
            with ExitStack() as ph1:
                xts = [self.sb(ph1, f"xt{i}", [128, 4, D], F32) for i in range(3)]
                pts = [self.sb(ph1, f"pt{i}", [128, 4, D], F32) for i in range(2)]
                junk, junk_b = self.sb(ph1, "junk", [128, D], BF16)
                ss, ss_b = self.sb(ph1, "ss", [128, 4], F32)
                rs, rs_b = self.sb(ph1, "rs", [128, 4], F32)
                xns = [self.sb(ph1, f"xn{i}", [128, 4, D], BF16) for i in range(2)]
                sss = [self.sb(ph1, f"ssA{i}", [128, 4], F32) for i in range(2)]
                rss = [self.sb(ph1, f"rsA{i}", [128, 4], F32) for i in range(2)]
                load_x(*xts[0], *pts[0], 0)
                load_x(*xts[1], *pts[1], 1)
                for g in range(6):
                    if g + 2 < 6:
                        load_x(*xts[(g + 2) % 3], *pts[(g + 2) % 2], g + 2)
                    xt, xt_b = xts[g % 3]
                    norm_to_fm((junk, junk_b, *sss[g % 2], *rss[g % 2], *xns[g % 2]), xt, xt_b, hnT, hnT_b, g * 512, 0, 0 if g < 4 else 1)
                kb.barrier()
            if DEBUG:
                self.store(dbg_hn.rearrange("(k p) t -> p k t", p=128), dbg_hn_b, hnT[:], hnT_b)
            if self.stop_after == 1:
                kb.barrier()
                return self.finish()
            with ExitStack() as ph2:
                wsl = [self.sb(ph2, f"wsl{i}", [128, 8, 512], BF16) for i in range(3)]
                wg, wg_b = self.sb(ph2, "wg", [128, 8, 72], BF16)
                bg, bg_b = self.sb(ph2, "bg", [36, 2], F32)
                browb, browb_b = self.sb(ph2, "browb", [128, 3 * D], F32)
                stF = [self.sb(ph2, f"stF{i}", [128, TT], F32) for i in range(2)]
                stB = [self.sb(ph2, f"stB{i}", [128, TT], BF16) for i in range(2)]
                stT = [self.sb(ph2, f"stT{i}", [128, 12, 512], BF16) for i in range(2)]
                tmpT, tmpT_b = self.sb(ph2, "tmpT", [128, 512], F32)
                self.load(bg[:], bg_b, b_gate)
                self.load(browb[:], browb_b, brow.partition_broadcast(128))
                wslot_load(wg, wg_b, w_gate, 72, 0)
                for tt in range(6):
                    for gi, dst, dst_b in ((0, LI, LI_b), (1, FR, FR_b)):
                        pg, pg_b = self.bank()
                        for k in range(8):
                            op("pe", lambda: nc.tensor.matmul(pg[0:36, :], lhsT=wg[:, k, gi * 36:(gi + 1) * 36], rhs=hnT[:, k, tt * 512:(tt + 1) * 512],
                                                              start=(k == 0), stop=(k == 7)), r=[wg_b, hnT_b], w=[pg_b], inc=(k == 7))
                        op("act", lambda: nc.scalar.activation(out=dst[:, tt * 512:(tt + 1) * 512], in_=pg[0:36, :], func=AF.Identity,
                                                               bias=bg[:, gi:gi + 1]), r=[pg_b, bg_b], w=[dst_b])
                si = [0, 0, 0, 0]
                fm_jobs = [(CQ, 2, BQ, qT_d, qT_b, 0, "q"), (CK, 2, BK, kT_d, kT_b, 0, "k"), (CU, 6, BU, uT_d, uT_b, 0, "u"),
                           (CGM, 2, BGM, sg_d, sg_b, 0, "g"), (CGH, 2, BGH, sg_d, sg_b, D, "g")]
                for (c0, nb, bc, dst, dst_b, row0, kind) in fm_jobs:
                    for blk in range(nb):
                        wt, wb = wsl[si[0] % 3]; si[0] += 1
                        wslot_load(wt, wb, w_in, 512, c0 + blk * 512)
                        for m in range(4):
                            ch = blk * 4 + m
                            if kind == "u":
                                st, st_b = stF[si[1] % 2]; si[1] += 1
                            else:
                                st, st_b = stB[si[2] % 2]; si[2] += 1
                            for tt in range(6):
                                pp, pp_b = self.bank()
                                for k in range(8):
                                    op("pe", lambda: nc.tensor.matmul(pp[:], lhsT=wt[:, k, m * 128:(m + 1) * 128], rhs=hnT[:, k, tt * 512:(tt + 1) * 512],
                                                                      start=(k == 0), stop=(k == 7)), r=[wb, hnT_b], w=[pp_b], inc=(k == 7))
                                o_ = st[:, tt * 512:(tt + 1) * 512]
                                b_ = bfm[:, bc + ch:bc + ch + 1]
                                if kind == "q":
                                    op("dve", lambda: nc.vector.tensor_scalar(out=o_, in0=pp[:], scalar1=b_, scalar2=1.0 / 16.0, op0=ALU.add, op1=ALU.mult),
                                       r=[pp_b, bfm_b], w=[st_b])
                                elif kind == "g":
                                    op("act", lambda: nc.scalar.activation(out=o_, in_=pp[:], func=AF.Sigmoid, bias=b_), r=[pp_b, bfm_b], w=[st_b])
                                else:
                                    eng = "act" if tt % 2 == 0 else "dve"
                                    if eng == "act":
                                        op("act", lambda: nc.scalar.activation(out=o_, in_=pp[:], func=AF.Identity, bias=b_), r=[pp_b, bfm_b], w=[st_b])
                                    else:
                                        op("dve", lambda: nc.vector.tensor_scalar(out=o_, in0=pp[:], scalar1=b_, scalar2=None, op0=ALU.add),
                                           r=[pp_b, bfm_b], w=[st_b])
                            self.store(dst[row0 + ch * 128: row0 + (ch + 1) * 128, :], dst_b, st[:], st_b)
                for j, c0 in ((1, CV), (2, CO)):
                    for blk in range(2):
                        wt, wb = wsl[si[0] % 3]; si[0] += 1
                        wslot_load(wt, wb, w_in, 512, c0 + blk * 512)
                        for hlf in range(2):
                            st, st_b = stT[si[3] % 2]; si[3] += 1
                            for ti in range(12):
                                tk = hlf * 12 + ti
                                pp, pp_b = self.bank()
                                for k in range(8):
                                    op("pe", lambda: nc.tensor.matmul(pp[:], lhsT=hnT[:, k, tk * 128:(tk + 1) * 128], rhs=wt[:, k, :],
                                                                      start=(k == 0), stop=(k == 7)), r=[wb, hnT_b], w=[pp_b], inc=(k == 7))
                                bb_ = browb[:, j * D + blk * 512: j * D + (blk + 1) * 512]
                                if j == 1:
                                    op("dve", lambda: nc.vector.tensor_tensor(out=st[:, ti, :], in0=pp[:], in1=bb_, op=ALU.add),
                                       r=[pp_b, browb_b], w=[st_b])
                                else:
                                    op("dve", lambda: nc.vector.tensor_tensor(out=tmpT[:], in0=pp[:], in1=bb_, op=ALU.add),
                                       r=[pp_b, browb_b], w=[tmpT_b])
                                    op("act", lambda: nc.scalar.activation(out=st[:, ti, :], in_=tmpT[:], func=AF.Sigmoid), r=[tmpT_b], w=[st_b])
                            self.store(kv_d[j, hlf * 1536:(hlf + 1) * 1536, blk * 512:(blk + 1) * 512].rearrange("(n p) c -> p n c", p=128),
                                       kv_b, st[:], st_b)
                kb.barrier()
            if DEBUG:
                self.store(dbg_row[0], dbg_row_b, LI[:], LI_b)
                self.store(dbg_row[1], dbg_row_b, FR[:], FR_b)
            if self.stop_after == 2:
                kb.barrier()
                return self.finish()
        pmB, genP = self.phaseB(None, LI, LI_b, FR, FR_b, locals())
        if self.stop_after in (3, 4):
            for _ in genP:
                pass
            kb.barrier()
            pmB.close()
            return self.finish()
        self.phaseC(locals(), genP, pmB, "conv")
        phTSC.close()
        if self.stop_after == 5:
            return self.finish()
        self.phaseD(locals())
        return self.finish()

    def finish(self):
        kb = self.kb
        kb.barrier()
        return self.nc


_CONST_CACHE = {}


def _consts():
    if _CONST_CACHE:
        return _CONST_CACHE
    c = {}
    c["pos"] = _pos_table()
    c["identf"] = np.eye(128, dtype=np.float32)
    c["identb"] = _bf(np.eye(128))
    s = np.arange(128)[:, None]
    j = np.arange(128)[None, :]
    c["masks"] = _bf(np.stack([(s <= j), (s >= j)], axis=1).astype(np.float32))
    sel = np.zeros((2, 2, 128), np.float32)
    sel[0, 0, :] = 1.0
    sel[1, 1, :] = 1.0
    c["sel"] = sel
    cm = np.ones(TT, np.float32)
    cm[::128] = 0.0
    c["cmask"] = cm
    c["deltas"] = np.abs(np.linspace(math.log(1e-2) / 1.5, math.log(1e-2) / 0.3, D, dtype=np.float32)).astype(np.float32)
    for nm_, L, N, ni in (("S", TS_, 3072, 256), ("P", LP, 512, 256)):
        zf, dist = _filt_feats(L)
        c["zf" + nm_] = zf
        c["nd" + nm_] = np.ascontiguousarray((-dist).reshape(L // 128, 128).T)
        FT, GT = _dft_mats(L, N)
        nt, nf = L // 128, N // 128
        c["FT" + nm_] = np.ascontiguousarray(FT.reshape(nt, 128, nf, 128).transpose(2, 1, 0, 3).reshape(nf, 128, nt * 128))
        ntile = L // ni
        c["GT" + nm_] = np.ascontiguousarray(GT.reshape(nf, 128, ntile, ni).transpose(2, 1, 0, 3).reshape(ntile, 128, nf * ni))
    _CONST_CACHE.update(c)
    return c


def _fm(v):
    return np.ascontiguousarray(np.asarray(v, np.float32).reshape(-1, 128).T)


def _host_maps(inp):
    c = _consts()
    f = lambda a: np.ascontiguousarray(np.asarray(a, dtype=np.float32))
    w_in = f(inp["w_in"][0]); b_in = f(inp["b_in"][0])
    wg = np.zeros((D, 72), np.float32); bgt = np.zeros((36, 2), np.float32)
    g0 = CG
    wg[:, 0:4] = w_in[:, g0:g0 + 4]; wg[:, 32:36] = w_in[:, g0 + 8:g0 + 12]
    wg[:, 36:40] = w_in[:, g0 + 4:g0 + 8]; wg[:, 68:72] = w_in[:, g0 + 12:g0 + 16]
    bgt[0:4, 0] = b_in[g0:g0 + 4]; bgt[32:36, 0] = b_in[g0 + 8:g0 + 12]
    bgt[0:4, 1] = b_in[g0 + 4:g0 + 8]; bgt[32:36, 1] = b_in[g0 + 12:g0 + 16]
    bfm = np.zeros((128, NBF), np.float32)
    bfm[:, BQ:BQ + 8] = _fm(b_in[CQ:CQ + D]); bfm[:, BK:BK + 8] = _fm(b_in[CK:CK + D])
    bfm[:, BU:BU + 24] = _fm(b_in[CU:CU + 3 * D])
    bfm[:, BGM:BGM + 8] = _fm(b_in[CGM:CGM + D]); bfm[:, BGH:BGH + 8] = _fm(b_in[CGH:CGH + D])
    bfm[:, BM1:BM1 + 32] = _fm(inp["b_mlp1"][0])
    bfm[:, N1W:N1W + 8] = _fm(inp["norm1_w"][0]); bfm[:, N2W:N2W + 8] = _fm(inp["norm2_w"][0])
    bfm[:, BADA:BADA + 48] = _fm(inp["b_ada"][0])
    cw = f(inp["hy_conv_w"][0])
    for j_ in range(3):
        bfm[:, CW + j_ * 24:CW + (j_ + 1) * 24] = _fm(cw[j_])
    bfm[:, CB:CB + 24] = _fm(inp["hy_conv_b"][0])
    sk = f(inp["hy_skip"][0])
    bfm[:, SKIP:SKIP + 8] = _fm(sk[0]); bfm[:, SKIP + 8:SKIP + 16] = _fm(sk[1])
    fbv = np.stack([f(inp["filt_b1"][0]), f(inp["filt_freq1"][0]), f(inp["filt_b2"][0]), f(inp["filt_freq2"][0])], axis=1)
    shared = {
        "pos": c["pos"], "w_ada": f(inp["w_ada"][0]), "b_ada_row": f(inp["b_ada"][0]), "w_in": w_in, "w_gate": wg, "b_gate": bgt,
        "bfm": bfm, "brow": np.ascontiguousarray(np.concatenate([b_in[CK:CK + D], b_in[CV:CV + D], b_in[CO:CO + D]])),
        "mnw": f(inp["mlstm_norm_w"][0]), "b2row": f(inp["b_mlp2"][0]), "fnw": f(inp["final_norm_w"]),
        "w_br_m": f(inp["w_br_m"][0]), "w_br_h": f(inp["w_br_h"][0]), "w_out": f(inp["w_out"][0]),
        "w_mlp1": f(inp["w_mlp1"][0]), "w_mlp2": f(inp["w_mlp2"][0]),
        "fw1": f(inp["filt_w1"][0]), "fw2": f(inp["filt_w2"][0]), "fw3": f(inp["filt_w3"][0]), "fbv": np.ascontiguousarray(fbv),
        "identf": c["identf"], "identb": c["identb"], "masks": c["masks"], "sel": c["sel"], "cmask": c["cmask"],
        "deltas": c["deltas"], "zfS": c["zfS"], "zfP": c["zfP"], "ndS": c["ndS"], "ndP": c["ndP"],
        "FTS": c["FTS"], "FTP": c["FTP"], "GTS": c["GTS"], "GTP": c["GTP"],
    }
    xp = f(inp["x_prompt"]); xs = f(inp["x_sample"])
    maps = []
    for i in range(NCORES):
        m = dict(shared)
        m["x_all"] = np.ascontiguousarray(np.concatenate([xs[i], xp[4 * i:4 * i + 4].reshape(TP_, D)], axis=0))
        m["C0"] = f(inp["state_mlstm_C"][i, 0]); m["n0"] = f(inp["state_mlstm_n"][i, 0]); m["m0"] = f(inp["state_mlstm_m"][i, 0])
        cv = np.stack([f(inp["c"][i]), f(inp["c_ctx"])], axis=0)
        m["cvT"] = np.ascontiguousarray(cv.reshape(2, 8, 128).transpose(2, 1, 0))
        maps.append(m)
    return maps


_PROG = {}


def _get_prog(stop_after=None):
    key = (stop_after, DEBUG)
    if key not in _PROG:
        p = Prog(stop_after)
        p.build()
        p.es.close()
        print("program built: instructions", p.kb.nins, "semaphores", p.kb.nsem, flush=True)
        _PROG[key] = p
    return _PROG[key]


def run(inp, stop_after=None, core_ids=None, trace=False):
    p = _get_prog(stop_after)
    maps = _host_maps(inp)
    used = set(p.din.keys())
    maps = [{k: v for k, v in m.items() if k in used} for m in maps]
    cids = list(range(NCORES)) if core_ids is None else core_ids
    res = run_bass_kernel_spmd(p.nc, [maps[i] for i in cids], core_ids=list(range(len(cids))), trace=trace)
    return res


def kernel(**inp):
    res = run(inp).results
    y_p = np.zeros((32, LP, D), np.float32); y_s = np.zeros((NCORES, TS_, D), np.float32)
    nC = np.zeros((32, 1, 2, NH, HD, HD), np.float32); nn = np.zeros((32, 1, 2, NH, HD), np.float32)
    nm = np.zeros((32, 1, 2, NH), np.float32)
    for i in range(NCORES):
        r = res[i]
        y_s[i] = r["y_all"][:TS_]
        y_p[4 * i:4 * i + 4] = r["y_all"][TS_:].reshape(4, LP, D)
        nC[4 * i:4 * i + 4, 0] = r["nC"]; nn[4 * i:4 * i + 4, 0] = r["nn"]; nm[4 * i:4 * i + 4, 0] = r["nm"]
    return (y_p, y_s, nC, nn, nm)


def _phaseB(self, ph, LI, LI_b, FR, FR_b, L_):
    nc, kb = self.nc, self.kb
    op = kb.op
    g = L_
    identf, identf_b, identb, identb_b, masks, masks_b = g["identf"], g["identf_b"], g["identb"], g["identb_b"], g["masks"], g["masks_b"]
    TSC, TSC_b = g["TSC"], g["TSC_b"]
    V3 = lambda t: t[:].rearrange("p (c j) -> p c j", j=128)
    with ExitStack() as pr:
        R = [self.sb(pr, f"R{i}", [36, TT], F32) for i in range(8)]
        cm, cm_b = self.sb(pr, "cm", [36, TT], F32)
        sm = {n: self.sb(pr, n, [36, NCH], F32) for n in ("tot", "a", "MS", "ML", "DEC", "tmp")}
        MF, MF_b = self.sb(pr, "MF", [36, NSEQP], F32)
        m0t, m0t_b = self.sb(pr, "m0t", [36, 1], F32)
        self.load(cm[:], cm_b, g["cmask_d"].partition_broadcast(36))
        op("dve", lambda: nc.vector.memset(m0t[:], 0.0), w=[m0t_b])
        self.load(m0t[0:4, :], m0t_b, g["m0"][0:1, :].rearrange("o h -> h o"))
        self.load(m0t[32:36, :], m0t_b, g["m0"][1:2, :].rearrange("o h -> h o"))
        (R0, R0b), (R1, R1b), (R2, R2b), (R3, R3b), (R4, R4b), (R5, R5b), (R6, R6b), (R7, R7b) = R
        tot, tot_b = sm["tot"]; a_, a_b = sm["a"]; MS, MS_b = sm["MS"]; ML, ML_b = sm["ML"]; DEC, DEC_b = sm["DEC"]; tmp, tmp_b = sm["tmp"]
        bc = lambda t: t[:].unsqueeze(2).to_broadcast([36, NCH, 128])
        op("act", lambda: nc.scalar.activation(out=R0[:], in_=FR[:], func=AF.Exp, scale=-1.0), r=[FR_b], w=[R0b])
        op("act", lambda: nc.scalar.activation(out=R0[:], in_=R0[:], func=AF.Ln, bias=1.0), r=[R0b], w=[R0b])
        op("dve", lambda: nc.vector.tensor_scalar(out=R0[:], in0=R0[:], scalar1=-1.0, scalar2=None, op0=ALU.mult), r=[R0b], w=[R0b])
        op("dve", lambda: nc.vector.tensor_tensor_scan(out=R1[:], data0=cm[:], data1=R0[:], initial=0.0, op0=ALU.mult, op1=ALU.add),
           r=[cm_b, R0b], w=[R1b])
        op("dve", lambda: nc.vector.tensor_copy(out=tot[:], in_=V3(R1)[:, :, 127]), r=[R1b], w=[tot_b])
        op("dve", lambda: nc.vector.tensor_tensor(out=R2[:], in0=R0[:], in1=R1[:], op=ALU.subtract), r=[R0b, R1b], w=[R2b])
        op("dve", lambda: nc.vector.tensor_tensor(out=V3(R2), in0=V3(R2), in1=bc(tot), op=ALU.add), r=[R2b, tot_b], w=[R2b])
        op("dve", lambda: nc.vector.tensor_copy(out=R3[0:32, :], in_=R1[0:32, :]), r=[R1b], w=[R3b])
        op("dve", lambda: nc.vector.tensor_copy(out=R3[32:36, :], in_=R2[32:36, :]), r=[R2b], w=[R3b])
        op("dve", lambda: nc.vector.tensor_tensor(out=R4[:], in0=LI[:], in1=R3[:], op=ALU.subtract), r=[LI_b, R3b], w=[R4b])
        op("dve", lambda: nc.vector.tensor_copy(out=R5[:], in_=R4[:]), r=[R4b], w=[R5b])
        src, srcb, dst, dstb = R5, R5b, R6, R6b
        for d_ in (1, 2, 4, 8, 16, 32, 64):
            op("dve", lambda: nc.vector.tensor_copy(out=dst[:], in_=src[:]), r=[srcb], w=[dstb])
            op("dve", lambda: nc.vector.tensor_tensor(out=V3(dst)[0:32, :, d_:], in0=V3(src)[0:32, :, d_:], in1=V3(src)[0:32, :, :128 - d_], op=ALU.max),
               r=[srcb], w=[dstb])
            op("dve", lambda: nc.vector.tensor_tensor(out=V3(dst)[32:36, :, :128 - d_], in0=V3(src)[32:36, :, :128 - d_], in1=V3(src)[32:36, :, d_:], op=ALU.max),
               r=[srcb], w=[dstb])
            src, srcb, dst, dstb = dst, dstb, src, srcb
        PX, PXb = src, srcb
        FREE, FREEb = dst, dstb
        op("dve", lambda: nc.vector.tensor_copy(out=a_[0:32, :], in_=V3(PX)[0:32, :, 127]), r=[PXb], w=[a_b])
        op("dve", lambda: nc.vector.tensor_copy(out=a_[32:36, :], in_=V3(PX)[32:36, :, 0]), r=[PXb], w=[a_b])
        op("dve", lambda: nc.vector.memset(MS[:], 0.0), w=[MS_b])
        op("dve", lambda: nc.vector.memset(MF[:], 0.0), w=[MF_b])

        def step(rows, dst_ap, dstbuf, src_c):
            p0, p1 = rows
            op("dve", lambda: nc.vector.tensor_tensor(out=tmp[p0:p1, src_c], in0=MS[p0:p1, src_c], in1=a_[p0:p1, src_c], op=ALU.max),
               r=[MS_b, a_b], w=[tmp_b])
            op("dve", lambda: nc.vector.tensor_tensor(out=dst_ap, in0=tmp[p0:p1, src_c], in1=tot[p0:p1, src_c], op=ALU.add),
               r=[tmp_b, tot_b], w=[dstbuf])

        nS = TS_ // 128
        op("dve", lambda: nc.vector.tensor_copy(out=MS[0:4, 0:1], in_=m0t[0:4, :]), r=[m0t_b], w=[MS_b])
        op("dve", lambda: nc.vector.tensor_copy(out=MS[32:36, nS - 1:nS], in_=m0t[32:36, :]), r=[m0t_b], w=[MS_b])
        for c in range(nS - 1):
            step((0, 4), MS[0:4, c + 1:c + 2], MS_b, slice(c, c + 1))
            cb_ = nS - 1 - c
            step((32, 36), MS[32:36, cb_ - 1:cb_], MS_b, slice(cb_, cb_ + 1))
        ev, od = slice(nS, NCH, 2), slice(nS + 1, NCH, 2)
        step((0, 4), MS[0:4, od], MS_b, ev)
        step((0, 4), MF[0:4, :], MF_b, od)
        step((32, 36), MS[32:36, ev], MS_b, od)
        step((32, 36), MF[32:36, :], MF_b, ev)
        op("dve", lambda: nc.vector.tensor_tensor(out=ML[:], in0=MS[:], in1=a_[:], op=ALU.max), r=[MS_b, a_b], w=[ML_b])
        op("dve", lambda: nc.vector.tensor_tensor(out=DEC[:], in0=MS[:], in1=ML[:], op=ALU.subtract), r=[MS_b, ML_b], w=[DEC_b])
        op("act", lambda: nc.scalar.activation(out=DEC[:], in_=DEC[:], func=AF.Exp), r=[DEC_b], w=[DEC_b])
        op("dve", lambda: nc.vector.tensor_tensor(out=V3(R7), in0=V3(PX), in1=bc(MS), op=ALU.max), r=[PXb, MS_b], w=[R7b])
        op("dve", lambda: nc.vector.tensor_tensor(out=V3(R1), in0=bc(ML), in1=V3(R7), op=ALU.subtract), r=[ML_b, R7b], w=[R1b])
        op("act", lambda: nc.scalar.activation(out=R1[:], in_=R1[:], func=AF.Exp), r=[R1b], w=[R1b])
        op("dve", lambda: nc.vector.tensor_tensor(out=V3(R2), in0=bc(MS), in1=V3(R7), op=ALU.subtract), r=[MS_b, R7b], w=[R2b])
        op("act", lambda: nc.scalar.activation(out=R2[:], in_=R2[:], func=AF.Exp), r=[R2b], w=[R2b])
        op("dve", lambda: nc.vector.tensor_tensor(out=FREE[:], in0=R3[:], in1=R7[:], op=ALU.add), r=[R3b, R7b], w=[FREEb])
        op("act", lambda: nc.scalar.activation(out=FREE[:], in_=FREE[:], func=AF.Exp, scale=-1.0), r=[FREEb], w=[FREEb])
        op("dve", lambda: nc.vector.tensor_tensor(out=V3(R0), in0=V3(R4), in1=bc(ML), op=ALU.subtract), r=[R4b, ML_b], w=[R0b])
        op("act", lambda: nc.scalar.activation(out=R0[:], in_=R0[:], func=AF.Exp), r=[R0b], w=[R0b])
        op("dve", lambda: nc.vector.tensor_copy(out=V3(R3), in_=bc(DEC)), r=[DEC_b, FREEb], w=[R3b])
        quants = [(R0, R0b), (R1, R1b), (R2, R2b), (FREE, FREEb), (R3, R3b)]
        if DEBUG:
            for qi, (t_, tb_) in enumerate(quants + [(R7, R7b)]):
                self.store(g["dbg_row"][2 + qi], g["dbg_row_b"], t_[:], tb_)
        for c in range(NCH):
            pt, pt_b = self.bank()
            for qi, (t_, tb_) in enumerate(quants):
                op("pe", lambda: nc.tensor.transpose(pt[:, qi * 36:(qi + 1) * 36], t_[0:36, c * 128:(c + 1) * 128], identf[0:36, 0:36]),
                   r=[tb_, identf_b], w=[pt_b], inc=(qi == 4))
            pv = pt[:, 0:180].rearrange("p (q r) -> p q r", r=36)
            op("act", lambda: nc.scalar.copy(out=TSC[:, c, :, 0:4], in_=pv[:, :, 0:4]), r=[pt_b], w=[TSC_b])
            op("dve", lambda: nc.vector.tensor_copy(out=TSC[:, c, :, 4:8], in_=pv[:, :, 32:36]), r=[pt_b], w=[TSC_b])
        nm_b = Buf("nm")
        self.store(g["nm"][:, 0, :].rearrange("s h -> h s"), nm_b, MF[0:4, :], MF_b, allow_slow_non_contiguous=True)
        self.store(g["nm"][:, 1, :].rearrange("s h -> h s"), nm_b, MF[32:36, :], MF_b, allow_slow_non_contiguous=True)
        kb.barrier()
    g["phLF"].close()
    if self.stop_after == 3:
        kb.barrier()
        return

    WSq, Uq, WIq, ENq, DECq = 0, 1, 2, 3, 4
    pm_ = ExitStack()
    pmi = ExitStack()
    if True:
        nS = TS_ // 128
        NJ = 4
        small = []
        for i in range(NJ):
            s_ = {"qT": self.sb(pm_, f"qTs{i}", [128, 2, LP], BF16), "kT": self.sb(pm_, f"kTs{i}", [128, 2, LP], BF16),
                  "vt": self.sb(pm_, f"vts{i}", [128, LP // 128, 257], BF16), "ktok": self.sb(pm_, f"ktoks{i}", [128, LP // 128, 256], BF16)}
            small.append(s_)
        ogP = [small[i]["ktok"] for i in range(NJ)]
        hsP = [self.sb(pm_, f"hsP{i}", [128, 2, LP], BF16) for i in range(NJ)]
        Cf = [[self.sb(pm_, f"Cf{j}_{d}", [128, 2, 257], F32) for d in range(2)] for j in range(NJ)]
        Cb = [[self.sb(pm_, f"Cb{j}_{d}", [128, 2, 257], BF16) for d in range(2)] for j in range(NJ)]
        HaccP = [self.sb(pm_, f"HaccP{i}", [128, LP // 128, 256], F32) for i in range(NJ)]
        NR = 7
        ncol, ncol_b = self.sb(pm_, "ncol", [128, 16], F32)
        nrow, nrow_b = self.sb(pm_, "nrow", [16, 128], F32)
        STs = [self.sb(pm_, f"ST{i}", [128, 128], BF16) for i in range(NR)]
        kws = [self.sb(pm_, f"kw{i}", [128, 256], BF16) for i in range(NR)]
        scs = [self.sb(pm_, f"sc{i}", [128, 4], F32) for i in range(NR)]
        mnwb, mnwb_b = self.sb(pm_, "mnwb", [128, D], F32)
        ssq, ssq_b = self.sb(pm_, "ssq", [128, nS], F32)
        junk, junk_b = self.sb(pm_, "junkB", [128, 256], BF16)
        og2s = [self.sb(pm_, f"og2{i}", [128, 256], F32) for i in range(2)]
        hfs = [self.sb(pm_, f"hf{i}", [128, 256], BF16) for i in range(2)]
        big = []
        for i in range(2):
            s_ = {"qT": self.sb(pmi, f"qT{i}", [128, 2, TS_], BF16), "kT": self.sb(pmi, f"kT{i}", [128, 2, TS_], BF16),
                  "vt": self.sb(pmi, f"vt{i}", [128, nS, 257], BF16), "ktok": self.sb(pmi, f"ktok{i}", [128, nS, 256], BF16)}
            big.append(s_)
        HaccT = [self.sb(pmi, f"Hacc{i}", [128, nS, 256], F32) for i in range(2)]
        for s_ in big + small:
            vt_, vtb_ = s_["vt"]
            op("dve", lambda: nc.vector.memset(vt_[:, :, 256:257], 1.0), w=[vtb_])
        self.load(mnwb[:], mnwb_b, g["mnw"].partition_broadcast(128))
        qT_d, kT_d, kv_d, hT_d = g["qT_d"], g["kT_d"], g["kv_d"], g["hT_d"]
        qT_b, kT_b, kv_b, hT_b = g["qT_b"], g["kT_b"], g["kv_b"], g["hT_b"]
        nC_b, nn_b = Buf("nC"), Buf("nn")
        seqs = [(0, TS_)] + [(TS_ + i * LP, LP) for i in range(NSEQP)]
        rot = [0]

        def run_groups(groups, bidx, bsz):
            XB = [(self.banks[bidx[2 * p]], self.bankb[bidx[2 * p]]) for p in range(bsz)]
            ZB = [(self.banks[bidx[2 * p + 1]], self.bankb[bidx[2 * p + 1]]) for p in range(bsz)]
            def issue_loads(grp):
                for js, (si, h) in enumerate(grp):
                    t0, L = seqs[si]
                    nck = L // 128
                    s_ = big[js] if si == 0 else small[js]
                    r0 = h * 256
                    self.load(s_["qT"][0][:, :, 0:L], s_["qT"][1], qT_d[r0:r0 + 256, t0:t0 + L].rearrange("(c p) t -> p c t", p=128), qT_b)
                    self.load(s_["kT"][0][:, :, 0:L], s_["kT"][1], kT_d[r0:r0 + 256, t0:t0 + L].rearrange("(c p) t -> p c t", p=128), kT_b)
                    self.load(s_["vt"][0][:, 0:nck, 0:256], s_["vt"][1], kv_d[1, t0:t0 + L, r0:r0 + 256].rearrange("(n p) c -> p n c", p=128), kv_b)

            issue_loads(groups[0])
            for gidx, grp in enumerate(groups):
                ctx = []
                for js, (si, h) in enumerate(grp):
                    t0, L = seqs[si]
                    nck = L // 128
                    s_ = big[js] if si == 0 else small[js]
                    r0 = h * 256
                    for d in range(2):
                        cf, cfb = Cf[js][d]
                        if si == 0:
                            self.load(cf[:, :, 0:256], cfb, g["C0"][d, h].rearrange("(c p) v -> p c v", p=128))
                        else:
                            op("dve", lambda: nc.vector.memset(cf[:], 0.0), w=[cfb])
                    hs, hsb = s_["kT"] if si == 0 else hsP[js]
                    if si == 0:
                        for d in range(2):
                            self.load(nrow[2 * d:2 * d + 2, :], nrow_b, g["n0"][d, h].rearrange("(c p) -> c p", p=128))
                        ptn0, ptn0b = self.bank()
                        op("pe", lambda: nc.tensor.transpose(ptn0[:, 0:4], nrow[0:4, :], identf[0:4, 0:4]), r=[nrow_b, identf_b], w=[ptn0b])
                        for d in range(2):
                            cf, cfb = Cf[js][d]
                            op("act", lambda: nc.scalar.copy(out=cf[:, :, 256], in_=ptn0[:, 2 * d:2 * d + 2]), r=[ptn0b], w=[cfb])
                    hacc, haccb = HaccT[js] if si == 0 else HaccP[js]
                    og, ogb = s_["ktok"]
                    kT, kTb = s_["kT"]; ktok, ktokb = s_["ktok"]
                    for c in range(nck):
                        pk, pkb = self.bank()
                        pkv = pk.bitcast(BF16)
                        for dk in range(2):
                            op("pe", lambda: nc.tensor.transpose(pkv[:, dk * 128:(dk + 1) * 128], kT[:, dk, c * 128:(c + 1) * 128], identb[:]),
                               r=[kTb, identb_b], w=[pkb], inc=(dk == 1))
                        op("act", lambda: nc.scalar.copy(out=ktok[:, c, :], in_=pkv[:, 0:256]), r=[pkb], w=[ktokb])
                    ctx.append(dict(si=si, h=h, t0=t0, L=L, nck=nck, s=s_, js=js, hacc=hacc, haccb=haccb, hs=hs, hsb=hsb, og=og, ogb=ogb, touched=set()))
                    yield
                nck = ctx[0]["nck"]
                pairs = [(c_, d) for c_ in ctx for d in range(2)]
                batches = [pairs[b:b + bsz] for b in range(0, len(pairs), bsz)]
                for i in range(nck):
                    for batch in batches:
                        st = []
                        for p, (c_, d) in enumerate(batch):
                            js, h, s_ = c_["js"], c_["h"], c_["s"]
                            c = i if d == 0 else nck - 1 - i
                            cg = c_["t0"] // 128 + c
                            col = d * 4 + h
                            ri = rot[0] % NR; rot[0] += 1
                            e = dict(c=c, cg=cg, col=col, tk=slice(c * 128, (c + 1) * 128), X=XB[p], Z=ZB[p], ST=STs[ri], kw=kws[ri], sc=scs[ri],
                                     cf=Cf[js][d], cb=Cb[js][d], s=s_, c_=c_, d=d)
                            st.append(e)
                            qT, qTb = s_["qT"]; kT, kTb = s_["kT"]
                            X, Xb = e["X"]
                            for dk in range(2):
                                op("pe", lambda: nc.tensor.matmul(X[:, 0:128], lhsT=kT[:, dk, e["tk"]], rhs=qT[:, dk, e["tk"]], start=(dk == 0), stop=(dk == 1)),
                                   r=[kTb, qTb], w=[Xb], inc=(dk == 1))
                        yield
                        for e in st:
                            S = lambda q: TSC[:, e["cg"], q, e["col"]:e["col"] + 1]
                            X, Xb = e["X"]; ST, STb = e["ST"]; kw, kwb = e["kw"]; cf, cfb = e["cf"]; cbt, cbb = e["cb"]
                            ktok, ktokb = e["s"]["ktok"]
                            op("dve", lambda: nc.vector.scalar_tensor_tensor(out=ST[:], in0=X[:, 0:128], scalar=S(WSq), in1=masks[:, e["d"], :],
                                                                             op0=ALU.mult, op1=ALU.mult), r=[Xb, TSC_b, masks_b], w=[STb])
                            op("act", lambda: nc.scalar.activation(out=cbt[:], in_=cf[:], func=AF.Copy, scale=S(DECq)), r=[cfb, TSC_b], w=[cbb])
                            op("act", lambda: nc.scalar.activation(out=kw[:], in_=ktok[:, e["c"], :], func=AF.Copy, scale=S(WSq)), r=[ktokb, TSC_b], w=[kwb])
                        yield
                        for e in st:
                            X, Xb = e["X"]; Z, Zb = e["Z"]; ST, STb = e["ST"]; kw, kwb = e["kw"]; cbt, cbb = e["cb"]
                            qT, qTb = e["s"]["qT"]; vt, vtb = e["s"]["vt"]
                            c = e["c"]
                            op("pe", lambda: nc.tensor.matmul(X[:, 128:385], lhsT=ST[:], rhs=vt[:, c, :], start=True, stop=False), r=[STb, vtb], w=[Xb], inc=False)
                            for dk in range(2):
                                op("pe", lambda: nc.tensor.matmul(X[:, 128:385], lhsT=qT[:, dk, e["tk"]], rhs=cbt[:, dk, :], start=False, stop=(dk == 1)),
                                   r=[qTb, cbb], w=[Xb], inc=False)
                            for dk in range(2):
                                op("pe", lambda: nc.tensor.matmul(X[:, 386 + 2 * dk:388 + 2 * dk], lhsT=kw[:, dk * 128:(dk + 1) * 128], rhs=vt[:, c, 255:257], start=True, stop=True),
                                   r=[kwb, vtb], w=[Xb], inc=(dk == 1))
                            for dk in range(2):
                                op("pe", lambda: nc.tensor.matmul(Z[:, dk * 256:(dk + 1) * 256], lhsT=kw[:, dk * 128:(dk + 1) * 128], rhs=vt[:, c, 0:256], start=True, stop=True),
                                   r=[kwb, vtb], w=[Zb], inc=(dk == 1))
                        yield
                        Sx = lambda e, q: TSC[:, e["cg"], q, e["col"]:e["col"] + 1]
                        for e in st:
                            X, Xb = e["X"]; Z, Zb = e["Z"]; cf, cfb = e["cf"]
                            op("dve", lambda: nc.vector.scalar_tensor_tensor(out=cf[:, :, 0:256], in0=cf[:, :, 0:256], scalar=Sx(e, DECq),
                                                                             in1=Z[:, 0:512].rearrange("p (k v) -> p k v", v=256),
                                                                             op0=ALU.mult, op1=ALU.add), r=[Zb, TSC_b, cfb], w=[cfb])
                        for e in st:
                            X, Xb = e["X"]; cf, cfb = e["cf"]
                            op("dve", lambda: nc.vector.scalar_tensor_tensor(out=cf[:, :, 256:257], in0=cf[:, :, 256:257], scalar=Sx(e, DECq),
                                                                             in1=X[:, 386:390].rearrange("p (k v) -> p k v", v=2)[:, :, 1:2],
                                                                             op0=ALU.mult, op1=ALU.add), r=[Xb, TSC_b, cfb], w=[cfb])
                        for e in st:
                            X, Xb = e["X"]; sc, scb = e["sc"]
                            op("dve", lambda: nc.vector.tensor_scalar(out=sc[:, 0:1], in0=X[:, 384:385], scalar1=Sx(e, Uq), scalar2=None, op0=ALU.mult),
                               r=[Xb, TSC_b], w=[scb])
                        for e in st:
                            sc, scb = e["sc"]
                            op("dve", lambda: nc.vector.scalar_tensor_tensor(out=sc[:, 3:4], in0=sc[:, 0:1], scalar=-1.0, in1=sc[:, 0:1],
                                                                             op0=ALU.mult, op1=ALU.max), r=[scb], w=[scb])
                        for e in st:
                            sc, scb = e["sc"]
                            op("dve", lambda: nc.vector.tensor_scalar(out=sc[:, 1:2], in0=sc[:, 3:4], scalar1=Sx(e, ENq), scalar2=None, op0=ALU.max),
                               r=[scb, TSC_b], w=[scb])
                        for e in st:
                            sc, scb = e["sc"]
                            op("dve", lambda: nc.vector.reciprocal(out=sc[:, 2:3], in_=sc[:, 1:2]), r=[scb], w=[scb])
                        for e in st:
                            sc, scb = e["sc"]
                            op("dve", lambda: nc.vector.tensor_scalar(out=sc[:, 3:4], in0=sc[:, 2:3], scalar1=Sx(e, Uq), scalar2=None, op0=ALU.mult),
                               r=[scb, TSC_b], w=[scb])
                        for e in st:
                            X, Xb = e["X"]; sc, scb = e["sc"]
                            c_, c = e["c_"], e["c"]
                            hacc, haccb = c_["hacc"], c_["haccb"]
                            if c not in c_["touched"]:
                                c_["touched"].add(c)
                                op("act", lambda: nc.scalar.activation(out=hacc[:, c, :], in_=X[:, 128:384], func=AF.Copy, scale=sc[:, 3:4]),
                                   r=[Xb, scb], w=[haccb])
                            else:
                                op("dve", lambda: nc.vector.scalar_tensor_tensor(out=hacc[:, c, :], in0=X[:, 128:384], scalar=sc[:, 3:4], in1=hacc[:, c, :],
                                                                                 op0=ALU.mult, op1=ALU.add), r=[Xb, scb, haccb], w=[haccb])
                if gidx + 1 < len(groups):
                    if grp[0][0] > 0:
                        issue_loads(groups[gidx + 1])
                if ctx[0]["si"] > 0:
                    for c_ in ctx:
                        for d in range(2):
                            cf, cfb = Cf[c_["js"]][d]
                            col = d * 8 + c_["h"] * 2
                            op("act", lambda: nc.scalar.copy(out=ncol[:, col:col + 2], in_=cf[:, :, 256]), r=[cfb], w=[ncol_b])
                    ptn, ptnb = self.bank()
                    op("pe", lambda: nc.tensor.transpose(ptn[0:16, 0:128], ncol[:, 0:16], identf[:]), r=[ncol_b, identf_b], w=[ptnb])
                    op("act", lambda: nc.scalar.copy(out=nrow[:], in_=ptn[0:16, 0:128]), r=[ptnb], w=[nrow_b])
                    self.store(g["nn"][ctx[0]["si"] - 1].rearrange("d h (c p) -> (d h c) p", p=128), nn_b, nrow[:], nrow_b)
                for c_ in ctx:
                    self.load(c_["og"][:, 0:c_["nck"], :], c_["ogb"],
                              kv_d[2, c_["t0"]:c_["t0"] + c_["L"], c_["h"] * 256:(c_["h"] + 1) * 256].rearrange("(n p) c -> p n c", p=128), kv_b)
                for c_ in ctx:
                    si, h, js, t0, L, nck = c_["si"], c_["h"], c_["js"], c_["t0"], c_["L"], c_["nck"]
                    hacc, haccb, hs, hsb = c_["hacc"], c_["haccb"], c_["hs"], c_["hsb"]
                    og, ogb = c_["og"], c_["ogb"]
                    yield
                    if si > 0:
                        for d in range(2):
                            cf, cfb = Cf[js][d]
                            self.store(g["nC"][si - 1, d, h].rearrange("(c p) v -> p c v", p=128), nC_b, cf[:, :, 0:256], cfb)
                    for c in range(nck):
                        op("act", lambda: nc.scalar.activation(out=junk[:], in_=hacc[:, c, :], func=AF.Square, accum_out=ssq[:, c:c + 1]),
                           r=[haccb], w=[junk_b, ssq_b])
                    op("dve", lambda: nc.vector.tensor_scalar(out=ssq[:, 0:nck], in0=ssq[:, 0:nck], scalar1=1.0 / HD, scalar2=EPS, op0=ALU.mult, op1=ALU.add),
                       r=[ssq_b], w=[ssq_b])
                    op("act", lambda: nc.scalar.activation(out=ssq[:, 0:nck], in_=ssq[:, 0:nck], func=AF.Sqrt), r=[ssq_b], w=[ssq_b])
                    op("dve", lambda: nc.vector.reciprocal(out=ssq[:, 0:nck], in_=ssq[:, 0:nck]), r=[ssq_b], w=[ssq_b])
                    for c in range(nck):
                        o2, o2b = og2s[c % 2]; hf, hfb = hfs[c % 2]
                        op("dve", lambda: nc.vector.tensor_tensor(out=o2[:], in0=og[:, c, :], in1=mnwb[:, h * 256:(h + 1) * 256], op=ALU.mult),
                           r=[ogb, mnwb_b], w=[o2b])
                        op("dve", lambda: nc.vector.scalar_tensor_tensor(out=hf[:], in0=hacc[:, c, :], scalar=ssq[:, c:c + 1], in1=o2[:],
                                                                         op0=ALU.mult, op1=ALU.mult), r=[haccb, ssq_b, o2b], w=[hfb])
                        pt, ptb = self.bank()
                        ptv = pt.bitcast(BF16)
                        for dv in range(2):
                            op("pe", lambda: nc.tensor.transpose(ptv[:, dv * 128:(dv + 1) * 128], hf[:, dv * 128:(dv + 1) * 128], identb[:]),
                               r=[hfb, identb_b], w=[ptb], inc=(dv == 1))
                        op("act", lambda: nc.scalar.copy(out=hs[:, :, c * 128:(c + 1) * 128], in_=ptv[:, 0:256].rearrange("p (v t) -> p v t", t=128)),
                           r=[ptb], w=[hsb])
                    self.store(hT_d[h * 256:(h + 1) * 256, t0:t0 + L].rearrange("(v p) t -> p v t", p=128), hT_b, hs[:, :, 0:L], hsb)
                if gidx + 1 < len(groups) and grp[0][0] == 0:
                    issue_loads(groups[gidx + 1])

        for _ in run_groups([[(0, 0), (0, 1)], [(0, 2), (0, 3)]], list(range(8)), 4):
            pass
        kb.barrier()
        pmi.close()
        genP = run_groups([[(si, h) for h in range(NH)] for si in range(1, 1 + NSEQP)], list(range(8)), 4)
        return pm_, genP


Prog.phaseB = _phaseB


def _phaseC(self, g, genP, pmB, mode):
    nc, kb = self.nc, self.kb
    op = kb.op
    identb, identb_b, bfm, bfm_b, onesf, onesf_b = g["identb"], g["identb_b"], g["bfm"], g["bfm_b"], g["onesf"], g["onesf_b"]
    uT_d, uT_b, yh_d, yh_b = g["uT_d"], g["uT_b"], g["yh_d"], g["yh_b"]
    PI = math.pi
    def seg_filter(sgi, pf):
        tok0, ntok, nseq, L, N = SEGS[sgi]
        nt, NF = L // 128, N // 128
        NFh = NF // 2
        FT_d, GT_d, Hs_d, Hs_b = g["FT_d"][sgi], g["GT_d"][sgi], g["Hs_d"][sgi], g["Hs_b"][sgi]
        FT_src, GT_src = Buf("FTsrc"), Buf("GTsrc")
        zf, zf_b = self.sb(pf, "zf", [33, L], F32)
        w1, w1_b = self.sb(pf, "fw1", [33, 64], F32)
        w2, w2_b = self.sb(pf, "fw2", [64, 64], F32)
        w3, w3_b = self.sb(pf, "fw3", [64, 2048], BF16)
        fb, fb_b = self.sb(pf, "fbv", [64, 4], F32)
        h1T, h1T_b = self.sb(pf, "h1T", [64, L], F32)
        h2T, h2T_b = self.sb(pf, "h2T", [64, L], BF16)
        pre, pre_b = self.sb(pf, "pre", [64, 512], F32)
        w_a, w_ab = self.sb(pf, "wra", [64, 512], F32)
        w_b, w_bb = self.sb(pf, "wrb", [64, 512], F32)
        deltab, deltab_b = self.sb(pf, "deltab", [128, D], F32)
        nd, nd_b = self.sb(pf, "nd", [128, nt], F32)
        wint, wint_b = self.sb(pf, "wint", [128, 512], F32)
        hwfs = [self.sb(pf, f"hwf{i}", [128, 512], F32) for i in range(2)]
        habss = [self.sb(pf, f"habs{i}", [128, 512], BF16) for i in range(2)]
        HWb, HWb_b = self.sb(pf, "HWb", [128, nt, 512], BF16)
        invt, invt_b = self.sb(pf, "invt", [128, 512], F32)
        FTs = [self.sb(pf, f"FTf{i}", [128, nt * 128], BF16) for i in range(4)]
        hss = [self.sb(pf, f"hss{i}", [128, 512], BF16) for i in range(3)]
        self.load(zf[:], zf_b, g["zf_d"][sgi]); self.load(w1[:], w1_b, g["fw1"]); self.load(w2[:], w2_b, g["fw2"])
        self.load(w3[:], w3_b, g["fw3"], q="pool"); self.load(fb[:], fb_b, g["fbv"]); self.load(nd[:], nd_b, g["nd_d"][sgi])
        self.load(deltab[:], deltab_b, g["deltas_d"].partition_broadcast(128))

        def sin_layer(w, wb, K, src, srcb, bcol, frcol, dst, dstb):
            for t in range(0, L, 512):
                n = min(512, L - t)
                ps, psb = self.bank()
                op("pe", lambda: nc.tensor.matmul(ps[0:64, 0:n], lhsT=w[0:K, :], rhs=src[0:K, t:t + n], start=True, stop=True),
                   r=[wb, srcb], w=[psb])
                op("dve", lambda: nc.vector.tensor_scalar(out=pre[:, 0:n], in0=ps[0:64, 0:n], scalar1=fb[:, bcol:bcol + 1],
                                                          scalar2=fb[:, frcol:frcol + 1], op0=ALU.add, op1=ALU.mult), r=[psb, fb_b], w=[pre_b])
                op("dve", lambda: nc.vector.tensor_scalar(out=w_a[:, 0:n], in0=pre[:, 0:n], scalar1=PI, scalar2=-2.0 * PI, op0=ALU.is_gt, op1=ALU.mult),
                   r=[pre_b], w=[w_ab])
                op("dve", lambda: nc.vector.tensor_scalar(out=w_b[:, 0:n], in0=pre[:, 0:n], scalar1=-PI, scalar2=2.0 * PI, op0=ALU.is_lt, op1=ALU.mult),
                   r=[pre_b], w=[w_bb])
                op("dve", lambda: nc.vector.tensor_tensor(out=pre[:, 0:n], in0=pre[:, 0:n], in1=w_a[:, 0:n], op=ALU.add), r=[pre_b, w_ab], w=[pre_b])
                op("dve", lambda: nc.vector.tensor_tensor(out=pre[:, 0:n], in0=pre[:, 0:n], in1=w_b[:, 0:n], op=ALU.add), r=[pre_b, w_bb], w=[pre_b])
                op("act", lambda: nc.scalar.activation(out=dst[:, t:t + n], in_=pre[:, 0:n], func=AF.Sin), r=[pre_b], w=[dstb])

        sin_layer(w1, w1_b, 33, zf, zf_b, 0, 1, h1T, h1T_b)
        yield
        sin_layer(w2, w2_b, 64, h1T, h1T_b, 2, 3, h2T, h2T_b)
        yield
        fi = 0
        onesb, onesb_b = self.sb(pf, "onesb", [128, 128], BF16)
        op("dve", lambda: nc.vector.memset(onesb[:], 1.0), w=[onesb_b])
        HWs = [(HWb, HWb_b), self.sb(pf, "HWb2", [128, nt, 512], BF16)]
        invs = [(invt, invt_b), self.sb(pf, "invt2", [128, 512], F32)]

        def prologue(ct):
            chh = ct % 2
            HW_, HW_b = HWs[ct % 2]; inv_, inv_b = invs[ct % 2]
            pn, pnb = self.bank(pin=True)
            pend = None
            for tc in range(nt):
                ps, psb = self.bank()
                hwf, hwf_b = hwfs[tc % 2]; habs, habs_b = habss[tc % 2]
                op("pe", lambda: nc.tensor.matmul(ps[:], lhsT=h2T[0:64, tc * 128:(tc + 1) * 128], rhs=w3[0:64, ct * 512:(ct + 1) * 512],
                                                  start=True, stop=True), r=[h2T_b, w3_b], w=[psb])
                if pend is not None:
                    ptc, pha, phab = pend
                    op("pe", lambda: nc.tensor.matmul(pn[:], lhsT=onesb[:], rhs=pha[:], start=(ptc == 0), stop=False),
                       r=[onesb_b, phab], w=[pnb], inc=True)
                op("act", lambda: nc.scalar.activation(out=wint[:], in_=deltab[:, chh * 512:(chh + 1) * 512], func=AF.Exp, scale=nd[:, tc:tc + 1]),
                   r=[deltab_b, nd_b], w=[wint_b])
                op("dve", lambda: nc.vector.tensor_tensor(out=hwf[:], in0=ps[:], in1=wint[:], op=ALU.mult), r=[psb, wint_b], w=[hwf_b])
                op("act", lambda: nc.scalar.copy(out=HW_[:, tc, :], in_=hwf[:]), r=[hwf_b], w=[HW_b])
                op("dve", lambda: nc.vector.scalar_tensor_tensor(out=habs[:], in0=hwf[:], scalar=-1.0, in1=hwf[:], op0=ALU.mult, op1=ALU.max),
                   r=[hwf_b], w=[habs_b])
                pend = (tc, habs, habs_b)
                yield
            ptc, pha, phab = pend
            op("pe", lambda: nc.tensor.matmul(pn[:], lhsT=onesb[:], rhs=pha[:], start=(ptc == 0), stop=True), r=[onesb_b, phab], w=[pnb], inc=True)
            op("dve", lambda: nc.vector.tensor_scalar(out=inv_[:], in0=pn[:], scalar1=1e-6, scalar2=None, op0=ALU.add), r=[pnb], w=[inv_b])
            op("dve", lambda: nc.vector.reciprocal(out=inv_[:], in_=inv_[:]), r=[inv_b], w=[inv_b])
            self.unpin(pn)
            yield

        for _ in prologue(0):
            yield
        PD = 3
        for f0 in range(min(PD, NF)):
            self.load(FTs[f0 % 4][0][:], FTs[f0 % 4][1], FT_d[f0], FT_src)
        for ct in range(4):
            HW_, HW_b = HWs[ct % 2]; inv_, inv_b = invs[ct % 2]
            nxt_pro = prologue(ct + 1) if ct < 3 else None
            for fc in range(NF):
                ft, ftb = FTs[fi % 4]; hs, hsb = hss[fi % 3]
                nxt = fc + PD
                if nxt < NF or ct < 3:
                    ft2, ft2b = FTs[(fi + PD) % 4]
                    self.load(ft2[:], ft2b, FT_d[nxt % NF], FT_src)
                fi += 1
                ps, psb = self.bank()
                for tc in range(nt):
                    op("pe", lambda: nc.tensor.matmul(ps[:], lhsT=ft[:, tc * 128:(tc + 1) * 128], rhs=HW_[:, tc, :], start=(tc == 0), stop=(tc == nt - 1)),
                       r=[ftb, HW_b], w=[psb], inc=(tc == nt - 1))
                op("dve", lambda: nc.vector.tensor_tensor(out=hs[:], in0=ps[:], in1=inv_[:], op=ALU.mult), r=[psb, inv_b], w=[hsb])
                self.store(Hs_d[fc * 128:(fc + 1) * 128, ct * 512:(ct + 1) * 512], Hs_b, hs[:], hsb)
                if nxt_pro is not None:
                    next(nxt_pro, None)
                yield
            if nxt_pro is not None:
                for _ in nxt_pro:
                    pass

    def seg_conv(sgi, pc):
        tok0, ntok, nseq, L, N = SEGS[sgi]
        nt, NF = L // 128, N // 128
        NFh = NF // 2
        FT_d, GT_d, Hs_d, Hs_b = g["FT_d"][sgi], g["GT_d"][sgi], g["Hs_d"][sgi], g["Hs_b"][sgi]
        FT_src, GT_src = Buf("FTsrc"), Buf("GTsrc")
        usts = [self.sb(pc, f"ust{i}", [128, L + 2], F32) for i in range(1 if sgi == 0 else 2)]
        for (u_, ub_) in usts:
            op("dve", lambda: nc.vector.memset(u_[:, 0:1], 0.0), w=[ub_])
            op("dve", lambda: nc.vector.memset(u_[:, L + 1:L + 2], 0.0), w=[ub_])
        zc, zc_b = self.sb(pc, "zc", [128, 4, L], BF16)
        xg, xg_b = self.sb(pc, "xg", [128, 4, L], BF16)
        ztok, ztok_b = self.sb(pc, "ztok", [128, nt, 512], BF16)
        Y, Y_b = self.sb(pc, "Y", [128, NF, 512], BF16)
        FTs = [self.sb(pc, f"FTc{i}", [128, nt * 128], BF16) for i in range(4)]
        GTs = [self.sb(pc, f"GTc{i}", [128, NF * 256], BF16) for i in range(2)]
        Hts = [self.sb(pc, f"Ht{i}", [128, 2, 512], BF16) for i in range(2)]
        tms = [self.sb(pc, f"tm{i}", [128, 512], F32) for i in range(4)]
        ctmp, ctmp_b = self.sb(pc, "ctmp", [128, L], F32)
        itmps = [self.sb(pc, f"itmp{i}", [128, 256], F32) for i in range(2)]
        rot = {"u": 0, "ft": 0, "gt": 0, "h": 0, "it": 0}

        def conv(uch, seq_t0, dst, dstb, j):
            u_, ub_ = usts[rot["u"] % len(usts)]; rot["u"] += 1
            self.load(u_[:, 1:L + 1], ub_, uT_d[uch * 128:(uch + 1) * 128, seq_t0:seq_t0 + L], uT_b)
            wc = lambda tap: bfm[:, CW + tap * 24 + uch: CW + tap * 24 + uch + 1]
            op("dve", lambda: nc.vector.tensor_scalar(out=ctmp[:], in0=u_[:, 0:L], scalar1=wc(0), scalar2=bfm[:, CB + uch:CB + uch + 1],
                                                      op0=ALU.mult, op1=ALU.add), r=[ub_, bfm_b], w=[ctmp_b])
            op("dve", lambda: nc.vector.scalar_tensor_tensor(out=ctmp[:], in0=u_[:, 1:L + 1], scalar=wc(1), in1=ctmp[:], op0=ALU.mult, op1=ALU.add),
               r=[ub_, bfm_b, ctmp_b], w=[ctmp_b])
            op("dve", lambda: nc.vector.scalar_tensor_tensor(out=dst[:, j, :], in0=u_[:, 2:L + 2], scalar=wc(2), in1=ctmp[:], op0=ALU.mult, op1=ALU.add),
               r=[ub_, bfm_b, ctmp_b], w=[dstb])

        for sq in range(nseq):
            st0 = tok0 + sq * L
            for half in range(2):
                for j in range(4):
                    conv(16 + half * 4 + j, st0, zc, zc_b, j)
                    yield
                for o in range(2):
                    for tc in range(nt):
                        pt, ptb = self.bank()
                        ptv = pt.bitcast(BF16)
                        for j in range(4):
                            op("pe", lambda: nc.tensor.transpose(ptv[:, j * 128:(j + 1) * 128], zc[:, j, tc * 128:(tc + 1) * 128], identb[:]),
                               r=[zc_b, identb_b], w=[ptb], inc=(j == 3))
                        op("act", lambda: nc.scalar.copy(out=ztok[:, tc, :], in_=ptv[:, 0:512]), r=[ptb], w=[ztok_b])
                        yield
                    gate_js = [0, 1, 2, 3]
                    for fp in range(NFh):
                        fre, freb = FTs[rot["ft"] % 4]; fim, fimb = FTs[(rot["ft"] + 1) % 4]; rot["ft"] += 2
                        ht, htb = Hts[rot["h"] % 2]; rot["h"] += 1
                        self.load(fre[:], freb, FT_d[fp], FT_src)
                        self.load(fim[:], fimb, FT_d[NFh + fp], FT_src)
                        c0 = o * 1024 + half * 512
                        self.load(ht[:, 0, :], htb, Hs_d[fp * 128:(fp + 1) * 128, c0:c0 + 512], Hs_b)
                        self.load(ht[:, 1, :], htb, Hs_d[(NFh + fp) * 128:(NFh + fp + 1) * 128, c0:c0 + 512], Hs_b)
                        pr_, prb = self.bank(); pi_, pib = self.bank()
                        for tc in range(nt):
                            op("pe", lambda: nc.tensor.matmul(pr_[:], lhsT=fre[:, tc * 128:(tc + 1) * 128], rhs=ztok[:, tc, :], start=(tc == 0), stop=(tc == nt - 1)),
                               r=[freb, ztok_b], w=[prb], inc=(tc == nt - 1))
                        for tc in range(nt):
                            op("pe", lambda: nc.tensor.matmul(pi_[:], lhsT=fim[:, tc * 128:(tc + 1) * 128], rhs=ztok[:, tc, :], start=(tc == 0), stop=(tc == nt - 1)),
                               r=[fimb, ztok_b], w=[pib], inc=(tc == nt - 1))
                        (t1, t1b), (t2, t2b), (t3, t3b), (t4, t4b) = tms
                        op("dve", lambda: nc.vector.tensor_tensor(out=t1[:], in0=pi_[:], in1=ht[:, 1, :], op=ALU.mult), r=[pib, htb], w=[t1b])
                        op("dve", lambda: nc.vector.tensor_tensor(out=t2[:], in0=pr_[:], in1=ht[:, 0, :], op=ALU.mult), r=[prb, htb], w=[t2b])
                        op("pool", lambda: nc.gpsimd.tensor_tensor(out=Y[:, fp, :], in0=t2[:], in1=t1[:], op=ALU.subtract), r=[t1b, t2b], w=[Y_b])
                        op("dve", lambda: nc.vector.tensor_tensor(out=t3[:], in0=pr_[:], in1=ht[:, 1, :], op=ALU.mult), r=[prb, htb], w=[t3b])
                        op("dve", lambda: nc.vector.tensor_tensor(out=t4[:], in0=pi_[:], in1=ht[:, 0, :], op=ALU.mult), r=[pib, htb], w=[t4b])
                        op("pool", lambda: nc.gpsimd.tensor_tensor(out=Y[:, NFh + fp, :], in0=t3[:], in1=t4[:], op=ALU.add), r=[t3b, t4b], w=[Y_b])
                        if gate_js and (NFh < 8 or fp % 3 == 1):
                            jg = gate_js.pop(0)
                            conv(o * 8 + half * 4 + jg, st0, xg, xg_b, jg)
                        yield
                    while gate_js:
                        jg = gate_js.pop(0)
                        conv(o * 8 + half * 4 + jg, st0, xg, xg_b, jg)
                        yield
                    dst, dstb = (zc, zc_b) if o == 0 else (xg, xg_b)
                    for tt in range(L // 256):
                        gt, gtb = GTs[rot["gt"] % 2]; rot["gt"] += 1
                        self.load(gt[:], gtb, GT_d[tt], GT_src)
                        tsl = slice(tt * 256, (tt + 1) * 256)
                        for j in range(4):
                            ps, psb = self.bank()
                            for fc in range(NF):
                                op("pe", lambda: nc.tensor.matmul(ps[:, 0:256], lhsT=Y[:, fc, j * 128:(j + 1) * 128], rhs=gt[:, fc * 256:(fc + 1) * 256],
                                                                  start=(fc == 0), stop=(fc == NF - 1)), r=[Y_b, gtb], w=[psb], inc=(fc == NF - 1))
                            it, itb = itmps[rot["it"] % 2]; rot["it"] += 1
                            skc = SKIP + o * 8 + half * 4 + j
                            op("dve", lambda: nc.vector.scalar_tensor_tensor(out=it[:], in0=zc[:, j, tsl], scalar=bfm[:, skc:skc + 1], in1=ps[:, 0:256],
                                                                             op0=ALU.mult, op1=ALU.add), r=[zc_b, bfm_b, psb], w=[itb])
                            op("pool", lambda: nc.gpsimd.tensor_tensor(out=dst[:, j, tsl], in0=it[:], in1=xg[:, j, tsl], op=ALU.mult),
                               r=[itb, xg_b] + ([zc_b] if o == 0 else []), w=[dstb])
                            yield
                self.store(yh_d[half * 512:(half + 1) * 512, st0:st0 + L].rearrange("(j p) t -> p j t", p=128), yh_b, xg[:], xg_b)

    if mode == "filter":
        with ExitStack() as pf:
            self.interleave([(seg_filter(0, pf), [0, 1, 2, 3], 8), (seg_filter(1, pf), [4, 5], 2), (g["phase0_gen"](pf), [6, 7], 1)])
            kb.barrier()
        return
    for _ in genP:
        pass
    kb.barrier()
    pmB.close()
    g["phTSC"].close()
    with ExitStack() as pc:
        self.interleave([(seg_conv(0, pc), [0, 1, 2, 3, 4, 5], 1), (seg_conv(1, pc), [6, 7], 1)])
        kb.barrier()


Prog.phaseC = _phaseC


def _phaseD(self, g):
    nc, kb = self.nc, self.kb
    op = kb.op
    bfm, bfm_b, Gb, Gb_b, B2G, B2G_b, FNW, FNW_b = g["bfm"], g["bfm_b"], g["Gb"], g["Gb_b"], g["B2G"], g["B2G_b"], g["FNW"], g["FNW_b"]
    hT_d, hT_b, yh_d, yh_b, sg_d, sg_b = g["hT_d"], g["hT_b"], g["yh_d"], g["yh_b"], g["sg_d"], g["sg_b"]
    x_all, pos, y_all = g["x_all"], g["pos"], g["y_all"]
    norm_to_fm, wslot_load = g["norm_to_fm"], g["wslot_load"]
    y_b = Buf("y_all")
    GT_ = 1024
    NTI = GT_ // 128
    with ExitStack() as pd:
        f1T, f1T_b = self.sb(pd, "f1T", [128, 32, GT_], BF16)
        hTg, yhTg = f1T[:, 0:8, :], f1T[:, 8:16, :]
        mh, mh_b = self.sb(pd, "mh", [128, 8, GT_], BF16)
        sgts = [self.sb(pd, f"sgt{i}", [128, 2, GT_], BF16) for i in range(2)]
        wsl = [self.sb(pd, f"wslD{i}", [128, 8, 512], BF16) for i in range(4)]
        xt, xt_b = self.sb(pd, "xtD", [128, NTI, D], F32)
        pt, ptb = self.sb(pd, "ptD", [128, D], F32)
        tAs = [self.sb(pd, f"tA{i}", [128, 512], F32) for i in range(3)]
        ss, ss_b = self.sb(pd, "ssD", [128, NTI], F32)
        rs, rs_b = self.sb(pd, "rsD", [128, NTI], F32)
        xn, xn_b = self.sb(pd, "xnD", [128, 4, D], BF16)
        ysts = [self.sb(pd, "yst0", [128, D], F32), (pt, ptb)]
        junk, junk_b = ysts[0]
        wi = [0]; ti = [0]

        def slot():
            s_ = wsl[wi[0] % 4]; wi[0] += 1
            return s_

        def tmp():
            s_ = tAs[ti[0] % 3]; ti[0] += 1
            return s_

        w2v = g["w_mlp2"].rearrange("(c p) n -> p c n", p=128)
        def load_hy(gi):
            tk = slice(gi * GT_, (gi + 1) * GT_)
            self.load(hTg, f1T_b, hT_d[:, tk].rearrange("(k p) t -> p k t", p=128), hT_b)
            self.load(yhTg, f1T_b, yh_d[:, tk].rearrange("(k p) t -> p k t", p=128), yh_b)

        load_hy(0)
        NG = TT // GT_

        def sec_merge(gi):
            r_ = 0 if gi * GT_ < TS_ else 1
            tk = slice(gi * GT_, (gi + 1) * GT_)
            for blk in range(2):
                wa, wab = slot(); wb_, wbb = slot()
                wslot_load(wa, wab, g["w_br_m"], 512, blk * 512)
                wslot_load(wb_, wbb, g["w_br_h"], 512, blk * 512)
                for m4 in range(4):
                    m = blk * 4 + m4
                    sgt, sgtb = sgts[m % 2]
                    self.load(sgt[:, 0, :], sgtb, sg_d[m * 128:(m + 1) * 128, tk], sg_b)
                    self.load(sgt[:, 1, :], sgtb, sg_d[D + m * 128:D + (m + 1) * 128, tk], sg_b)
                    for sub in range(GT_ // 512):
                        ts_ = slice(sub * 512, (sub + 1) * 512)
                        pA, pAb = self.bank(); pB, pBb = self.bank()
                        for k in range(8):
                            op("pe", lambda: nc.tensor.matmul(pA[:], lhsT=wa[:, k, m4 * 128:(m4 + 1) * 128], rhs=hTg[:, k, ts_], start=(k == 0), stop=(k == 7)),
                               r=[wab, f1T_b], w=[pAb], inc=(k == 7))
                        for k in range(8):
                            op("pe", lambda: nc.tensor.matmul(pB[:], lhsT=wb_[:, k, m4 * 128:(m4 + 1) * 128], rhs=yhTg[:, k, ts_], start=(k == 0), stop=(k == 7)),
                               r=[wbb, f1T_b], w=[pBb], inc=(k == 7))
                        (ta, tab), (tb, tbb) = tmp(), tmp()
                        op("dve", lambda: nc.vector.tensor_tensor(out=ta[:], in0=pA[:], in1=sgt[:, 0, ts_], op=ALU.mult), r=[pAb, sgtb], w=[tab])
                        op("dve", lambda: nc.vector.tensor_tensor(out=tb[:], in0=pB[:], in1=sgt[:, 1, ts_], op=ALU.mult), r=[pBb, sgtb], w=[tbb])
                        op("dve", lambda: nc.vector.tensor_tensor(out=mh[:, m, ts_], in0=ta[:], in1=tb[:], op=ALU.add), r=[tab, tbb], w=[mh_b])

        def sec_xload(gi):
            r_ = 0 if gi * GT_ < TS_ else 1
            tk = slice(gi * GT_, (gi + 1) * GT_)
            self.load(xt[:], xt_b, x_all[tk, :].rearrange("(i p) d -> p i d", p=128))
            if r_ == 0:
                for i in range(NTI):
                    self.load(pt[:], ptb, pos[gi * GT_ + i * 128: gi * GT_ + (i + 1) * 128, :])
                    op("dve", lambda: nc.vector.tensor_tensor(out=xt[:, i, :], in0=xt[:, i, :], in1=pt[:], op=ALU.add), r=[xt_b, ptb], w=[xt_b])

        def sec_mid(gi):
            r_ = 0 if gi * GT_ < TS_ else 1
            tk = slice(gi * GT_, (gi + 1) * GT_)
            for hlf in range(2):
                wo, wob = slot()
                wslot_load(wo, wob, g["w_out"], 512, hlf * 512)
                cs = slice(hlf * 512, (hlf + 1) * 512)
                for i in range(NTI):
                    ps, psb = self.bank()
                    for k in range(8):
                        op("pe", lambda: nc.tensor.matmul(ps[:], lhsT=mh[:, k, i * 128:(i + 1) * 128], rhs=wo[:, k, :], start=(k == 0), stop=(k == 7)),
                           r=[mh_b, wob], w=[psb], inc=(k == 7))
                    ta, tab = tmp()
                    op("dve", lambda: nc.vector.tensor_tensor(out=ta[:], in0=ps[:], in1=Gb[:, r_, 0, cs], op=ALU.mult), r=[psb, Gb_b], w=[tab])
                    op("dve", lambda: nc.vector.tensor_tensor(out=xt[:, i, cs], in0=ta[:], in1=xt[:, i, cs], op=ALU.add), r=[tab, xt_b], w=[xt_b])
            for sub in range(GT_ // 512):
                norm_to_fm((junk, junk_b, ss[:, 0:4], ss_b, rs[:, 0:4], rs_b, xn, xn_b), xt[:, sub * 4:(sub + 1) * 4, :], xt_b, mh, mh_b, sub * 512, 2, r_)
            for i in range(NTI):
                op("dve", lambda: nc.vector.tensor_tensor(out=xt[:, i, :], in0=xt[:, i, :], in1=B2G[:, r_], op=ALU.add), r=[xt_b, B2G_b], w=[xt_b])
            for fb_ in range(8):
                w1t, w1b = slot()
                wslot_load(w1t, w1b, g["w_mlp1"], 512, fb_ * 512)
                for m4 in range(4):
                    fc = fb_ * 4 + m4
                    for sub in range(GT_ // 512):
                        ts_ = slice(sub * 512, (sub + 1) * 512)
                        ps, psb = self.bank()
                        for k in range(8):
                            op("pe", lambda: nc.tensor.matmul(ps[:], lhsT=w1t[:, k, m4 * 128:(m4 + 1) * 128], rhs=mh[:, k, ts_], start=(k == 0), stop=(k == 7)),
                               r=[w1b, mh_b], w=[psb], inc=(k == 7))
                        ta, tab = tmp()
                        op("act", lambda: nc.scalar.activation(out=ta[:], in_=ps[:], func=AF.Relu, bias=bfm[:, BM1 + fc:BM1 + fc + 1]), r=[psb, bfm_b], w=[tab])
                        op("dve", lambda: nc.vector.tensor_tensor(out=f1T[:, fc, ts_], in0=ta[:], in1=ta[:], op=ALU.mult), r=[tab], w=[f1T_b])
            for hlf in range(2):
                cs = slice(hlf * 512, (hlf + 1) * 512)
                bks = [self.bank() for _ in range(NTI)]
                for fbk in range(4):
                    w2t, w2b = slot()
                    self.load(w2t[:], w2b, w2v[:, fbk * 8:(fbk + 1) * 8, cs], q="pool")
                    for i in range(NTI):
                        ps, psb = bks[i]
                        for q in range(8):
                            first = (fbk == 0 and q == 0); last = (fbk == 3 and q == 7)
                            op("pe", lambda: nc.tensor.matmul(ps[:], lhsT=f1T[:, fbk * 8 + q, i * 128:(i + 1) * 128], rhs=w2t[:, q, :], start=first, stop=last),
                               r=[f1T_b, w2b], w=[psb], inc=(q == 7))
                for i in range(NTI):
                    ps, psb = bks[i]
                    ta, tab = tmp()
                    op("dve", lambda: nc.vector.tensor_tensor(out=ta[:], in0=ps[:], in1=Gb[:, r_, 1, cs], op=ALU.mult), r=[psb, Gb_b], w=[tab])
                    op("dve", lambda: nc.vector.tensor_tensor(out=xt[:, i, cs], in0=ta[:], in1=xt[:, i, cs], op=ALU.add), r=[tab, xt_b], w=[xt_b])

        def sec_final(gi):
            r_ = 0 if gi * GT_ < TS_ else 1
            for i in range(NTI):
                op("act", lambda: nc.scalar.activation(out=junk[:], in_=xt[:, i, :], func=AF.Square, accum_out=ss[:, i:i + 1]), r=[xt_b], w=[junk_b, ss_b])
            op("dve", lambda: nc.vector.tensor_scalar(out=rs[:], in0=ss[:], scalar1=1.0 / D, scalar2=EPS, op0=ALU.mult, op1=ALU.add), r=[ss_b], w=[rs_b])
            op("act", lambda: nc.scalar.activation(out=rs[:], in_=rs[:], func=AF.Sqrt), r=[rs_b], w=[rs_b])
            op("dve", lambda: nc.vector.reciprocal(out=rs[:], in_=rs[:]), r=[rs_b], w=[rs_b])
            for i in range(NTI):
                ys, ysb = ysts[i % 2]
                op("act", lambda: nc.scalar.activation(out=ys[:], in_=xt[:, i, :], func=AF.Copy, scale=rs[:, i:i + 1]), r=[xt_b, rs_b], w=[ysb])
                op("dve", lambda: nc.vector.tensor_tensor(out=ys[:], in0=ys[:], in1=FNW[:], op=ALU.mult), r=[ysb, FNW_b], w=[ysb])
                self.store(y_all[gi * GT_ + i * 128: gi * GT_ + (i + 1) * 128, :], y_b, ys[:], ysb)

        sec_xload(0)
        sec_merge(0)
        for gi in range(NG):
            sec_mid(gi)
            if gi + 1 < NG:
                load_hy(gi + 1)
                sec_merge(gi + 1)
            sec_final(gi)
            if gi + 1 < NG:
                sec_xload(gi + 1)
        kb.barrier()


Prog.phaseD = _phaseD
```

```python
import math
from contextlib import ExitStack
import numpy as np
import ml_dtypes
import concourse.bass as bass
import concourse.mybir as mybir
from concourse.bass_utils import run_bass_kernel_spmd

F32 = mybir.dt.float32
BF16 = mybir.dt.bfloat16
AF = mybir.ActivationFunctionType
ALU = mybir.AluOpType

D = 1024
NH = 4
HD = 256
TS_ = 2048
TP_ = 1024
TT = TS_ + TP_
LP = 256
NSEQP = 4
FF = 4096
EPS = 1e-6
NCORES = 8
DEBUG = False


class Buf:
    __slots__ = ("name", "w", "rd", "dsem", "excl")

    def __init__(self, name, excl=False):
        self.name = name
        self.excl = excl
        self.w = {}
        self.rd = {}
        self.dsem = None


class KB:
    def __init__(self, nc, es):
        self.nc = nc
        self.es = es
        self.eng = {"pe": nc.tensor, "act": nc.scalar, "dve": nc.vector, "pool": nc.gpsimd, "sp": nc.sync}
        self.sem = {e: es.enter_context(nc.semaphore("sem_" + e)) for e in ("pe", "act", "dve", "pool")}
        self.cnt = {e: 0 for e in self.sem}
        self.known = {e: {} for e in self.eng}
        self.dsems = {}
        self.nsem = 4
        self.nins = 0
        self.nkey = 0
        self.free = {}
        self.dbufs = []

    def buf(self, name):
        return Buf(name)

    def _need(self, e, reads, writes):
        need = {}
        is_dma = e.startswith("dma:")

        def add(ev, key, kind):
            sem, val, eng = ev
            if is_dma and kind == "waw" and eng.startswith("dma:"):
                return
            if eng == e:
                if e == "pe" and kind != "raw":
                    return
                if is_dma:
                    if kind == "waw":
                        return
                elif val > self.cnt.get(e, 0):
                    return
            if key not in need or need[key][1] < val:
                need[key] = (sem, val)

        for b in reads:
            for key, ev in b.w.items():
                add(ev, key, "raw")
            if b.excl:
                for key, ev in b.rd.items():
                    if ev[2] != e:
                        add(ev, key, "rar")
        for b in writes:
            for key, ev in b.w.items():
                add(ev, key, "waw")
            for key, ev in b.rd.items():
                add(ev, key, "war")
        return need

    def _wait(self, e, need):
        eng = self.eng[e]
        kn = self.known[e]
        for key, (sem, val) in need.items():
            if kn.get(key, 0) >= val:
                continue
            eng.wait_ge(sem, val)
            kn[key] = val

    def op(self, e, fn, r=(), w=(), inc=True):
        need = self._need(e, r, w)
        self._wait(e, need)
        ins = fn()
        self.nins += 1
        if inc:
            self.cnt[e] += 1
            ins.then_inc(self.sem[e], 1)
            ev = (self.sem[e], self.cnt[e], e)
        else:
            ev = (self.sem[e], self.cnt[e] + 1, e)
        for b in w:
            b.w = {e: ev}
            b.rd = {}
        for b in r:
            b.rd[e] = ev
        return ins

    def _dsem(self, b, q="sp"):
        q = "sw" if q == "pool" else "hw"
        if b.dsem is None:
            self.nkey += 1
            fl = self.free.setdefault(q, [])
            if fl:
                s, tot = fl.pop()
            else:
                s = self.es.enter_context(self.nc.semaphore("d%d_%s" % (self.nkey, b.name)))
                tot = 0
                self.nsem += 1
            b.dsem = [s, tot, "d_%s_%d" % (b.name, self.nkey), q]
            self.dbufs.append(b)
        return b.dsem

    def mark(self):
        return len(self.dbufs)

    def release(self, mark):
        for b in self.dbufs[mark:]:
            if b.dsem is not None:
                self.free.setdefault(b.dsem[3], []).append((b.dsem[0], b.dsem[1]))
                self._all.pop(b.dsem[2], None)
                b.dsem = None
        del self.dbufs[mark:]

    def dma(self, q, out, in_, r=(), w=(), owner=None, **kw):
        ds = self._dsem(owner, q)
        assert ds[3] == ("sw" if q == "pool" else "hw"), (owner.name, q)
        key = ds[2]
        need = self._need("dma:" + key, r, w)
        self._wait(q, need)
        ds[1] += 16
        kw.setdefault('allow_slow_non_contiguous', True)
        self.eng[q].dma_start(out=out, in_=in_, **kw).then_inc(ds[0], 16)
        self.nins += 1
        ev = (ds[0], ds[1], "dma:" + key)
        for b in w:
            neww = {k: v for k, v in b.w.items() if k.startswith("d_")}
            neww[key] = ev
            b.w = neww
            b.rd = {}
        for b in r:
            b.rd[key] = ev

    def barrier(self):
        evs = {}
        for e in self.sem:
            if self.cnt[e] > 0:
                evs[e] = (self.sem[e], self.cnt[e])
        for key, ds in self.dsems_all().items():
            evs[key] = (ds[0], ds[1])
        for e in self.eng:
            self._wait(e, {k: v for k, v in evs.items() if k != e})
        self.release(0)

    def dsems_all(self):
        return self._all

    _all = None


def _bf(a):
    return np.ascontiguousarray(a.astype(np.float32)).astype(ml_dtypes.bfloat16)


def _dft_mats(L, N):
    k = np.arange(N // 2, dtype=np.float64) + 0.5
    t = np.arange(L, dtype=np.float64)
    th = 2 * np.pi * np.outer(t, k) / N
    FT = np.concatenate([np.cos(th), -np.sin(th)], axis=1)
    th2 = 2 * np.pi * np.outer(k, t + L // 2) / N
    GT = np.concatenate([np.cos(th2), -np.sin(th2)], axis=0) * (2.0 / N)
    return _bf(FT), _bf(GT)


def _pos_table():
    rows = TS_ // 64
    quarter = D // 4
    omega = 1.0 / (10000.0 ** (np.arange(quarter, dtype=np.float32) / quarter))

    def ax(pos):
        a = pos[:, None].astype(np.float32) * omega[None, :]
        return np.concatenate([np.sin(a), np.cos(a)], axis=-1)

    er = ax(np.arange(rows, dtype=np.float32))
    ec = ax(np.arange(64, dtype=np.float32))
    half = D // 2
    pos = np.concatenate([np.broadcast_to(er[:, None, :], (rows, 64, half)),
                          np.broadcast_to(ec[None, :, :], (rows, 64, half))], axis=-1)
    return np.ascontiguousarray(pos.reshape(rows * 64, D).astype(np.float32))


def _filt_feats(L):
    t_idx = np.arange(L, dtype=np.float32)
    t = np.linspace(0.0, 1.0, L, dtype=np.float32)
    bands = np.arange(1, 17, dtype=np.float32)
    ang = (np.float32(2.0 * math.pi / L) * t_idx[:, None] * bands[None, :]).astype(np.float32)
    z = np.concatenate([t[:, None], np.cos(ang), np.sin(ang)], axis=-1).astype(np.float32)
    centre = L // 2
    dist = (np.abs(t_idx - centre) / centre).astype(np.float32)
    return np.ascontiguousarray(z.T), dist


BQ, BK, BU, BGM, BGH, BM1, N1W, N2W, BADA, CW, CB, SKIP = 0, 8, 16, 40, 48, 56, 88, 96, 104, 152, 224, 248
NBF = 264
CQ, CK, CV, CO, CG, CU, CGM, CGH = 0, 1024, 2048, 3072, 4096, 4112, 7184, 8208
NCH = TT // 128
SEGS = [(0, TS_, 1, TS_, 3072), (TS_, TP_, NSEQP, LP, 512)]


class Prog:
    def __init__(self, stop_after=None):
        self.stop_after = stop_after
        self.nc = nc = bass.Bass("TRN2", target_bir_lowering=False)
        self.es = ExitStack()
        self.kb = KB(nc, self.es)
        self.kb._all = {}
        self.din = {}
        self.dout = {}
        self._uid = 0

    def inp(self, name, shape, dt=F32):
        t = self.nc.dram_tensor(name, list(shape), dt, kind="ExternalInput")
        self.din[name] = t
        return t

    def outp(self, name, shape, dt=F32):
        t = self.nc.dram_tensor(name, list(shape), dt, kind="ExternalOutput")
        self.dout[name] = t
        return t

    def scratch(self, name, shape, dt):
        kind = "ExternalOutput" if DEBUG else "Internal"
        t = self.nc.dram_tensor(name, list(shape), dt, kind=kind)
        if DEBUG:
            self.dout[name] = t
        return t

    def sb(self, es, name, shape, dt):
        self._uid += 1
        t = es.enter_context(self.nc.sbuf_tensor(f"{name}_{self._uid}", list(shape), dt))
        return t, Buf(name)

    def dma(self, q, out, in_, r=(), w=(), owner=None, **kw):
        kb = self.kb
        ds_before = owner.dsem
        kb.dma(q, out, in_, r=r, w=w, owner=owner, **kw)
        kb._all[owner.dsem[2]] = owner.dsem

    def load(self, tile_ap, buf, src_ap, srcbuf=None, q="sp", **kw):
        self.dma(q, tile_ap, src_ap, r=([srcbuf] if srcbuf else []), w=[buf], owner=buf, **kw)

    def store(self, dst_ap, dstbuf, tile_ap, buf, q="sp", **kw):
        self.dma(q, dst_ap, tile_ap, r=[buf], w=([dstbuf] if dstbuf else []), owner=buf, **kw)

    def bank(self, pin=False):
        pool = self.cur_pool
        key = tuple(pool)
        k = self._bkp.get(key, 0)
        n = len(pool)
        for _ in range(n):
            i = pool[k % n]
            if i not in self._pinned:
                break
            k += 1
        self._bkp[key] = (k + 1) % n
        if pin:
            self._pinned.add(i)
        return self.banks[i], self.bankb[i]

    def interleave(self, items):
        active = list(items)
        while active:
            for it in list(active):
                gen, pool, stride = it
                self.cur_pool = pool
                for _ in range(stride):
                    try:
                        next(gen)
                    except StopIteration:
                        active.remove(it)
                        break
        self.cur_pool = list(range(8))

    def unpin(self, t):
        self._pinned.discard(self.banks.index(t))

    def build(self):
        nc, kb, es = self.nc, self.kb, self.es
        op = kb.op
        x_all = self.inp("x_all", [TT, D]).ap()
        pos = self.inp("pos", [TS_, D]).ap()
        C0 = self.inp("C0", [2, NH, HD, HD]).ap()
        n0 = self.inp("n0", [2, NH, HD]).ap()
        m0 = self.inp("m0", [2, NH]).ap()
        cvT = self.inp("cvT", [128, 8, 2]).ap()
        w_ada = self.inp("w_ada", [D, 6 * D]).ap()
        b_ada_row = self.inp("b_ada_row", [6 * D]).ap()
        w_in = self.inp("w_in", [D, 9232]).ap()
        w_gate = self.inp("w_gate", [D, 72]).ap()
        b_gate = self.inp("b_gate", [36, 2]).ap()
        bfm_d = self.inp("bfm", [128, NBF]).ap()
        brow = self.inp("brow", [3 * D]).ap()
        mnw = self.inp("mnw", [D]).ap()
        b2row = self.inp("b2row", [D]).ap()
        fnw = self.inp("fnw", [D]).ap()
        w_br_m = self.inp("w_br_m", [D, D]).ap()
        w_br_h = self.inp("w_br_h", [D, D]).ap()
        w_out = self.inp("w_out", [D, D]).ap()
        w_mlp1 = self.inp("w_mlp1", [D, FF]).ap()
        w_mlp2 = self.inp("w_mlp2", [FF, D]).ap()
        fw1 = self.inp("fw1", [33, 64]).ap()
        fw2 = self.inp("fw2", [64, 64]).ap()
        fw3 = self.inp("fw3", [64, 2048]).ap()
        fbv = self.inp("fbv", [64, 4]).ap()
        identf_d = self.inp("identf", [128, 128]).ap()
        identb_d = self.inp("identb", [128, 128], BF16).ap()
        masks_d = self.inp("masks", [128, 2, 128], BF16).ap()
        sel_d = self.inp("sel", [2, 2, 128]).ap()
        cmask_d = self.inp("cmask", [TT]).ap()
        deltas_d = self.inp("deltas", [D]).ap()
        zf_d = [self.inp("zfS", [33, TS_]).ap(), self.inp("zfP", [33, LP]).ap()]
        nd_d = [self.inp("ndS", [128, TS_ // 128]).ap(), self.inp("ndP", [128, LP // 128]).ap()]
        FT_d = [self.inp("FTS", [24, 128, 16 * 128], BF16).ap(), self.inp("FTP", [4, 128, 2 * 128], BF16).ap()]
        GT_d = [self.inp("GTS", [8, 128, 24 * 256], BF16).ap(), self.inp("GTP", [1, 128, 4 * 256], BF16).ap()]

        y_all = self.outp("y_all", [TT, D]).ap()
        nC = self.outp("nC", [NSEQP, 2, NH, HD, HD]).ap()
        nn = self.outp("nn", [NSEQP, 2, NH, HD]).ap()
        nm = self.outp("nm", [NSEQP, 2, NH]).ap()

        qT_d = self.scratch("s_qT", [D, TT], BF16).ap(); qT_b = Buf("s_qT")
        kT_d = self.scratch("s_kT", [D, TT], BF16).ap(); kT_b = Buf("s_kT")
        kv_d = self.scratch("s_kvo", [3, TT, D], BF16).ap(); kv_b = Buf("s_kvo")
        uT_d = self.scratch("s_uT", [3 * D, TT], F32).ap(); uT_b = Buf("s_uT")
        sg_d = self.scratch("s_sg", [2 * D, TT], BF16).ap(); sg_b = Buf("s_sg")
        hT_d = self.scratch("s_hT", [D, TT], BF16).ap(); hT_b = Buf("s_hT")
        yh_d = self.scratch("s_yhT", [D, TT], BF16).ap(); yh_b = Buf("s_yhT")
        Hs_d = [self.scratch("s_HsS", [3072, 2048], BF16).ap(), self.scratch("s_HsP", [512, 2048], BF16).ap()]
        Hs_b = [Buf("s_HsS"), Buf("s_HsP")]
        if DEBUG:
            dbg_hn = self.scratch("s_hn", [D, TT], BF16).ap(); dbg_hn_b = Buf("s_hn")
            dbg_row = self.scratch("s_rows", [8, 36, TT], F32).ap(); dbg_row_b = Buf("s_rows")

        self.banks, self.bankb, self._bk = [], [], 0
        self._pinned = set()
        self._bkp = {}
        self.cur_pool = list(range(8))
        for i in range(8):
            t = es.enter_context(nc.psum_tensor(f"bank{i}", [128, 512], F32))
            self.banks.append(t); self.bankb.append(Buf(f"bank{i}", excl=True))
        identf, identf_b = self.sb(es, "identf", [128, 128], F32)
        identb, identb_b = self.sb(es, "identb", [128, 128], BF16)
        masks, masks_b = self.sb(es, "masks", [128, 2, 128], BF16)
        bfm, bfm_b = self.sb(es, "bfm", [128, NBF], F32)
        onesf, onesf_b = self.sb(es, "onesf", [128, 128], F32)
        sel, sel_b = self.sb(es, "sel", [2, 2, 128], F32)
        Gb, Gb_b = self.sb(es, "Gb", [128, 2, 2, D], F32)
        B2G, B2G_b = self.sb(es, "B2G", [128, 2, D], F32)
        FNW, FNW_b = self.sb(es, "FNW", [128, D], F32)
        AB, AB_b = self.sb(es, "AB", [128, 4, 8, 2], F32)
        self.load(identf[:], identf_b, identf_d)
        self.load(identb[:], identb_b, identb_d)
        self.load(masks[:], masks_b, masks_d)
        self.load(bfm[:], bfm_b, bfm_d)
        self.load(sel[:], sel_b, sel_d)
        self.load(FNW[:], FNW_b, fnw.partition_broadcast(128))
        op("dve", lambda: nc.vector.memset(onesf[:], 1.0), w=[onesf_b])

        def wslot_load(tile, buf, src, cols, c0, kch=8):
            v = src.rearrange("(k p) n -> p k n", p=128)[:, :, c0:c0 + cols]
            self.load(tile[:, 0:kch, 0:cols], buf, v, q="pool")

        def phase0_gen(ph):
            cv, cv_b = self.sb(ph, "cv", [128, 8, 2], F32)
            scv, scv_b = self.sb(ph, "scv", [128, 8, 2], BF16)
            modfm, modfm_b = self.sb(ph, "modfm", [128, 48, 2], F32)
            grow, grow_b = self.sb(ph, "grow", [2, 2 * D], F32)
            brow2, brow2_b = self.sb(ph, "brow2", [2, 2 * D], F32)
            b2b, b2b_b = self.sb(ph, "b2b", [128, D], F32)
            wsl = [self.sb(ph, f"wada{i}", [128, 8, 512], BF16) for i in range(2)]
            self.load(cv[:], cv_b, cvT)
            self.load(brow2[:, 0:D], brow2_b, b_ada_row[2 * D:3 * D].partition_broadcast(2))
            self.load(brow2[:, D:2 * D], brow2_b, b_ada_row[5 * D:6 * D].partition_broadcast(2))
            self.load(b2b[:], b2b_b, b2row.partition_broadcast(128))
            op("act", lambda: nc.scalar.activation(out=scv[:], in_=cv[:], func=AF.Silu), r=[cv_b], w=[scv_b])
            pm, pm_b = self.bank(pin=True)
            yield
            for blk in range(12):
                wt, wb = wsl[blk % 2]
                wslot_load(wt, wb, w_ada, 512, blk * 512)
                for oc in range(4):
                    g_ = blk * 4 + oc
                    for k in range(8):
                        op("pe", lambda: nc.tensor.matmul(pm[:, 2 * g_:2 * g_ + 2], lhsT=wt[:, k, oc * 128:(oc + 1) * 128],
                                                          rhs=scv[:, k, :], start=(k == 0), stop=(k == 7)),
                           r=[wb, scv_b], w=[pm_b], inc=(k == 7))
                if blk in (4, 5, 10, 11):
                    gi = 0 if blk < 6 else 1
                    ct = blk % 2
                    pr, pr_b = self.bank()
                    for k in range(8):
                        op("pe", lambda: nc.tensor.matmul(pr[0:2, :], lhsT=scv[:, k, :], rhs=wt[:, k, 0:512],
                                                          start=(k == 0), stop=(k == 7)),
                           r=[wb, scv_b], w=[pr_b], inc=(k == 7))
                    cs_ = slice(gi * D + ct * 512, gi * D + (ct + 1) * 512)
                    op("dve", lambda: nc.vector.tensor_tensor(out=grow[:, cs_], in0=pr[0:2, :], in1=brow2[:, cs_], op=ALU.add),
                       r=[pr_b, brow2_b], w=[grow_b])
                yield
            op("dve", lambda: nc.vector.tensor_tensor(out=modfm[:], in0=pm[:, 0:96].rearrange("p (c r) -> p c r", r=2),
                                                      in1=bfm[:, BADA:BADA + 48].unsqueeze(2).to_broadcast([128, 48, 2]), op=ALU.add),
               r=[pm_b, bfm_b], w=[modfm_b])
            self.unpin(pm)
            for (ai, sc0, sh0, nw0) in ((0, 8, 0, N1W), (2, 32, 24, N2W)):
                op("dve", lambda: nc.vector.scalar_tensor_tensor(out=AB[:, ai], in0=modfm[:, sc0:sc0 + 8, :], scalar=1.0,
                                                                 in1=bfm[:, nw0:nw0 + 8].unsqueeze(2).to_broadcast([128, 8, 2]),
                                                                 op0=ALU.add, op1=ALU.mult),
                   r=[modfm_b, bfm_b], w=[AB_b])
                op("dve", lambda: nc.vector.tensor_copy(out=AB[:, ai + 1], in_=modfm[:, sh0:sh0 + 8, :]), r=[modfm_b], w=[AB_b])
            yield
            for r_ in range(2):
                for gi in range(2):
                    for ct in range(2):
                        pr, pr_b = self.bank()
                        op("pe", lambda: nc.tensor.matmul(pr[:], lhsT=sel[:, r_, :], rhs=grow[:, gi * D + ct * 512: gi * D + (ct + 1) * 512],
                                                          start=True, stop=True), r=[sel_b, grow_b], w=[pr_b])
                        op("act", lambda: nc.scalar.copy(out=Gb[:, r_, gi, ct * 512:(ct + 1) * 512], in_=pr[:]), r=[pr_b], w=[Gb_b])
                        yield
                op("dve", lambda: nc.vector.tensor_tensor(out=B2G[:, r_], in0=Gb[:, r_, 1], in1=b2b[:], op=ALU.mult),
                   r=[Gb_b, b2b_b], w=[B2G_b])
            yield

        self.phaseC(locals(), None, None, "filter")
        phTSC = es.enter_context(ExitStack())
        TSC, TSC_b = self.sb(phTSC, "TSC", [128, NCH, 5, 8], F32)
        if self.stop_after == 0:
            return self.finish()

        def norm_to_fm(ph_tiles, xt, xt_b, dstT, dstT_b, col0, ai, r_):
            (junk, junk_b, ss, ss_b, rs, rs_b, xn, xn_b) = ph_tiles
            for i in range(4):
                op("act", lambda: nc.scalar.activation(out=junk[:], in_=xt[:, i, :], func=AF.Square, accum_out=ss[:, i:i + 1]),
                   r=[xt_b], w=[junk_b, ss_b])
            op("dve", lambda: nc.vector.tensor_scalar(out=rs[:], in0=ss[:], scalar1=1.0 / D, scalar2=EPS, op0=ALU.mult, op1=ALU.add),
               r=[ss_b], w=[rs_b])
            op("act", lambda: nc.scalar.activation(out=rs[:], in_=rs[:], func=AF.Sqrt), r=[rs_b], w=[rs_b])
            op("dve", lambda: nc.vector.reciprocal(out=rs[:], in_=rs[:]), r=[rs_b], w=[rs_b])
            for i in range(4):
                op("act", lambda: nc.scalar.activation(out=xn[:, i, :], in_=xt[:, i, :], func=AF.Copy, scale=rs[:, i:i + 1]),
                   r=[xt_b, rs_b], w=[xn_b])
            for k in range(8):
                pt, pt_b = self.bank()
                ptb = pt.bitcast(BF16)
                for i in range(4):
                    op("pe", lambda: nc.tensor.transpose(ptb[:, i * 128:(i + 1) * 128], xn[:, i, k * 128:(k + 1) * 128], identb[:]),
                       r=[xn_b, identb_b], w=[pt_b], inc=(i == 3))
                op("dve", lambda: nc.vector.tensor_scalar(out=dstT[:, k, col0:col0 + 512], in0=ptb[:, 0:512],
                                                          scalar1=AB[:, ai, k, r_:r_ + 1], scalar2=AB[:, ai + 1, k, r_:r_ + 1],
                                                          op0=ALU.mult, op1=ALU.add),
                   r=[pt_b, AB_b], w=[dstT_b])

        def load_x(xt, xt_b, pt, pt_b, g):
            self.load(xt[:], xt_b, x_all[g * 512:(g + 1) * 512, :].rearrange("(i p) d -> p i d", p=128))
            if g < 4:
                self.load(pt[:], pt_b, pos[g * 512:(g + 1) * 512, :].rearrange("(i p) d -> p i d", p=128))
                op("dve", lambda: nc.vector.tensor_tensor(out=xt[:], in0=xt[:], in1=pt[:], op=ALU.add), r=[xt_b, pt_b], w=[xt_b])

        phLF = es.enter_context(ExitStack())
        LI, LI_b = self.sb(phLF, "LI", [36, TT], F32)
        FR, FR_b = self.sb(phLF, "FR", [36, TT], F32)
        with ExitStack() as ph:
            hnT, hnT_b = self.sb(ph, "hnT", [128, 8, TT], BF16)
            with ExitStack() as ph1:
                xts = [self.sb(ph1, f"xt{i}", [128, 4, D], F32) for i in range(3)]
                pts = [self.sb(ph1, f"pt{i}", [128, 4, D], F32) for i in range(2)]
                junk, junk_b = self.sb(ph1, "junk", [128, D], BF16)
                ss, ss_b = self.sb(ph1, "ss", [128, 4], F32)
                rs, rs_b = self.sb(ph1, "rs", [128, 4], F32)
                xns = [self.sb(ph1, f"xn{i}", [128, 4, D], BF16) for i in range(2)]
                sss = [self.sb(ph1, f"ssA{i}", [128, 4], F32) for i in range(2)]
                rss = [self.sb(ph1, f"rsA{i}", [128, 4], F32) for i in range(2)]
                load_x(*xts[0], *pts[0], 0)
                load_x(*xts[1], *pts[1], 1)
                for g in range(6):
                    if g + 2 < 6:
                        load_x(*xts[(g + 2) % 3], *pts[(g + 2) % 2], g + 2)
                    xt, xt_b = xts[g % 3]
                    norm_to_fm((junk, junk_b, *sss[g % 2], *rss[g % 2], *xns[g % 2]), xt, xt_b, hnT, hnT_b, g * 512, 0, 0 if g < 4 else 1)
                kb.barrier()
            if DEBUG:
                self.store(dbg_hn.rearrange("(k p) t -> p k t", p=128), dbg_hn_b, hnT[:], hnT_b)
            if self.stop_after == 1:
                kb.barrier()
                return self.finish()
            with ExitStack() as ph2:
                wsl = [self.sb(ph2, f"wsl{i}", [128, 8, 512], BF16) for i in range(3)]
                wg, wg_b = self.sb(ph2, "wg", [128, 8, 72], BF16)
                bg, bg_b = self.sb(ph2, "bg", [36, 2], F32)
                browb, browb_b = self.sb(ph2, "browb", [128, 3 * D], F32)
                stF = [self.sb(ph2, f"stF{i}", [128, TT], F32) for i in range(2)]
                stB = [self.sb(ph2, f"stB{i}", [128, TT], BF16) for i in range(2)]
                stT = [self.sb(ph2, f"stT{i}", [128, 12, 512], BF16) for i in range(2)]
                tmpT, tmpT_b = self.sb(ph2, "tmpT", [128, 512], F32)
                self.load(bg[:], bg_b, b_gate)
                self.load(browb[:], browb_b, brow.partition_broadcast(128))
                wslot_load(wg, wg_b, w_gate, 72, 0)
                for tt in range(6):
                    for gi, dst, dst_b in ((0, LI, LI_b), (1, FR, FR_b)):
                        pg, pg_b = self.bank()
                        for k in range(8):
                            op("pe", lambda: nc.tensor.matmul(pg[0:36, :], lhsT=wg[:, k, gi * 36:(gi + 1) * 36], rhs=hnT[:, k, tt * 512:(tt + 1) * 512],
                                                              start=(k == 0), stop=(k == 7)), r=[wg_b, hnT_b], w=[pg_b], inc=(k == 7))
                        op("act", lambda: nc.scalar.activation(out=dst[:, tt * 512:(tt + 1) * 512], in_=pg[0:36, :], func=AF.Identity,
                                                               bias=bg[:, gi:gi + 1]), r=[pg_b, bg_b], w=[dst_b])
                si = [0, 0, 0, 0]
                fm_jobs = [(CQ, 2, BQ, qT_d, qT_b, 0, "q"), (CK, 2, BK, kT_d, kT_b, 0, "k"), (CU, 6, BU, uT_d, uT_b, 0, "u"),
                           (CGM, 2, BGM, sg_d, sg_b, 0, "g"), (CGH, 2, BGH, sg_d, sg_b, D, "g")]
                for (c0, nb, bc, dst, dst_b, row0, kind) in fm_jobs:
                    for blk in range(nb):
                        wt, wb = wsl[si[0] % 3]; si[0] += 1
                        wslot_load(wt, wb, w_in, 512, c0 + blk * 512)
                        for m in range(4):
                            ch = blk * 4 + m
                            if kind == "u":
                                st, st_b = stF[si[1] % 2]; si[1] += 1
                            else:
                                st, st_b = stB[si[2] % 2]; si[2] += 1
                            for tt in range(6):
                                pp, pp_b = self.bank()
                                for k in range(8):
                                    op("pe", lambda: nc.tensor.matmul(pp[:], lhsT=wt[:, k, m * 128:(m + 1) * 128], rhs=hnT[:, k, tt * 512:(tt + 1) * 512],
                                                                      start=(k == 0), stop=(k == 7)), r=[wb, hnT_b], w=[pp_b], inc=(k == 7))
                                o_ = st[:, tt * 512:(tt + 1) * 512]
                                b_ = bfm[:, bc + ch:bc + ch + 1]
                                if kind == "q":
                                    op("dve", lambda: nc.vector.tensor_scalar(out=o_, in0=pp[:], scalar1=b_, scalar2=1.0 / 16.0, op0=ALU.add, op1=ALU.mult),
                                       r=[pp_b, bfm_b], w=[st_b])
                                elif kind == "g":
                                    op("act", lambda: nc.scalar.activation(out=o_, in_=pp[:], func=AF.Sigmoid, bias=b_), r=[pp_b, bfm_b], w=[st_b])
                                else:
                                    eng = "act" if tt % 2 == 0 else "dve"
                                    if eng == "act":
                                        op("act", lambda: nc.scalar.activation(out=o_, in_=pp[:], func=AF.Identity, bias=b_), r=[pp_b, bfm_b], w=[st_b])
                                    else:
                                        op("dve", lambda: nc.vector.tensor_scalar(out=o_, in0=pp[:], scalar1=b_, scalar2=None, op0=ALU.add),
                                           r=[pp_b, bfm_b], w=[st_b])
                            self.store(dst[row0 + ch * 128: row0 + (ch + 1) * 128, :], dst_b, st[:], st_b)
                for j, c0 in ((1, CV), (2, CO)):
                    for blk in range(2):
                        wt, wb = wsl[si[0] % 3]; si[0] += 1
                        wslot_load(wt, wb, w_in, 512, c0 + blk * 512)
                        for hlf in range(2):
                            st, st_b = stT[si[3] % 2]; si[3] += 1
                            for ti in range(12):
                                tk = hlf * 12 + ti
                                pp, pp_b = self.bank()
                                for k in range(8):
                                    op("pe", lambda: nc.tensor.matmul(pp[:], lhsT=hnT[:, k, tk * 128:(tk + 1) * 128], rhs=wt[:, k, :],
                                                                      start=(k == 0), stop=(k == 7)), r=[wb, hnT_b], w=[pp_b], inc=(k == 7))
                                bb_ = browb[:, j * D + blk * 512: j * D + (blk + 1) * 512]
                                if j == 1:
                                    op("dve", lambda: nc.vector.tensor_tensor(out=st[:, ti, :], in0=pp[:], in1=bb_, op=ALU.add),
                                       r=[pp_b, browb_b], w=[st_b])
                                else:
                                    op("dve", lambda: nc.vector.tensor_tensor(out=tmpT[:], in0=pp[:], in1=bb_, op=ALU.add),
                                       r=[pp_b, browb_b], w=[tmpT_b])
                                    op("act", lambda: nc.scalar.activation(out=st[:, ti, :], in_=tmpT[:], func=AF.Sigmoid), r=[tmpT_b], w=[st_b])
                            self.store(kv_d[j, hlf * 1536:(hlf + 1) * 1536, blk * 512:(blk + 1) * 512].rearrange("(n p) c -> p n c", p=128),
                                       kv_b, st[:], st_b)
                kb.barrier()
            if DEBUG:
                self.store(dbg_row[0], dbg_row_b, LI[:], LI_b)
                self.store(dbg_row[1], dbg_row_b, FR[:], FR_b)
            if self.stop_after == 2:
                kb.barrier()
                return self.finish()
        pmB, genP = self.phaseB(None, LI, LI_b, FR, FR_b, locals())
        if self.stop_after in (3, 4):
            for _ in genP:
                pass
            kb.barrier()
            pmB.close()
            return self.finish()
        self.phaseC(locals(), genP, pmB, "conv")
        phTSC.close()
        if self.stop_after == 5:
            return self.finish()
        self.phaseD(locals())
        return self.finish()

    def finish(self):
        kb = self.kb
        kb.barrier()
        return self.nc


_CONST_CACHE = {}


def _consts():
    if _CONST_CACHE:
        return _CONST_CACHE
    c = {}
    c["pos"] = _pos_table()
    c["identf"] = np.eye(128, dtype=np.float32)
    c["identb"] = _bf(np.eye(128))
    s = np.arange(128)[:, None]
    j = np.arange(128)[None, :]
    c["masks"] = _bf(np.stack([(s <= j), (s >= j)], axis=1).astype(np.float32))
    sel = np.zeros((2, 2, 128), np.float32)
    sel[0, 0, :] = 1.0
    sel[1, 1, :] = 1.0
    c["sel"] = sel
    cm = np.ones(TT, np.float32)
    cm[::128] = 0.0
    c["cmask"] = cm
    c["deltas"] = np.abs(np.linspace(math.log(1e-2) / 1.5, math.log(1e-2) / 0.3, D, dtype=np.float32)).astype(np.float32)
    for nm_, L, N, ni in (("S", TS_, 3072, 256), ("P", LP, 512, 256)):
        zf, dist = _filt_feats(L)
        c["zf" + nm_] = zf
        c["nd" + nm_] = np.ascontiguousarray((-dist).reshape(L // 128, 128).T)
        FT, GT = _dft_mats(L, N)
        nt, nf = L // 128, N // 128
        c["FT" + nm_] = np.ascontiguousarray(FT.reshape(nt, 128, nf, 128).transpose(2, 1, 0, 3).reshape(nf, 128, nt * 128))
        ntile = L // ni
        c["GT" + nm_] = np.ascontiguousarray(GT.reshape(nf, 128, ntile, ni).transpose(2, 1, 0, 3).reshape(ntile, 128, nf * ni))
    _CONST_CACHE.update(c)
    return c


def _fm(v):
    return np.ascontiguousarray(np.asarray(v, np.float32).reshape(-1, 128).T)


def _host_maps(inp):
    c = _consts()
    f = lambda a: np.ascontiguousarray(np.asarray(a, dtype=np.float32))
    w_in = f(inp["w_in"][0]); b_in = f(inp["b_in"][0])
    wg = np.zeros((D, 72), np.float32); bgt = np.zeros((36, 2), np.float32)
    g0 = CG
    wg[:, 0:4] = w_in[:, g0:g0 + 4]; wg[:, 32:36] = w_in[:, g0 + 8:g0 + 12]
    wg[:, 36:40] = w_in[:, g0 + 4:g0 + 8]; wg[:, 68:72] = w_in[:, g0 + 12:g0 + 16]
    bgt[0:4, 0] = b_in[g0:g0 + 4]; bgt[32:36, 0] = b_in[g0 + 8:g0 + 12]
    bgt[0:4, 1] = b_in[g0 + 4:g0 + 8]; bgt[32:36, 1] = b_in[g0 + 12:g0 + 16]
    bfm = np.zeros((128, NBF), np.float32)
    bfm[:, BQ:BQ + 8] = _fm(b_in[CQ:CQ + D]); bfm[:, BK:BK + 8] = _fm(b_in[CK:CK + D])
    bfm[:, BU:BU + 24] = _fm(b_in[CU:CU + 3 * D])
    bfm[:, BGM:BGM + 8] = _fm(b_in[CGM:CGM + D]); bfm[:, BGH:BGH + 8] = _fm(b_in[CGH:CGH + D])
    bfm[:, BM1:BM1 + 32] = _fm(inp["b_mlp1"][0])
    bfm[:, N1W:N1W + 8] = _fm(inp["norm1_w"][0]); bfm[:, N2W:N2W + 8] = _fm(inp["norm2_w"][0])
    bfm[:, BADA:BADA + 48] = _fm(inp["b_ada"][0])
    cw = f(inp["hy_conv_w"][0])
    for j_ in range(3):
        bfm[:, CW + j_ * 24:CW + (j_ + 1) * 24] = _fm(cw[j_])
    bfm[:, CB:CB + 24] = _fm(inp["hy_conv_b"][0])
    sk = f(inp["hy_skip"][0])
    bfm[:, SKIP:SKIP + 8] = _fm(sk[0]); bfm[:, SKIP + 8:SKIP + 16] = _fm(sk[1])
    fbv = np.stack([f(inp["filt_b1"][0]), f(inp["filt_freq1"][0]), f(inp["filt_b2"][0]), f(inp["filt_freq2"][0])], axis=1)
    shared = {
        "pos": c["pos"], "w_ada": f(inp["w_ada"][0]), "b_ada_row": f(inp["b_ada"][0]), "w_in": w_in, "w_gate": wg, "b_gate": bgt,
        "bfm": bfm, "brow": np.ascontiguousarray(np.concatenate([b_in[CK:CK + D], b_in[CV:CV + D], b_in[CO:CO + D]])),
        "mnw": f(inp["mlstm_norm_w"][0]), "b2row": f(inp["b_mlp2"][0]), "fnw": f(inp["final_norm_w"]),
        "w_br_m": f(inp["w_br_m"][0]), "w_br_h": f(inp["w_br_h"][0]), "w_out": f(inp["w_out"][0]),
        "w_mlp1": f(inp["w_mlp1"][0]), "w_mlp2": f(inp["w_mlp2"][0]),
        "fw1": f(inp["filt_w1"][0]), "fw2": f(inp["filt_w2"][0]), "fw3": f(inp["filt_w3"][0]), "fbv": np.ascontiguousarray(fbv),
        "identf": c["identf"], "identb": c["identb"], "masks": c["masks"], "sel": c["sel"], "cmask": c["cmask"],
        "deltas": c["deltas"], "zfS": c["zfS"], "zfP": c["zfP"], "ndS": c["ndS"], "ndP": c["ndP"],
        "FTS": c["FTS"], "FTP": c["FTP"], "GTS": c["GTS"], "GTP": c["GTP"],
    }
    xp = f(inp["x_prompt"]); xs = f(inp["x_sample"])
    maps = []
    for i in range(NCORES):
        m = dict(shared)
        m["x_all"] = np.ascontiguousarray(np.concatenate([xs[i], xp[4 * i:4 * i + 4].reshape(TP_, D)], axis=0))
        m["C0"] = f(inp["state_mlstm_C"][i, 0]); m["n0"] = f(inp["state_mlstm_n"][i, 0]); m["m0"] = f(inp["state_mlstm_m"][i, 0])
        cv = np.stack([f(inp["c"][i]), f(inp["c_ctx"])], axis=0)
        m["cvT"] = np.ascontiguousarray(cv.reshape(2, 8, 128).transpose(2, 1, 0))
        maps.append(m)
    return maps


_PROG = {}


def _get_prog(stop_after=None):
    key = (stop_after, DEBUG)
    if key not in _PROG:
        p = Prog(stop_after)
        p.build()
        p.es.close()
        print("program built: instructions", p.kb.nins, "semaphores", p.kb.nsem, flush=True)
        _PROG[key] = p
    return _PROG[key]


def run(inp, stop_after=None, core_ids=None, trace=False):
    p = _get_prog(stop_after)
    maps = _host_maps(inp)
    used = set(p.din.keys())
    maps = [{k: v for k, v in m.items() if k in used} for m in maps]
    cids = list(range(NCORES)) if core_ids is None else core_ids
    res = run_bass_kernel_spmd(p.nc, [maps[i] for i in cids], core_ids=list(range(len(cids))), trace=trace)
    return res


def kernel(**inp):
    res = run(inp).results
    y_p = np.zeros((32, LP, D), np.float32); y_s = np.zeros((NCORES, TS_, D), np.float32)
    nC = np.zeros((32, 1, 2, NH, HD, HD), np.float32); nn = np.zeros((32, 1, 2, NH, HD), np.float32)
    nm = np.zeros((32, 1, 2, NH), np.float32)
    for i in range(NCORES):
        r = res[i]
        y_s[i] = r["y_all"][:TS_]
        y_p[4 * i:4 * i + 4] = r["y_all"][TS_:].reshape(4, LP, D)
        nC[4 * i:4 * i + 4, 0] = r["nC"]; nn[4 * i:4 * i + 4, 0] = r["nn"]; nm[4 * i:4 * i + 4, 0] = r["nm"]
    return (y_p, y_s, nC, nn, nm)


def _phaseB(self, ph, LI, LI_b, FR, FR_b, L_):
    nc, kb = self.nc, self.kb
    op = kb.op
    g = L_
    identf, identf_b, identb, identb_b, masks, masks_b = g["identf"], g["identf_b"], g["identb"], g["identb_b"], g["masks"], g["masks_b"]
    TSC, TSC_b = g["TSC"], g["TSC_b"]
    V3 = lambda t: t[:].rearrange("p (c j) -> p c j", j=128)
    with ExitStack() as pr:
        R = [self.sb(pr, f"R{i}", [36, TT], F32) for i in range(8)]
        cm, cm_b = self.sb(pr, "cm", [36, TT], F32)
        sm = {n: self.sb(pr, n, [36, NCH], F32) for n in ("tot", "a", "MS", "ML", "DEC", "tmp")}
        MF, MF_b = self.sb(pr, "MF", [36, NSEQP], F32)
        m0t, m0t_b = self.sb(pr, "m0t", [36, 1], F32)
        self.load(cm[:], cm_b, g["cmask_d"].partition_broadcast(36))
        op("dve", lambda: nc.vector.memset(m0t[:], 0.0), w=[m0t_b])
        self.load(m0t[0:4, :], m0t_b, g["m0"][0:1, :].rearrange("o h -> h o"))
        self.load(m0t[32:36, :], m0t_b, g["m0"][1:2, :].rearrange("o h -> h o"))
        (R0, R0b), (R1, R1b), (R2, R2b), (R3, R3b), (R4, R4b), (R5, R5b), (R6, R6b), (R7, R7b) = R
        tot, tot_b = sm["tot"]; a_, a_b = sm["a"]; MS, MS_b = sm["MS"]; ML, ML_b = sm["ML"]; DEC, DEC_b = sm["DEC"]; tmp, tmp_b = sm["tmp"]
        bc = lambda t: t[:].unsqueeze(2).to_broadcast([36, NCH, 128])
        op("act", lambda: nc.scalar.activation(out=R0[:], in_=FR[:], func=AF.Exp, scale=-1.0), r=[FR_b], w=[R0b])
        op("act", lambda: nc.scalar.activation(out=R0[:], in_=R0[:], func=AF.Ln, bias=1.0), r=[R0b], w=[R0b])
        op("dve", lambda: nc.vector.tensor_scalar(out=R0[:], in0=R0[:], scalar1=-1.0, scalar2=None, op0=ALU.mult), r=[R0b], w=[R0b])
        op("dve", lambda: nc.vector.tensor_tensor_scan(out=R1[:], data0=cm[:], data1=R0[:], initial=0.0, op0=ALU.mult, op1=ALU.add),
           r=[cm_b, R0b], w=[R1b])
        op("dve", lambda: nc.vector.tensor_copy(out=tot[:], in_=V3(R1)[:, :, 127]), r=[R1b], w=[tot_b])
        op("dve", lambda: nc.vector.tensor_tensor(out=R2[:], in0=R0[:], in1=R1[:], op=ALU.subtract), r=[R0b, R1b], w=[R2b])
        op("dve", lambda: nc.vector.tensor_tensor(out=V3(R2), in0=V3(R2), in1=bc(tot), op=ALU.add), r=[R2b, tot_b], w=[R2b])
        op("dve", lambda: nc.vector.tensor_copy(out=R3[0:32, :], in_=R1[0:32, :]), r=[R1b], w=[R3b])
        op("dve", lambda: nc.vector.tensor_copy(out=R3[32:36, :], in_=R2[32:36, :]), r=[R2b], w=[R3b])
        op("dve", lambda: nc.vector.tensor_tensor(out=R4[:], in0=LI[:], in1=R3[:], op=ALU.subtract), r=[LI_b, R3b], w=[R4b])
        op("dve", lambda: nc.vector.tensor_copy(out=R5[:], in_=R4[:]), r=[R4b], w=[R5b])
        src, srcb, dst, dstb = R5, R5b, R6, R6b
        for d_ in (1, 2, 4, 8, 16, 32, 64):
            op("dve", lambda: nc.vector.tensor_copy(out=dst[:], in_=src[:]), r=[srcb], w=[dstb])
            op("dve", lambda: nc.vector.tensor_tensor(out=V3(dst)[0:32, :, d_:], in0=V3(src)[0:32, :, d_:], in1=V3(src)[0:32, :, :128 - d_], op=ALU.max),
               r=[srcb], w=[dstb])
            op("dve", lambda: nc.vector.tensor_tensor(out=V3(dst)[32:36, :, :128 - d_], in0=V3(src)[32:36, :, :128 - d_], in1=V3(src)[32:36, :, d_:], op=ALU.max),
               r=[srcb], w=[dstb])
            src, srcb, dst, dstb = dst, dstb, src, srcb
        PX, PXb = src, srcb
        FREE, FREEb = dst, dstb
        op("dve", lambda: nc.vector.tensor_copy(out=a_[0:32, :], in_=V3(PX)[0:32, :, 127]), r=[PXb], w=[a_b])
        op("dve", lambda: nc.vector.tensor_copy(out=a_[32:36, :], in_=V3(PX)[32:36, :, 0]), r=[PXb], w=[a_b])
        op("dve", lambda: nc.vector.memset(MS[:], 0.0), w=[MS_b])
        op("dve", lambda: nc.vector.memset(MF[:], 0.0), w=[MF_b])

        def step(rows, dst_ap, dstbuf, src_c):
            p0, p1 = rows
            op("dve", lambda: nc.vector.tensor_tensor(out=tmp[p0:p1, src_c], in0=MS[p0:p1, src_c], in1=a_[p0:p1, src_c], op=ALU.max),
               r=[MS_b, a_b], w=[tmp_b])
            op("dve", lambda: nc.vector.tensor_tensor(out=dst_ap, in0=tmp[p0:p1, src_c], in1=tot[p0:p1, src_c], op=ALU.add),
               r=[tmp_b, tot_b], w=[dstbuf])

        nS = TS_ // 128
        op("dve", lambda: nc.vector.tensor_copy(out=MS[0:4, 0:1], in_=m0t[0:4, :]), r=[m0t_b], w=[MS_b])
        op("dve", lambda: nc.vector.tensor_copy(out=MS[32:36, nS - 1:nS], in_=m0t[32:36, :]), r=[m0t_b], w=[MS_b])
        for c in range(nS - 1):
            step((0, 4), MS[0:4, c + 1:c + 2], MS_b, slice(c, c + 1))
            cb_ = nS - 1 - c
            step((32, 36), MS[32:36, cb_ - 1:cb_], MS_b, slice(cb_, cb_ + 1))
        ev, od = slice(nS, NCH, 2), slice(nS + 1, NCH, 2)
        step((0, 4), MS[0:4, od], MS_b, ev)
        step((0, 4), MF[0:4, :], MF_b, od)
        step((32, 36), MS[32:36, ev], MS_b, od)
        step((32, 36), MF[32:36, :], MF_b, ev)
        op("dve", lambda: nc.vector.tensor_tensor(out=ML[:], in0=MS[:], in1=a_[:], op=ALU.max), r=[MS_b, a_b], w=[ML_b])
        op("dve", lambda: nc.vector.tensor_tensor(out=DEC[:], in0=MS[:], in1=ML[:], op=ALU.subtract), r=[MS_b, ML_b], w=[DEC_b])
        op("act", lambda: nc.scalar.activation(out=DEC[:], in_=DEC[:], func=AF.Exp), r=[DEC_b], w=[DEC_b])
        op("dve", lambda: nc.vector.tensor_tensor(out=V3(R7), in0=V3(PX), in1=bc(MS), op=ALU.max), r=[PXb, MS_b], w=[R7b])
        op("dve", lambda: nc.vector.tensor_tensor(out=V3(R1), in0=bc(ML), in1=V3(R7), op=ALU.subtract), r=[ML_b, R7b], w=[R1b])
        op("act", lambda: nc.scalar.activation(out=R1[:], in_=R1[:], func=AF.Exp), r=[R1b], w=[R1b])
        op("dve", lambda: nc.vector.tensor_tensor(out=V3(R2), in0=bc(MS), in1=V3(R7), op=ALU.subtract), r=[MS_b, R7b], w=[R2b])
        op("act", lambda: nc.scalar.activation(out=R2[:], in_=R2[:], func=AF.Exp), r=[R2b], w=[R2b])
        op("dve", lambda: nc.vector.tensor_tensor(out=FREE[:], in0=R3[:], in1=R7[:], op=ALU.add), r=[R3b, R7b], w=[FREEb])
        op("act", lambda: nc.scalar.activation(out=FREE[:], in_=FREE[:], func=AF.Exp, scale=-1.0), r=[FREEb], w=[FREEb])
        op("dve", lambda: nc.vector.tensor_tensor(out=V3(R0), in0=V3(R4), in1=bc(ML), op=ALU.subtract), r=[R4b, ML_b], w=[R0b])
        op("act", lambda: nc.scalar.activation(out=R0[:], in_=R0[:], func=AF.Exp), r=[R0b], w=[R0b])
        op("dve", lambda: nc.vector.tensor_copy(out=V3(R3), in_=bc(DEC)), r=[DEC_b, FREEb], w=[R3b])
        quants = [(R0, R0b), (R1, R1b), (R2, R2b), (FREE, FREEb), (R3, R3b)]
        if DEBUG:
            for qi, (t_, tb_) in enumerate(quants + [(R7, R7b)]):
                self.store(g["dbg_row"][2 + qi], g["dbg_row_b"], t_[:], tb_)
        for c in range(NCH):
            pt, pt_b = self.bank()
            for qi, (t_, tb_) in enumerate(quants):
                op("pe", lambda: nc.tensor.transpose(pt[:, qi * 36:(qi + 1) * 36], t_[0:36, c * 128:(c + 1) * 128], identf[0:36, 0:36]),
                   r=[tb_, identf_b], w=[pt_b], inc=(qi == 4))
            pv = pt[:, 0:180].rearrange("p (q r) -> p q r", r=36)
            op("act", lambda: nc.scalar.copy(out=TSC[:, c, :, 0:4], in_=pv[:, :, 0:4]), r=[pt_b], w=[TSC_b])
            op("dve", lambda: nc.vector.tensor_copy(out=TSC[:, c, :, 4:8], in_=pv[:, :, 32:36]), r=[pt_b], w=[TSC_b])
        nm_b = Buf("nm")
        self.store(g["nm"][:, 0, :].rearrange("s h -> h s"), nm_b, MF[0:4, :], MF_b, allow_slow_non_contiguous=True)
        self.store(g["nm"][:, 1, :].rearrange("s h -> h s"), nm_b, MF[32:36, :], MF_b, allow_slow_non_contiguous=True)
        kb.barrier()
    g["phLF"].close()
    if self.stop_after == 3:
        kb.barrier()
        return

    WSq, Uq, WIq, ENq, DECq = 0, 1, 2, 3, 4
    pm_ = ExitStack()
    pmi = ExitStack()
    if True:
        nS = TS_ // 128
        NJ = 4
        small = []
        for i in range(NJ):
            s_ = {"qT": self.sb(pm_, f"qTs{i}", [128, 2, LP], BF16), "kT": self.sb(pm_, f"kTs{i}", [128, 2, LP], BF16),
                  "vt": self.sb(pm_, f"vts{i}", [128, LP // 128, 257], BF16), "ktok": self.sb(pm_, f"ktoks{i}", [128, LP // 128, 256], BF16)}
            small.append(s_)
        ogP = [small[i]["ktok"] for i in range(NJ)]
        hsP = [self.sb(pm_, f"hsP{i}", [128, 2, LP], BF16) for i in range(NJ)]
        Cf = [[self.sb(pm_, f"Cf{j}_{d}", [128, 2, 257], F32) for d in range(2)] for j in range(NJ)]
        Cb = [[self.sb(pm_, f"Cb{j}_{d}", [128, 2, 257], BF16) for d in range(2)] for j in range(NJ)]
        HaccP = [self.sb(pm_, f"HaccP{i}", [128, LP // 128, 256], F32) for i in range(NJ)]
        NR = 7
        ncol, ncol_b = self.sb(pm_, "ncol", [128, 16], F32)
        nrow, nrow_b = self.sb(pm_, "nrow", [16, 128], F32)
        STs = [self.sb(pm_, f"ST{i}", [128, 128], BF16) for i in range(NR)]
        kws = [self.sb(pm_, f"kw{i}", [128, 256], BF16) for i in range(NR)]
        scs = [self.sb(pm_, f"sc{i}", [128, 4], F32) for i in range(NR)]
        mnwb, mnwb_b = self.sb(pm_, "mnwb", [128, D], F32)
        ssq, ssq_b = self.sb(pm_, "ssq", [128, nS], F32)
        junk, junk_b = self.sb(pm_, "junkB", [128, 256], BF16)
        og2s = [self.sb(pm_, f"og2{i}", [128, 256], F32) for i in range(2)]
        hfs = [self.sb(pm_, f"hf{i}", [128, 256], BF16) for i in range(2)]
        big = []
        for i in range(2):
            s_ = {"qT": self.sb(pmi, f"qT{i}", [128, 2, TS_], BF16), "kT": self.sb(pmi, f"kT{i}", [128, 2, TS_], BF16),
                  "vt": self.sb(pmi, f"vt{i}", [128, nS, 257], BF16), "ktok": self.sb(pmi, f"ktok{i}", [128, nS, 256], BF16)}
            big.append(s_)
        HaccT = [self.sb(pmi, f"Hacc{i}", [128, nS, 256], F32) for i in range(2)]
        for s_ in big + small:
            vt_, vtb_ = s_["vt"]
            op("dve", lambda: nc.vector.memset(vt_[:, :, 256:257], 1.0), w=[vtb_])
        self.load(mnwb[:], mnwb_b, g["mnw"].partition_broadcast(128))
        qT_d, kT_d, kv_d, hT_d = g["qT_d"], g["kT_d"], g["kv_d"], g["hT_d"]
        qT_b, kT_b, kv_b, hT_b = g["qT_b"], g["kT_b"], g["kv_b"], g["hT_b"]
        nC_b, nn_b = Buf("nC"), Buf("nn")
        seqs = [(0, TS_)] + [(TS_ + i * LP, LP) for i in range(NSEQP)]
        rot = [0]

        def run_groups(groups, bidx, bsz):
            XB = [(self.banks[bidx[2 * p]], self.bankb[bidx[2 * p]]) for p in range(bsz)]
            ZB = [(self.banks[bidx[2 * p + 1]], self.bankb[bidx[2 * p + 1]]) for p in range(bsz)]
            def issue_loads(grp):
                for js, (si, h) in enumerate(grp):
                    t0, L = seqs[si]
                    nck = L // 128
                    s_ = big[js] if si == 0 else small[js]
                    r0 = h * 256
                    self.load(s_["qT"][0][:, :, 0:L], s_["qT"][1], qT_d[r0:r0 + 256, t0:t0 + L].rearrange("(c p) t -> p c t", p=128), qT_b)
                    self.load(s_["kT"][0][:, :, 0:L], s_["kT"][1], kT_d[r0:r0 + 256, t0:t0 + L].rearrange("(c p) t -> p c t", p=128), kT_b)
                    self.load(s_["vt"][0][:, 0:nck, 0:256], s_["vt"][1], kv_d[1, t0:t0 + L, r0:r0 + 256].rearrange("(n p) c -> p n c", p=128), kv_b)

            issue_loads(groups[0])
            for gidx, grp in enumerate(groups):
                ctx = []
                for js, (si, h) in enumerate(grp):
                    t0, L = seqs[si]
                    nck = L // 128
                    s_ = big[js] if si == 0 else small[js]
                    r0 = h * 256
                    for d in range(2):
                        cf, cfb = Cf[js][d]
                        if si == 0:
                            self.load(cf[:, :, 0:256], cfb, g["C0"][d, h].rearrange("(c p) v -> p c v", p=128))
                            self.load(cf[:, :, 256:257], cfb, g["n0"][d, h].rearrange("(c p o) -> p c o", p=128, o=1))
                        else:
                            op("dve", lambda: nc.vector.memset(cf[:], 0.0), w=[cfb])
                    hs, hsb = s_["kT"] if si == 0 else hsP[js]
                    hacc, haccb = HaccT[js] if si == 0 else HaccP[js]
                    og, ogb = s_["ktok"]
                    kT, kTb = s_["kT"]; ktok, ktokb = s_["ktok"]
                    for c in range(nck):
                        pk, pkb = self.bank()
                        pkv = pk.bitcast(BF16)
                        for dk in range(2):
                            op("pe", lambda: nc.tensor.transpose(pkv[:, dk * 128:(dk + 1) * 128], kT[:, dk, c * 128:(c + 1) * 128], identb[:]),
                               r=[kTb, identb_b], w=[pkb], inc=(dk == 1))
                        op("act", lambda: nc.scalar.copy(out=ktok[:, c, :], in_=pkv[:, 0:256]), r=[pkb], w=[ktokb])
                    ctx.append(dict(si=si, h=h, t0=t0, L=L, nck=nck, s=s_, js=js, hacc=hacc, haccb=haccb, hs=hs, hsb=hsb, og=og, ogb=ogb, touched=set()))
                    yield
                nck = ctx[0]["nck"]
                pairs = [(c_, d) for c_ in ctx for d in range(2)]
                batches = [pairs[b:b + bsz] for b in range(0, len(pairs), bsz)]
                for i in range(nck):
                    for batch in batches:
                        st = []
                        for p, (c_, d) in enumerate(batch):
                            js, h, s_ = c_["js"], c_["h"], c_["s"]
                            c = i if d == 0 else nck - 1 - i
                            cg = c_["t0"] // 128 + c
                            col = d * 4 + h
                            ri = rot[0] % NR; rot[0] += 1
                            e = dict(c=c, cg=cg, col=col, tk=slice(c * 128, (c + 1) * 128), X=XB[p], Z=ZB[p], ST=STs[ri], kw=kws[ri], sc=scs[ri],
                                     cf=Cf[js][d], cb=Cb[js][d], s=s_, c_=c_, d=d)
                            st.append(e)
                            qT, qTb = s_["qT"]; kT, kTb = s_["kT"]
                            X, Xb = e["X"]
                            for dk in range(2):
                                op("pe", lambda: nc.tensor.matmul(X[:, 0:128], lhsT=kT[:, dk, e["tk"]], rhs=qT[:, dk, e["tk"]], start=(dk == 0), stop=(dk == 1)),
                                   r=[kTb, qTb], w=[Xb], inc=(dk == 1))
                        yield
                        for e in st:
                            S = lambda q: TSC[:, e["cg"], q, e["col"]:e["col"] + 1]
                            X, Xb = e["X"]; ST, STb = e["ST"]; kw, kwb = e["kw"]; cf, cfb = e["cf"]; cbt, cbb = e["cb"]
                            ktok, ktokb = e["s"]["ktok"]
                            op("dve", lambda: nc.vector.scalar_tensor_tensor(out=ST[:], in0=X[:, 0:128], scalar=S(WSq), in1=masks[:, e["d"], :],
                                                                             op0=ALU.mult, op1=ALU.mult), r=[Xb, TSC_b, masks_b], w=[STb])
                            op("act", lambda: nc.scalar.activation(out=cbt[:], in_=cf[:], func=AF.Copy, scale=S(DECq)), r=[cfb, TSC_b], w=[cbb])
                            op("act", lambda: nc.scalar.activation(out=kw[:], in_=ktok[:, e["c"], :], func=AF.Copy, scale=S(WSq)), r=[ktokb, TSC_b], w=[kwb])
                        yield
                        for e in st:
                            X, Xb = e["X"]; Z, Zb = e["Z"]; ST, STb = e["ST"]; kw, kwb = e["kw"]; cbt, cbb = e["cb"]
                            qT, qTb = e["s"]["qT"]; vt, vtb = e["s"]["vt"]
                            c = e["c"]
                            op("pe", lambda: nc.tensor.matmul(X[:, 128:385], lhsT=ST[:], rhs=vt[:, c, :], start=True, stop=False), r=[STb, vtb], w=[Xb], inc=False)
                            for dk in range(2):
                                op("pe", lambda: nc.tensor.matmul(X[:, 128:385], lhsT=qT[:, dk, e["tk"]], rhs=cbt[:, dk, :], start=False, stop=(dk == 1)),
                                   r=[qTb, cbb], w=[Xb], inc=False)
                            for dk in range(2):
                                op("pe", lambda: nc.tensor.matmul(X[:, 386 + 2 * dk:388 + 2 * dk], lhsT=kw[:, dk * 128:(dk + 1) * 128], rhs=vt[:, c, 255:257], start=True, stop=True),
                                   r=[kwb, vtb], w=[Xb], inc=(dk == 1))
                            for dk in range(2):
                                op("pe", lambda: nc.tensor.matmul(Z[:, dk * 256:(dk + 1) * 256], lhsT=kw[:, dk * 128:(dk + 1) * 128], rhs=vt[:, c, 0:256], start=True, stop=True),
                                   r=[kwb, vtb], w=[Zb], inc=(dk == 1))
                        yield
                        Sx = lambda e, q: TSC[:, e["cg"], q, e["col"]:e["col"] + 1]
                        for e in st:
                            X, Xb = e["X"]; Z, Zb = e["Z"]; cf, cfb = e["cf"]
                            op("dve", lambda: nc.vector.scalar_tensor_tensor(out=cf[:, :, 0:256], in0=cf[:, :, 0:256], scalar=Sx(e, DECq),
                                                                             in1=Z[:, 0:512].rearrange("p (k v) -> p k v", v=256),
                                                                             op0=ALU.mult, op1=ALU.add), r=[Zb, TSC_b, cfb], w=[cfb])
                        for e in st:
                            X, Xb = e["X"]; cf, cfb = e["cf"]
                            op("dve", lambda: nc.vector.scalar_tensor_tensor(out=cf[:, :, 256:257], in0=cf[:, :, 256:257], scalar=Sx(e, DECq),
                                                                             in1=X[:, 386:390].rearrange("p (k v) -> p k v", v=2)[:, :, 1:2],
                                                                             op0=ALU.mult, op1=ALU.add), r=[Xb, TSC_b, cfb], w=[cfb])
                        for e in st:
                            X, Xb = e["X"]; sc, scb = e["sc"]
                            op("dve", lambda: nc.vector.tensor_scalar(out=sc[:, 0:1], in0=X[:, 384:385], scalar1=Sx(e, Uq), scalar2=None, op0=ALU.mult),
                               r=[Xb, TSC_b], w=[scb])
                        for e in st:
                            sc, scb = e["sc"]
                            op("dve", lambda: nc.vector.scalar_tensor_tensor(out=sc[:, 3:4], in0=sc[:, 0:1], scalar=-1.0, in1=sc[:, 0:1],
                                                                             op0=ALU.mult, op1=ALU.max), r=[scb], w=[scb])
                        for e in st:
                            sc, scb = e["sc"]
                            op("dve", lambda: nc.vector.tensor_scalar(out=sc[:, 1:2], in0=sc[:, 3:4], scalar1=Sx(e, ENq), scalar2=None, op0=ALU.max),
                               r=[scb, TSC_b], w=[scb])
                        for e in st:
                            sc, scb = e["sc"]
                            op("dve", lambda: nc.vector.reciprocal(out=sc[:, 2:3], in_=sc[:, 1:2]), r=[scb], w=[scb])
                        for e in st:
                            sc, scb = e["sc"]
                            op("dve", lambda: nc.vector.tensor_scalar(out=sc[:, 3:4], in0=sc[:, 2:3], scalar1=Sx(e, Uq), scalar2=None, op0=ALU.mult),
                               r=[scb, TSC_b], w=[scb])
                        for e in st:
                            X, Xb = e["X"]; sc, scb = e["sc"]
                            c_, c = e["c_"], e["c"]
                            hacc, haccb = c_["hacc"], c_["haccb"]
                            if c not in c_["touched"]:
                                c_["touched"].add(c)
                                op("act", lambda: nc.scalar.activation(out=hacc[:, c, :], in_=X[:, 128:384], func=AF.Copy, scale=sc[:, 3:4]),
                                   r=[Xb, scb], w=[haccb])
                            else:
                                op("dve", lambda: nc.vector.scalar_tensor_tensor(out=hacc[:, c, :], in0=X[:, 128:384], scalar=sc[:, 3:4], in1=hacc[:, c, :],
                                                                                 op0=ALU.mult, op1=ALU.add), r=[Xb, scb, haccb], w=[haccb])
                if gidx + 1 < len(groups):
                    if grp[0][0] > 0:
                        issue_loads(groups[gidx + 1])
                if ctx[0]["si"] > 0:
                    for c_ in ctx:
                        for d in range(2):
                            cf, cfb = Cf[c_["js"]][d]
                            col = d * 8 + c_["h"] * 2
                            op("act", lambda: nc.scalar.copy(out=ncol[:, col:col + 2], in_=cf[:, :, 256]), r=[cfb], w=[ncol_b])
                    ptn, ptnb = self.bank()
                    op("pe", lambda: nc.tensor.transpose(ptn[0:16, 0:128], ncol[:, 0:16], identf[:]), r=[ncol_b, identf_b], w=[ptnb])
                    op("act", lambda: nc.scalar.copy(out=nrow[:], in_=ptn[0:16, 0:128]), r=[ptnb], w=[nrow_b])
                    self.store(g["nn"][ctx[0]["si"] - 1].rearrange("d h (c p) -> (d h c) p", p=128), nn_b, nrow[:], nrow_b)
                for c_ in ctx:
                    self.load(c_["og"][:, 0:c_["nck"], :], c_["ogb"],
                              kv_d[2, c_["t0"]:c_["t0"] + c_["L"], c_["h"] * 256:(c_["h"] + 1) * 256].rearrange("(n p) c -> p n c", p=128), kv_b)
                for c_ in ctx:
                    si, h, js, t0, L, nck = c_["si"], c_["h"], c_["js"], c_["t0"], c_["L"], c_["nck"]
                    hacc, haccb, hs, hsb = c_["hacc"], c_["haccb"], c_["hs"], c_["hsb"]
                    og, ogb = c_["og"], c_["ogb"]
                    yield
                    if si > 0:
                        for d in range(2):
                            cf, cfb = Cf[js][d]
                            self.store(g["nC"][si - 1, d, h].rearrange("(c p) v -> p c v", p=128), nC_b, cf[:, :, 0:256], cfb)
                    for c in range(nck):
                        op("act", lambda: nc.scalar.activation(out=junk[:], in_=hacc[:, c, :], func=AF.Square, accum_out=ssq[:, c:c + 1]),
                           r=[haccb], w=[junk_b, ssq_b])
                    op("dve", lambda: nc.vector.tensor_scalar(out=ssq[:, 0:nck], in0=ssq[:, 0:nck], scalar1=1.0 / HD, scalar2=EPS, op0=ALU.mult, op1=ALU.add),
                       r=[ssq_b], w=[ssq_b])
                    op("act", lambda: nc.scalar.activation(out=ssq[:, 0:nck], in_=ssq[:, 0:nck], func=AF.Sqrt), r=[ssq_b], w=[ssq_b])
                    op("dve", lambda: nc.vector.reciprocal(out=ssq[:, 0:nck], in_=ssq[:, 0:nck]), r=[ssq_b], w=[ssq_b])
                    for c in range(nck):
                        o2, o2b = og2s[c % 2]; hf, hfb = hfs[c % 2]
                        op("dve", lambda: nc.vector.tensor_tensor(out=o2[:], in0=og[:, c, :], in1=mnwb[:, h * 256:(h + 1) * 256], op=ALU.mult),
                           r=[ogb, mnwb_b], w=[o2b])
                        op("dve", lambda: nc.vector.scalar_tensor_tensor(out=hf[:], in0=hacc[:, c, :], scalar=ssq[:, c:c + 1], in1=o2[:],
                                                                         op0=ALU.mult, op1=ALU.mult), r=[haccb, ssq_b, o2b], w=[hfb])
                        pt, ptb = self.bank()
                        ptv = pt.bitcast(BF16)
                        for dv in range(2):
                            op("pe", lambda: nc.tensor.transpose(ptv[:, dv * 128:(dv + 1) * 128], hf[:, dv * 128:(dv + 1) * 128], identb[:]),
                               r=[hfb, identb_b], w=[ptb], inc=(dv == 1))
                        op("act", lambda: nc.scalar.copy(out=hs[:, :, c * 128:(c + 1) * 128], in_=ptv[:, 0:256].rearrange("p (v t) -> p v t", t=128)),
                           r=[ptb], w=[hsb])
                    self.store(hT_d[h * 256:(h + 1) * 256, t0:t0 + L].rearrange("(v p) t -> p v t", p=128), hT_b, hs[:, :, 0:L], hsb)
                if gidx + 1 < len(groups) and grp[0][0] == 0:
                    issue_loads(groups[gidx + 1])

        for _ in run_groups([[(0, 0), (0, 1)], [(0, 2), (0, 3)]], list(range(8)), 4):
            pass
        kb.barrier()
        pmi.close()
        genP = run_groups([[(si, h) for h in range(NH)] for si in range(1, 1 + NSEQP)], list(range(8)), 4)
        return pm_, genP


Prog.phaseB = _phaseB


def _phaseC(self, g, genP, pmB, mode):
    nc, kb = self.nc, self.kb
    op = kb.op
    identb, identb_b, bfm, bfm_b, onesf, onesf_b = g["identb"], g["identb_b"], g["bfm"], g["bfm_b"], g["onesf"], g["onesf_b"]
    uT_d, uT_b, yh_d, yh_b = g["uT_d"], g["uT_b"], g["yh_d"], g["yh_b"]
    PI = math.pi
    def seg_filter(sgi, pf):
        tok0, ntok, nseq, L, N = SEGS[sgi]
        nt, NF = L // 128, N // 128
        NFh = NF // 2
        FT_d, GT_d, Hs_d, Hs_b = g["FT_d"][sgi], g["GT_d"][sgi], g["Hs_d"][sgi], g["Hs_b"][sgi]
        FT_src, GT_src = Buf("FTsrc"), Buf("GTsrc")
        zf, zf_b = self.sb(pf, "zf", [33, L], F32)
        w1, w1_b = self.sb(pf, "fw1", [33, 64], F32)
        w2, w2_b = self.sb(pf, "fw2", [64, 64], F32)
        w3, w3_b = self.sb(pf, "fw3", [64, 2048], BF16)
        fb, fb_b = self.sb(pf, "fbv", [64, 4], F32)
        h1T, h1T_b = self.sb(pf, "h1T", [64, L], F32)
        h2T, h2T_b = self.sb(pf, "h2T", [64, L], BF16)
        pre, pre_b = self.sb(pf, "pre", [64, 512], F32)
        w_a, w_ab = self.sb(pf, "wra", [64, 512], F32)
        w_b, w_bb = self.sb(pf, "wrb", [64, 512], F32)
        deltab, deltab_b = self.sb(pf, "deltab", [128, D], F32)
        nd, nd_b = self.sb(pf, "nd", [128, nt], F32)
        wint, wint_b = self.sb(pf, "wint", [128, 512], F32)
        hwfs = [self.sb(pf, f"hwf{i}", [128, 512], F32) for i in range(2)]
        habss = [self.sb(pf, f"habs{i}", [128, 512], BF16) for i in range(2)]
        HWb, HWb_b = self.sb(pf, "HWb", [128, nt, 512], BF16)
        invt, invt_b = self.sb(pf, "invt", [128, 512], F32)
        FTs = [self.sb(pf, f"FTf{i}", [128, nt * 128], BF16) for i in range(4)]
        hss = [self.sb(pf, f"hss{i}", [128, 512], BF16) for i in range(3)]
        self.load(zf[:], zf_b, g["zf_d"][sgi]); self.load(w1[:], w1_b, g["fw1"]); self.load(w2[:], w2_b, g["fw2"])
        self.load(w3[:], w3_b, g["fw3"], q="pool"); self.load(fb[:], fb_b, g["fbv"]); self.load(nd[:], nd_b, g["nd_d"][sgi])
        self.load(deltab[:], deltab_b, g["deltas_d"].partition_broadcast(128))

        def sin_layer(w, wb, K, src, srcb, bcol, frcol, dst, dstb):
            for t in range(0, L, 512):
                n = min(512, L - t)
                ps, psb = self.bank()
                op("pe", lambda: nc.tensor.matmul(ps[0:64, 0:n], lhsT=w[0:K, :], rhs=src[0:K, t:t + n], start=True, stop=True),
                   r=[wb, srcb], w=[psb])
                op("dve", lambda: nc.vector.tensor_scalar(out=pre[:, 0:n], in0=ps[0:64, 0:n], scalar1=fb[:, bcol:bcol + 1],
                                                          scalar2=fb[:, frcol:frcol + 1], op0=ALU.add, op1=ALU.mult), r=[psb, fb_b], w=[pre_b])
                op("dve", lambda: nc.vector.tensor_scalar(out=w_a[:, 0:n], in0=pre[:, 0:n], scalar1=PI, scalar2=-2.0 * PI, op0=ALU.is_gt, op1=ALU.mult),
                   r=[pre_b], w=[w_ab])
                op("dve", lambda: nc.vector.tensor_scalar(out=w_b[:, 0:n], in0=pre[:, 0:n], scalar1=-PI, scalar2=2.0 * PI, op0=ALU.is_lt, op1=ALU.mult),
                   r=[pre_b], w=[w_bb])
                op("dve", lambda: nc.vector.tensor_tensor(out=pre[:, 0:n], in0=pre[:, 0:n], in1=w_a[:, 0:n], op=ALU.add), r=[pre_b, w_ab], w=[pre_b])
                op("dve", lambda: nc.vector.tensor_tensor(out=pre[:, 0:n], in0=pre[:, 0:n], in1=w_b[:, 0:n], op=ALU.add), r=[pre_b, w_bb], w=[pre_b])
                op("act", lambda: nc.scalar.activation(out=dst[:, t:t + n], in_=pre[:, 0:n], func=AF.Sin), r=[pre_b], w=[dstb])

        sin_layer(w1, w1_b, 33, zf, zf_b, 0, 1, h1T, h1T_b)
        yield
        sin_layer(w2, w2_b, 64, h1T, h1T_b, 2, 3, h2T, h2T_b)
        yield
        fi = 0
        onesb, onesb_b = self.sb(pf, "onesb", [128, 128], BF16)
        op("dve", lambda: nc.vector.memset(onesb[:], 1.0), w=[onesb_b])
        HWs = [(HWb, HWb_b), self.sb(pf, "HWb2", [128, nt, 512], BF16)]
        invs = [(invt, invt_b), self.sb(pf, "invt2", [128, 512], F32)]

        def prologue(ct):
            chh = ct % 2
            HW_, HW_b = HWs[ct % 2]; inv_, inv_b = invs[ct % 2]
            pn, pnb = self.bank(pin=True)
            pend = None
            for tc in range(nt):
                ps, psb = self.bank()
                hwf, hwf_b = hwfs[tc % 2]; habs, habs_b = habss[tc % 2]
                op("pe", lambda: nc.tensor.matmul(ps[:], lhsT=h2T[0:64, tc * 128:(tc + 1) * 128], rhs=w3[0:64, ct * 512:(ct + 1) * 512],
                                                  start=True, stop=True), r=[h2T_b, w3_b], w=[psb])
                if pend is not None:
                    ptc, pha, phab = pend
                    op("pe", lambda: nc.tensor.matmul(pn[:], lhsT=onesb[:], rhs=pha[:], start=(ptc == 0), stop=False),
                       r=[onesb_b, phab], w=[pnb], inc=True)
                op("act", lambda: nc.scalar.activation(out=wint[:], in_=deltab[:, chh * 512:(chh + 1) * 512], func=AF.Exp, scale=nd[:, tc:tc + 1]),
                   r=[deltab_b, nd_b], w=[wint_b])
                op("dve", lambda: nc.vector.tensor_tensor(out=hwf[:], in0=ps[:], in1=wint[:], op=ALU.mult), r=[psb, wint_b], w=[hwf_b])
                op("act", lambda: nc.scalar.copy(out=HW_[:, tc, :], in_=hwf[:]), r=[hwf_b], w=[HW_b])
                op("dve", lambda: nc.vector.scalar_tensor_tensor(out=habs[:], in0=hwf[:], scalar=-1.0, in1=hwf[:], op0=ALU.mult, op1=ALU.max),
                   r=[hwf_b], w=[habs_b])
                pend = (tc, habs, habs_b)
                yield
            ptc, pha, phab = pend
            op("pe", lambda: nc.tensor.matmul(pn[:], lhsT=onesb[:], rhs=pha[:], start=(ptc == 0), stop=True), r=[onesb_b, phab], w=[pnb], inc=True)
            op("dve", lambda: nc.vector.tensor_scalar(out=inv_[:], in0=pn[:], scalar1=1e-6, scalar2=None, op0=ALU.add), r=[pnb], w=[inv_b])
            op("dve", lambda: nc.vector.reciprocal(out=inv_[:], in_=inv_[:]), r=[inv_b], w=[inv_b])
            self.unpin(pn)
            yield

        for _ in prologue(0):
            yield
        PD = 3
        for f0 in range(min(PD, NF)):
            self.load(FTs[f0 % 4][0][:], FTs[f0 % 4][1], FT_d[f0], FT_src)
        for ct in range(4):
            HW_, HW_b = HWs[ct % 2]; inv_, inv_b = invs[ct % 2]
            nxt_pro = prologue(ct + 1) if ct < 3 else None
            for fc in range(NF):
                ft, ftb = FTs[fi % 4]; hs, hsb = hss[fi % 3]
                nxt = fc + PD
                if nxt < NF or ct < 3:
                    ft2, ft2b = FTs[(fi + PD) % 4]
                    self.load(ft2[:], ft2b, FT_d[nxt % NF], FT_src)
                fi += 1
                ps, psb = self.bank()
                for tc in range(nt):
                    op("pe", lambda: nc.tensor.matmul(ps[:], lhsT=ft[:, tc * 128:(tc + 1) * 128], rhs=HW_[:, tc, :], start=(tc == 0), stop=(tc == nt - 1)),
                       r=[ftb, HW_b], w=[psb], inc=(tc == nt - 1))
                op("dve", lambda: nc.vector.tensor_tensor(out=hs[:], in0=ps[:], in1=inv_[:], op=ALU.mult), r=[psb, inv_b], w=[hsb])
                self.store(Hs_d[fc * 128:(fc + 1) * 128, ct * 512:(ct + 1) * 512], Hs_b, hs[:], hsb)
                if nxt_pro is not None:
                    next(nxt_pro, None)
                yield
            if nxt_pro is not None:
                for _ in nxt_pro:
                    pass

    def seg_conv(sgi, pc):
        tok0, ntok, nseq, L, N = SEGS[sgi]
        nt, NF = L // 128, N // 128
        NFh = NF // 2
        FT_d, GT_d, Hs_d, Hs_b = g["FT_d"][sgi], g["GT_d"][sgi], g["Hs_d"][sgi], g["Hs_b"][sgi]
        FT_src, GT_src = Buf("FTsrc"), Buf("GTsrc")
        usts = [self.sb(pc, f"ust{i}", [128, L + 2], F32) for i in range(1 if sgi == 0 else 2)]
        for (u_, ub_) in usts:
            op("dve", lambda: nc.vector.memset(u_[:, 0:1], 0.0), w=[ub_])
            op("dve", lambda: nc.vector.memset(u_[:, L + 1:L + 2], 0.0), w=[ub_])
        zc, zc_b = self.sb(pc, "zc", [128, 4, L], BF16)
        xg, xg_b = self.sb(pc, "xg", [128, 4, L], BF16)
        ztok, ztok_b = self.sb(pc, "ztok", [128, nt, 512], BF16)
        Y, Y_b = self.sb(pc, "Y", [128, NF, 512], BF16)
        FTs = [self.sb(pc, f"FTc{i}", [128, nt * 128], BF16) for i in range(4)]
        GTs = [self.sb(pc, f"GTc{i}", [128, NF * 256], BF16) for i in range(2)]
        Hts = [self.sb(pc, f"Ht{i}", [128, 2, 512], BF16) for i in range(2)]
        tms = [self.sb(pc, f"tm{i}", [128, 512], F32) for i in range(4)]
        ctmp, ctmp_b = self.sb(pc, "ctmp", [128, L], F32)
        itmps = [self.sb(pc, f"itmp{i}", [128, 256], F32) for i in range(2)]
        rot = {"u": 0, "ft": 0, "gt": 0, "h": 0, "it": 0}

        def conv(uch, seq_t0, dst, dstb, j):
            u_, ub_ = usts[rot["u"] % len(usts)]; rot["u"] += 1
            self.load(u_[:, 1:L + 1], ub_, uT_d[uch * 128:(uch + 1) * 128, seq_t0:seq_t0 + L], uT_b)
            wc = lambda tap: bfm[:, CW + tap * 24 + uch: CW + tap * 24 + uch + 1]
            op("dve", lambda: nc.vector.tensor_scalar(out=ctmp[:], in0=u_[:, 0:L], scalar1=wc(0), scalar2=bfm[:, CB + uch:CB + uch + 1],
                                                      op0=ALU.mult, op1=ALU.add), r=[ub_, bfm_b], w=[ctmp_b])
            op("dve", lambda: nc.vector.scalar_tensor_tensor(out=ctmp[:], in0=u_[:, 1:L + 1], scalar=wc(1), in1=ctmp[:], op0=ALU.mult, op1=ALU.add),
               r=[ub_, bfm_b, ctmp_b], w=[ctmp_b])
            op("dve", lambda: nc.vector.scalar_tensor_tensor(out=dst[:, j, :], in0=u_[:, 2:L + 2], scalar=wc(2), in1=ctmp[:], op0=ALU.mult, op1=ALU.add),
               r=[ub_, bfm_b, ctmp_b], w=[dstb])

        for sq in range(nseq):
            st0 = tok0 + sq * L
            for half in range(2):
                for j in range(4):
                    conv(16 + half * 4 + j, st0, zc, zc_b, j)
                    yield
                for o in range(2):
                    for tc in range(nt):
                        pt, ptb = self.bank()
                        ptv = pt.bitcast(BF16)
                        for j in range(4):
                            op("pe", lambda: nc.tensor.transpose(ptv[:, j * 128:(j + 1) * 128], zc[:, j, tc * 128:(tc + 1) * 128], identb[:]),
                               r=[zc_b, identb_b], w=[ptb], inc=(j == 3))
                        op("act", lambda: nc.scalar.copy(out=ztok[:, tc, :], in_=ptv[:, 0:512]), r=[ptb], w=[ztok_b])
                        yield
                    gate_js = [0, 1, 2, 3]
                    for fp in range(NFh):
                        fre, freb = FTs[rot["ft"] % 4]; fim, fimb = FTs[(rot["ft"] + 1) % 4]; rot["ft"] += 2
                        ht, htb = Hts[rot["h"] % 2]; rot["h"] += 1
                        self.load(fre[:], freb, FT_d[fp], FT_src)
                        self.load(fim[:], fimb, FT_d[NFh + fp], FT_src)
                        c0 = o * 1024 + half * 512
                        self.load(ht[:, 0, :], htb, Hs_d[fp * 128:(fp + 1) * 128, c0:c0 + 512], Hs_b)
                        self.load(ht[:, 1, :], htb, Hs_d[(NFh + fp) * 128:(NFh + fp + 1) * 128, c0:c0 + 512], Hs_b)
                        pr_, prb = self.bank(); pi_, pib = self.bank()
                        for tc in range(nt):
                            op("pe", lambda: nc.tensor.matmul(pr_[:], lhsT=fre[:, tc * 128:(tc + 1) * 128], rhs=ztok[:, tc, :], start=(tc == 0), stop=(tc == nt - 1)),
                               r=[freb, ztok_b], w=[prb], inc=(tc == nt - 1))
                        for tc in range(nt):
                            op("pe", lambda: nc.tensor.matmul(pi_[:], lhsT=fim[:, tc * 128:(tc + 1) * 128], rhs=ztok[:, tc, :], start=(tc == 0), stop=(tc == nt - 1)),
                               r=[fimb, ztok_b], w=[pib], inc=(tc == nt - 1))
                        (t1, t1b), (t2, t2b), (t3, t3b), (t4, t4b) = tms
                        op("dve", lambda: nc.vector.tensor_tensor(out=t1[:], in0=pi_[:], in1=ht[:, 1, :], op=ALU.mult), r=[pib, htb], w=[t1b])
                        op("dve", lambda: nc.vector.tensor_tensor(out=t2[:], in0=pr_[:], in1=ht[:, 0, :], op=ALU.mult), r=[prb, htb], w=[t2b])
                        op("pool", lambda: nc.gpsimd.tensor_tensor(out=Y[:, fp, :], in0=t2[:], in1=t1[:], op=ALU.subtract), r=[t1b, t2b], w=[Y_b])
                        op("dve", lambda: nc.vector.tensor_tensor(out=t3[:], in0=pr_[:], in1=ht[:, 1, :], op=ALU.mult), r=[prb, htb], w=[t3b])
                        op("dve", lambda: nc.vector.tensor_tensor(out=t4[:], in0=pi_[:], in1=ht[:, 0, :], op=ALU.mult), r=[pib, htb], w=[t4b])
                        op("pool", lambda: nc.gpsimd.tensor_tensor(out=Y[:, NFh + fp, :], in0=t3[:], in1=t4[:], op=ALU.add), r=[t3b, t4b], w=[Y_b])
                        if gate_js and (NFh < 8 or fp % 3 == 1):
                            jg = gate_js.pop(0)
                            conv(o * 8 + half * 4 + jg, st0, xg, xg_b, jg)
                        yield
                    while gate_js:
                        jg = gate_js.pop(0)
                        conv(o * 8 + half * 4 + jg, st0, xg, xg_b, jg)
                        yield
                    dst, dstb = (zc, zc_b) if o == 0 else (xg, xg_b)
                    for tt in range(L // 256):
                        gt, gtb = GTs[rot["gt"] % 2]; rot["gt"] += 1
                        self.load(gt[:], gtb, GT_d[tt], GT_src)
                        tsl = slice(tt * 256, (tt + 1) * 256)
                        for j in range(4):
                            ps, psb = self.bank()
                            for fc in range(NF):
                                op("pe", lambda: nc.tensor.matmul(ps[:, 0:256], lhsT=Y[:, fc, j * 128:(j + 1) * 128], rhs=gt[:, fc * 256:(fc + 1) * 256],
                                                                  start=(fc == 0), stop=(fc == NF - 1)), r=[Y_b, gtb], w=[psb], inc=(fc == NF - 1))
                            it, itb = itmps[rot["it"] % 2]; rot["it"] += 1
                            skc = SKIP + o * 8 + half * 4 + j
                            op("dve", lambda: nc.vector.scalar_tensor_tensor(out=it[:], in0=zc[:, j, tsl], scalar=bfm[:, skc:skc + 1], in1=ps[:, 0:256],
                                                                             op0=ALU.mult, op1=ALU.add), r=[zc_b, bfm_b, psb], w=[itb])
                            op("pool", lambda: nc.gpsimd.tensor_tensor(out=dst[:, j, tsl], in0=it[:], in1=xg[:, j, tsl], op=ALU.mult),
                               r=[itb, xg_b] + ([zc_b] if o == 0 else []), w=[dstb])
                            yield
                self.store(yh_d[half * 512:(half + 1) * 512, st0:st0 + L].rearrange("(j p) t -> p j t", p=128), yh_b, xg[:], xg_b)

    if mode == "filter":
        with ExitStack() as pf:
            self.interleave([(seg_filter(0, pf), [0, 1, 2, 3], 8), (seg_filter(1, pf), [4, 5], 2), (g["phase0_gen"](pf), [6, 7], 1)])
            kb.barrier()
        return
    for _ in genP:
        pass
    kb.barrier()
    pmB.close()
    g["phTSC"].close()
    with ExitStack() as pc:
        self.interleave([(seg_conv(0, pc), [0, 1, 2, 3, 4, 5], 3), (seg_conv(1, pc), [6, 7], 2)])
        kb.barrier()


Prog.phaseC = _phaseC


def _phaseD(self, g):
    nc, kb = self.nc, self.kb
    op = kb.op
    bfm, bfm_b, Gb, Gb_b, B2G, B2G_b, FNW, FNW_b = g["bfm"], g["bfm_b"], g["Gb"], g["Gb_b"], g["B2G"], g["B2G_b"], g["FNW"], g["FNW_b"]
    hT_d, hT_b, yh_d, yh_b, sg_d, sg_b = g["hT_d"], g["hT_b"], g["yh_d"], g["yh_b"], g["sg_d"], g["sg_b"]
    x_all, pos, y_all = g["x_all"], g["pos"], g["y_all"]
    norm_to_fm, wslot_load = g["norm_to_fm"], g["wslot_load"]
    y_b = Buf("y_all")
    GT_ = 1024
    NTI = GT_ // 128
    with ExitStack() as pd:
        f1T, f1T_b = self.sb(pd, "f1T", [128, 32, GT_], BF16)
        hTg, yhTg = f1T[:, 0:8, :], f1T[:, 8:16, :]
        mh, mh_b = self.sb(pd, "mh", [128, 8, GT_], BF16)
        sgts = [self.sb(pd, f"sgt{i}", [128, 2, GT_], BF16) for i in range(2)]
        wsl = [self.sb(pd, f"wslD{i}", [128, 8, 512], BF16) for i in range(4)]
        xt, xt_b = self.sb(pd, "xtD", [128, NTI, D], F32)
        pt, ptb = self.sb(pd, "ptD", [128, D], F32)
        tAs = [self.sb(pd, f"tA{i}", [128, 512], F32) for i in range(3)]
        ss, ss_b = self.sb(pd, "ssD", [128, NTI], F32)
        rs, rs_b = self.sb(pd, "rsD", [128, NTI], F32)
        xn, xn_b = self.sb(pd, "xnD", [128, 4, D], BF16)
        ysts = [self.sb(pd, "yst0", [128, D], F32), (pt, ptb)]
        junk, junk_b = ysts[0]
        wi = [0]; ti = [0]

        def slot():
            s_ = wsl[wi[0] % 4]; wi[0] += 1
            return s_

        def tmp():
            s_ = tAs[ti[0] % 3]; ti[0] += 1
            return s_

        w2v = g["w_mlp2"].rearrange("(c p) n -> p c n", p=128)
        def load_hy(gi):
            tk = slice(gi * GT_, (gi + 1) * GT_)
            self.load(hTg, f1T_b, hT_d[:, tk].rearrange("(k p) t -> p k t", p=128), hT_b)
            self.load(yhTg, f1T_b, yh_d[:, tk].rearrange("(k p) t -> p k t", p=128), yh_b)

        load_hy(0)
        NG = TT // GT_

        def sec_merge(gi):
            r_ = 0 if gi * GT_ < TS_ else 1
            tk = slice(gi * GT_, (gi + 1) * GT_)
            for blk in range(2):
                wa, wab = slot(); wb_, wbb = slot()
                wslot_load(wa, wab, g["w_br_m"], 512, blk * 512)
                wslot_load(wb_, wbb, g["w_br_h"], 512, blk * 512)
                for m4 in range(4):
                    m = blk * 4 + m4
                    sgt, sgtb = sgts[m % 2]
                    self.load(sgt[:, 0, :], sgtb, sg_d[m * 128:(m + 1) * 128, tk], sg_b)
                    self.load(sgt[:, 1, :], sgtb, sg_d[D + m * 128:D + (m + 1) * 128, tk], sg_b)
                    for sub in range(GT_ // 512):
                        ts_ = slice(sub * 512, (sub + 1) * 512)
                        pA, pAb = self.bank(); pB, pBb = self.bank()
                        for k in range(8):
                            op("pe", lambda: nc.tensor.matmul(pA[:], lhsT=wa[:, k, m4 * 128:(m4 + 1) * 128], rhs=hTg[:, k, ts_], start=(k == 0), stop=(k == 7)),
                               r=[wab, f1T_b], w=[pAb], inc=(k == 7))
                        for k in range(8):
                            op("pe", lambda: nc.tensor.matmul(pB[:], lhsT=wb_[:, k, m4 * 128:(m4 + 1) * 128], rhs=yhTg[:, k, ts_], start=(k == 0), stop=(k == 7)),
                               r=[wbb, f1T_b], w=[pBb], inc=(k == 7))
                        (ta, tab), (tb, tbb) = tmp(), tmp()
                        op("dve", lambda: nc.vector.tensor_tensor(out=ta[:], in0=pA[:], in1=sgt[:, 0, ts_], op=ALU.mult), r=[pAb, sgtb], w=[tab])
                        op("dve", lambda: nc.vector.tensor_tensor(out=tb[:], in0=pB[:], in1=sgt[:, 1, ts_], op=ALU.mult), r=[pBb, sgtb], w=[tbb])
                        op("dve", lambda: nc.vector.tensor_tensor(out=mh[:, m, ts_], in0=ta[:], in1=tb[:], op=ALU.add), r=[tab, tbb], w=[mh_b])

        def sec_xload(gi):
            r_ = 0 if gi * GT_ < TS_ else 1
            tk = slice(gi * GT_, (gi + 1) * GT_)
            self.load(xt[:], xt_b, x_all[tk, :].rearrange("(i p) d -> p i d", p=128))
            if r_ == 0:
                for i in range(NTI):
                    self.load(pt[:], ptb, pos[gi * GT_ + i * 128: gi * GT_ + (i + 1) * 128, :])
                    op("dve", lambda: nc.vector.tensor_tensor(out=xt[:, i, :], in0=xt[:, i, :], in1=pt[:], op=ALU.add), r=[xt_b, ptb], w=[xt_b])

        def sec_mid(gi):
            r_ = 0 if gi * GT_ < TS_ else 1
            tk = slice(gi * GT_, (gi + 1) * GT_)
            for hlf in range(2):
                wo, wob = slot()
                wslot_load(wo, wob, g["w_out"], 512, hlf * 512)
                cs = slice(hlf * 512, (hlf + 1) * 512)
                for i in range(NTI):
                    ps, psb = self.bank()
                    for k in range(8):
                        op("pe", lambda: nc.tensor.matmul(ps[:], lhsT=mh[:, k, i * 128:(i + 1) * 128], rhs=wo[:, k, :], start=(k == 0), stop=(k == 7)),
                           r=[mh_b, wob], w=[psb], inc=(k == 7))
                    ta, tab = tmp()
                    op("dve", lambda: nc.vector.tensor_tensor(out=ta[:], in0=ps[:], in1=Gb[:, r_, 0, cs], op=ALU.mult), r=[psb, Gb_b], w=[tab])
                    op("dve", lambda: nc.vector.tensor_tensor(out=xt[:, i, cs], in0=ta[:], in1=xt[:, i, cs], op=ALU.add), r=[tab, xt_b], w=[xt_b])
            for sub in range(GT_ // 512):
                norm_to_fm((junk, junk_b, ss[:, 0:4], ss_b, rs[:, 0:4], rs_b, xn, xn_b), xt[:, sub * 4:(sub + 1) * 4, :], xt_b, mh, mh_b, sub * 512, 2, r_)
            for i in range(NTI):
                op("dve", lambda: nc.vector.tensor_tensor(out=xt[:, i, :], in0=xt[:, i, :], in1=B2G[:, r_], op=ALU.add), r=[xt_b, B2G_b], w=[xt_b])
            for fb_ in range(8):
                w1t, w1b = slot()
                wslot_load(w1t, w1b, g["w_mlp1"], 512, fb_ * 512)
                for m4 in range(4):
                    fc = fb_ * 4 + m4
                    for sub in range(GT_ // 512):
                        ts_ = slice(sub * 512, (sub + 1) * 512)
                        ps, psb = self.bank()
                        for k in range(8):
                            op("pe", lambda: nc.tensor.matmul(ps[:], lhsT=w1t[:, k, m4 * 128:(m4 + 1) * 128], rhs=mh[:, k, ts_], start=(k == 0), stop=(k == 7)),
                               r=[w1b, mh_b], w=[psb], inc=(k == 7))
                        ta, tab = tmp()
                        op("act", lambda: nc.scalar.activation(out=ta[:], in_=ps[:], func=AF.Relu, bias=bfm[:, BM1 + fc:BM1 + fc + 1]), r=[psb, bfm_b], w=[tab])
                        op("dve", lambda: nc.vector.tensor_tensor(out=f1T[:, fc, ts_], in0=ta[:], in1=ta[:], op=ALU.mult), r=[tab], w=[f1T_b])
            for hlf in range(2):
                cs = slice(hlf * 512, (hlf + 1) * 512)
                bks = [self.bank() for _ in range(NTI)]
                for fbk in range(4):
                    w2t, w2b = slot()
                    self.load(w2t[:], w2b, w2v[:, fbk * 8:(fbk + 1) * 8, cs], q="pool")
                    for i in range(NTI):
                        ps, psb = bks[i]
                        for q in range(8):
                            first = (fbk == 0 and q == 0); last = (fbk == 3 and q == 7)
                            op("pe", lambda: nc.tensor.matmul(ps[:], lhsT=f1T[:, fbk * 8 + q, i * 128:(i + 1) * 128], rhs=w2t[:, q, :], start=first, stop=last),
                               r=[f1T_b, w2b], w=[psb], inc=(q == 7))
                for i in range(NTI):
                    ps, psb = bks[i]
                    ta, tab = tmp()
                    op("dve", lambda: nc.vector.tensor_tensor(out=ta[:], in0=ps[:], in1=Gb[:, r_, 1, cs], op=ALU.mult), r=[psb, Gb_b], w=[tab])
                    op("dve", lambda: nc.vector.tensor_tensor(out=xt[:, i, cs], in0=ta[:], in1=xt[:, i, cs], op=ALU.add), r=[tab, xt_b], w=[xt_b])

        def sec_final(gi):
            r_ = 0 if gi * GT_ < TS_ else 1
            for i in range(NTI):
                op("act", lambda: nc.scalar.activation(out=junk[:], in_=xt[:, i, :], func=AF.Square, accum_out=ss[:, i:i + 1]), r=[xt_b], w=[junk_b, ss_b])
            op("dve", lambda: nc.vector.tensor_scalar(out=rs[:], in0=ss[:], scalar1=1.0 / D, scalar2=EPS, op0=ALU.mult, op1=ALU.add), r=[ss_b], w=[rs_b])
            op("act", lambda: nc.scalar.activation(out=rs[:], in_=rs[:], func=AF.Sqrt), r=[rs_b], w=[rs_b])
            op("dve", lambda: nc.vector.reciprocal(out=rs[:], in_=rs[:]), r=[rs_b], w=[rs_b])
            for i in range(NTI):
                ys, ysb = ysts[i % 2]
                op("act", lambda: nc.scalar.activation(out=ys[:], in_=xt[:, i, :], func=AF.Copy, scale=rs[:, i:i + 1]), r=[xt_b, rs_b], w=[ysb])
                op("dve", lambda: nc.vector.tensor_tensor(out=ys[:], in0=ys[:], in1=FNW[:], op=ALU.mult), r=[ysb, FNW_b], w=[ysb])
                self.store(y_all[gi * GT_ + i * 128: gi * GT_ + (i + 1) * 128, :], y_b, ys[:], ysb)

        sec_xload(0)
        sec_merge(0)
        for gi in range(NG):
            sec_mid(gi)
            if gi + 1 < NG:
                load_hy(gi + 1)
                sec_merge(gi + 1)
            sec_final(gi)
            if gi + 1 < NG:
                sec_xload(gi + 1)
        kb.barrier()


Prog.phaseD = _phaseD
```
